# Optimizing a Trainium2 kernel written in Bass

```python
import math
import jax, jax.numpy as jnp
from jax import lax
import numpy as np

D_MODEL = 2048
BATCH = 4
SEQ = 2048
DEPTH = 4
DEC_BATCH = 128
DEC_SEQ = 4
PAST_LEN = 16384
PAGE_SIZE = 128

MIX_WIDTH = D_MODEL
POOL_WIDTH = MIX_WIDTH // 4
POOL_WINDOWS = (2, 4, 8, 16)
POOL_GROUPS = len(POOL_WINDOWS)
POOL_GROUP_DIM = POOL_WIDTH // POOL_GROUPS
POOL_BUF = max(POOL_WINDOWS) - 1
SSD_WIDTH = MIX_WIDTH - POOL_WIDTH
SSD_HEAD_DIM = 64
SSD_HEADS = SSD_WIDTH // SSD_HEAD_DIM
SSD_GROUPS = 4
SSD_HPG = SSD_HEADS // SSD_GROUPS
D_STATE = 128
SSD_CONV = 4
SSD_CHUNK = 128
XBC_WIDTH = SSD_WIDTH + 2 * SSD_GROUPS * D_STATE
IN_WIDTH = POOL_WIDTH + SSD_WIDTH + XBC_WIDTH + SSD_HEADS
D_FF = 128 * ((8 * D_MODEL // 3 + 127) // 128)
FFN_CONV = 3
EPS = 1e-6

kernel_name = 'hybrid_pool_ssd_convffn_adaln_step'


def rmsnorm(x, g):
    xf = x.astype(jnp.float32)
    y = xf * lax.rsqrt(jnp.mean(xf * xf, axis=-1, keepdims=True) + EPS)
    return (y * g.astype(jnp.float32)).astype(x.dtype)


def causal_dwconv(u, prev, w, b):
    k = w.shape[0]
    L = u.shape[1]
    full = jnp.concatenate([prev.astype(u.dtype), u], axis=1)
    out = b.astype(u.dtype)
    for i in range(k):
        out = out + full[:, i:i + L] * w[i].astype(u.dtype)
    return out, full[:, full.shape[1] - (k - 1):]


def pool_mixer(u, prev, pos0, pool_w, pool_scale):
    L = u.shape[1]
    full = jnp.concatenate([prev.astype(u.dtype), u], axis=1)
    cs = jnp.cumsum(full.astype(jnp.float32), axis=1)
    cs = jnp.pad(cs, ((0, 0), (1, 0), (0, 0)))
    pos = pos0 + jnp.arange(L, dtype=jnp.int32)
    outs = []
    for g, w in enumerate(POOL_WINDOWS):
        lo, hi = g * POOL_GROUP_DIM, (g + 1) * POOL_GROUP_DIM
        s = cs[:, POOL_BUF + 1:POOL_BUF + 1 + L, lo:hi] - cs[:, POOL_BUF + 1 - w:POOL_BUF + 1 - w + L, lo:hi]
        cnt = jnp.minimum(pos + 1, w).astype(jnp.float32)[None, :, None]
        d = s / cnt - u[:, :, lo:hi].astype(jnp.float32)
        outs.append(d @ pool_w[g].astype(jnp.float32))
    out = jnp.concatenate(outs, axis=-1) * pool_scale.astype(jnp.float32)
    return out.astype(u.dtype), full[:, full.shape[1] - POOL_BUF:]


def ssd_chunked(xdt, dA, Bm, Cm, h0):
    b, L = xdt.shape[0], xdt.shape[1]
    q = min(SSD_CHUNK, L)
    nc = -(-L // q)
    pad = nc * q - L

    def padc(a):
        a = jnp.pad(a, ((0, 0), (0, pad)) + ((0, 0),) * (a.ndim - 2))
        return a.reshape((b, nc, q) + a.shape[2:])

    xdt, dA, Bm, Cm = padc(xdt), padc(dA), padc(Bm), padc(Cm)
    cs = jnp.cumsum(dA, axis=2)
    seg = cs[:, :, :, None] - cs[:, :, None, :]
    mask = jnp.tril(jnp.ones((q, q), dtype=bool))[:, :, None, None]
    Lm = jnp.exp(jnp.where(mask, seg, -jnp.inf))
    CB = jnp.einsum('bclgn,bcsgn->bclsg', Cm, Bm)
    y_diag = jnp.einsum('bclsg,bclsgr,bcsgrp->bclgrp', CB, Lm, xdt)
    decay_to_end = jnp.exp(cs[:, :, -1:] - cs)
    chunk_states = jnp.einsum('bclgn,bclgr,bclgrp->bcgrpn', Bm, decay_to_end, xdt)
    chunk_decay = jnp.exp(cs[:, :, -1])

    def step(h, inp):
        s, dec = inp
        return h * dec[..., None, None] + s, h

    h_final, h_starts = lax.scan(step, h0, (jnp.moveaxis(chunk_states, 1, 0), jnp.moveaxis(chunk_decay, 1, 0)))
    h_starts = jnp.moveaxis(h_starts, 0, 1)
    y_off = jnp.einsum('bclgn,bcgrpn,bclgr->bclgrp', Cm, h_starts, jnp.exp(cs))
    y = (y_diag + y_off).reshape((b, nc * q) + xdt.shape[3:])[:, :L]
    return y, h_final


def ssd_mixer(z, xbc, dt_raw, conv_prev, h0, conv_w, conv_b, dt_bias, a_log, d_skip, norm_g):
    b, L = z.shape[0], z.shape[1]
    xbc_c, new_conv = causal_dwconv(xbc, conv_prev, conv_w, conv_b)
    xbc_c = jax.nn.silu(xbc_c.astype(jnp.float32))
    xs = xbc_c[..., :SSD_WIDTH].reshape(b, L, SSD_GROUPS, SSD_HPG, SSD_HEAD_DIM)
    Bm = xbc_c[..., SSD_WIDTH:SSD_WIDTH + SSD_GROUPS * D_STATE].reshape(b, L, SSD_GROUPS, D_STATE)
    Cm = xbc_c[..., SSD_WIDTH + SSD_GROUPS * D_STATE:].reshape(b, L, SSD_GROUPS, D_STATE)
    dt = jax.nn.softplus(dt_raw.astype(jnp.float32) + dt_bias.astype(jnp.float32)).reshape(b, L, SSD_GROUPS, SSD_HPG)
    A = -jnp.exp(a_log.astype(jnp.float32)).reshape(SSD_GROUPS, SSD_HPG)
    h0 = h0.astype(jnp.float32).reshape(b, SSD_GROUPS, SSD_HPG, SSD_HEAD_DIM, D_STATE)
    y, h_final = ssd_chunked(xs * dt[..., None], dt * A, Bm, Cm, h0)
    y = y + d_skip.astype(jnp.float32).reshape(SSD_GROUPS, SSD_HPG)[..., None] * xs
    y = y.reshape(b, L, SSD_WIDTH) * jax.nn.silu(z.astype(jnp.float32))
    y = rmsnorm(y, norm_g)
    return y.astype(z.dtype), new_conv, h_final.reshape(b, SSD_HEADS, SSD_HEAD_DIM, D_STATE)


def conv_ffn(h, prev, w_up, conv_w, conv_b, w_down):
    up = h @ w_up
    gate, val = up[..., :D_FF], up[..., D_FF:]
    gate_c, new_buf = causal_dwconv(gate, prev, conv_w, conv_b)
    act = jax.nn.silu(gate_c) * val
    return act @ w_down, new_buf


def block(x, c, st_pool, st_conv, st_ssm, st_ffn, pos0, lp):
    mod = (jax.nn.silu(c.astype(jnp.float32)) @ lp['w_ada'].astype(jnp.float32) + lp['b_ada'].astype(jnp.float32)).astype(x.dtype)
    sh1, sc1, g1, sh2, sc2, g2 = [m[:, None] for m in jnp.split(mod, 6, axis=-1)]
    h = rmsnorm(x, lp['norm1']) * (1 + sc1) + sh1
    proj = h @ lp['w_in']
    u_pool, z, xbc, dt_raw = jnp.split(proj, [POOL_WIDTH, POOL_WIDTH + SSD_WIDTH, POOL_WIDTH + SSD_WIDTH + XBC_WIDTH], axis=-1)
    pool_out, new_pool = pool_mixer(u_pool, st_pool, pos0, lp['pool_w'], lp['pool_scale'])
    ssd_out, new_conv, new_ssm = ssd_mixer(z, xbc, dt_raw, st_conv, st_ssm, lp['conv_w'], lp['conv_b'],
                                           lp['dt_bias'], lp['a_log'], lp['d_skip'], lp['ssd_norm'])
    mix = jnp.concatenate([pool_out, ssd_out], axis=-1) @ lp['w_out']
    x = x + g1 * mix
    h2 = rmsnorm(x, lp['norm2']) * (1 + sc2) + sh2
    ffn_out, new_ffn = conv_ffn(h2, st_ffn, lp['w_up'], lp['ffn_conv_w'], lp['ffn_conv_b'], lp['w_down'])
    x = x + g2 * ffn_out
    return x, new_pool, new_conv, new_ssm, new_ffn


def setup_inputs(seed: int = 0) -> dict:
    key = jax.random.key(seed)
    ks = jax.random.split(key, 32)

    def nrm(k, shape, scale=1.0):
        return scale * jax.random.normal(k, shape, jnp.float32)

    dt0 = jnp.exp(jax.random.uniform(ks[14], (DEPTH, SSD_HEADS), jnp.float32, math.log(1e-3), math.log(1e-1)))
    return {
        'x_prompt': nrm(ks[0], (BATCH, SEQ, D_MODEL)),
        'x_sample': nrm(ks[1], (DEC_BATCH, DEC_SEQ, D_MODEL)),
        'c_prompt': nrm(ks[2], (BATCH, D_MODEL)),
        'c_sample': nrm(ks[3], (DEC_BATCH, D_MODEL)),
        'state_pool': nrm(ks[4], (DEPTH, DEC_BATCH, POOL_BUF, POOL_WIDTH)),
        'state_conv': nrm(ks[5], (DEPTH, DEC_BATCH, SSD_CONV - 1, XBC_WIDTH)),
        'state_ssm': nrm(ks[6], (DEPTH, DEC_BATCH, SSD_HEADS, SSD_HEAD_DIM, D_STATE), 0.1),
        'state_ffn': nrm(ks[7], (DEPTH, DEC_BATCH, FFN_CONV - 1, D_FF)),
        'norm1': 1.0 + nrm(ks[8], (DEPTH, D_MODEL), 0.02),
        'norm2': 1.0 + nrm(ks[9], (DEPTH, D_MODEL), 0.02),
        'w_ada': nrm(ks[10], (DEPTH, D_MODEL, 6 * D_MODEL), 0.5 * D_MODEL ** -0.5),
        'b_ada': nrm(ks[11], (DEPTH, 6 * D_MODEL), 0.02),
        'w_in': nrm(ks[12], (DEPTH, D_MODEL, IN_WIDTH), D_MODEL ** -0.5),
        'pool_w': nrm(ks[13], (DEPTH, POOL_GROUPS, POOL_GROUP_DIM, POOL_GROUP_DIM), POOL_GROUP_DIM ** -0.5),
        'pool_scale': 1.0 + nrm(ks[15], (DEPTH, POOL_WIDTH), 0.1),
        'conv_w': nrm(ks[16], (DEPTH, SSD_CONV, XBC_WIDTH), SSD_CONV ** -0.5),
        'conv_b': nrm(ks[17], (DEPTH, XBC_WIDTH), 0.02),
        'dt_bias': dt0 + jnp.log(-jnp.expm1(-dt0)),
        'a_log': jnp.log(jax.random.uniform(ks[18], (DEPTH, SSD_HEADS), jnp.float32, 1.0, 16.0)),
        'd_skip': 1.0 + nrm(ks[19], (DEPTH, SSD_HEADS), 0.1),
        'ssd_norm': 1.0 + nrm(ks[20], (DEPTH, SSD_WIDTH), 0.02),
        'w_out': nrm(ks[21], (DEPTH, MIX_WIDTH, D_MODEL), MIX_WIDTH ** -0.5),
        'w_up': nrm(ks[22], (DEPTH, D_MODEL, 2 * D_FF), D_MODEL ** -0.5),
        'ffn_conv_w': nrm(ks[23], (DEPTH, FFN_CONV, D_FF), FFN_CONV ** -0.5),
        'ffn_conv_b': nrm(ks[24], (DEPTH, D_FF), 0.02),
        'w_down': nrm(ks[25], (DEPTH, D_FF, D_MODEL), D_FF ** -0.5),
        'norm_f': 1.0 + nrm(ks[26], (D_MODEL,), 0.02),
    }


def reference(x_prompt, x_sample, c_prompt, c_sample, state_pool, state_conv, state_ssm, state_ffn,
              norm1, norm2, w_ada, b_ada, w_in, pool_w, pool_scale, conv_w, conv_b, dt_bias, a_log,
              d_skip, ssd_norm, w_out, w_up, ffn_conv_w, ffn_conv_b, w_down, norm_f):
    bp = x_prompt.shape[0]
    xp, xs = x_prompt, x_sample
    pp_pool, pp_conv, pp_ssm, pp_ffn = [], [], [], []
    ps_pool, ps_conv, ps_ssm, ps_ffn = [], [], [], []
    for l in range(DEPTH):
        lp = {'norm1': norm1[l], 'norm2': norm2[l], 'w_ada': w_ada[l], 'b_ada': b_ada[l], 'w_in': w_in[l],
              'pool_w': pool_w[l], 'pool_scale': pool_scale[l], 'conv_w': conv_w[l], 'conv_b': conv_b[l],
              'dt_bias': dt_bias[l], 'a_log': a_log[l], 'd_skip': d_skip[l], 'ssd_norm': ssd_norm[l],
              'w_out': w_out[l], 'w_up': w_up[l], 'ffn_conv_w': ffn_conv_w[l], 'ffn_conv_b': ffn_conv_b[l],
              'w_down': w_down[l]}
        xp, a, b_, c_, d_ = block(
            xp, c_prompt,
            jnp.zeros((bp, POOL_BUF, POOL_WIDTH), xp.dtype),
            jnp.zeros((bp, SSD_CONV - 1, XBC_WIDTH), xp.dtype),
            jnp.zeros((bp, SSD_HEADS, SSD_HEAD_DIM, D_STATE), jnp.float32),
            jnp.zeros((bp, FFN_CONV - 1, D_FF), xp.dtype),
            0, lp)
        pp_pool.append(a); pp_conv.append(b_); pp_ssm.append(c_); pp_ffn.append(d_)
        xs, a, b_, c_, d_ = block(xs, c_sample, state_pool[l], state_conv[l], state_ssm[l], state_ffn[l], PAST_LEN, lp)
        ps_pool.append(a); ps_conv.append(b_); ps_ssm.append(c_); ps_ffn.append(d_)
    y_prompt = rmsnorm(xp, norm_f)
    y_sample = rmsnorm(xs, norm_f)
    return (y_prompt, y_sample,
            jnp.stack(pp_pool), jnp.stack(pp_conv), jnp.stack(pp_ssm), jnp.stack(pp_ffn),
            jnp.stack(ps_pool), jnp.stack(ps_conv), jnp.stack(ps_ssm), jnp.stack(ps_ffn))
```

```python
import bisect
import numpy as np
import concourse.bass as bass
import concourse.mybir as mybir
from concourse.bass_utils import run_bass_kernel_spmd

F32 = mybir.dt.float32
BF16 = mybir.dt.bfloat16
ALU = mybir.AluOpType
AF = mybir.ActivationFunctionType

D = 2048
KC = 16
DFF = 5504
JC = 43
NHEAD = 24
EPS = 1e-6
PL = 428
O_N1, O_N2, O_BADA, O_PSC, O_CW, O_CB, O_SN, O_FW, O_FB, O_DS = 0, 16, 32, 128, 132, 212, 232, 244, 373, 416
WINS = (2, 4, 8, 16)
SAME_ENGINE_SYNC = True


class Prog:
    def __init__(self):
        self.ops = []
        self.overlaps = {}
        self.epoch = 0

    def alias(self, a_keys, b_keys):
        for a in a_keys:
            self.overlaps.setdefault(a, set()).update(b_keys)
        for b in b_keys:
            self.overlaps.setdefault(b, set()).update(a_keys)

    def op(self, eng, method, *args, reads=(), writes=(), dsem=None, **kw):
        self.ops.append([eng, method, args, kw, tuple(reads), tuple(writes), dsem, self.epoch, None, False, 0])

    def analyze(self):
        lastw = {}
        readers = {}
        dma_lists = {}
        for i, o in enumerate(self.ops):
            eng, reads, writes, dsem = o[0], o[4], o[5], o[6]
            deps = {}

            def add(j):
                if j is None or j == i:
                    return
                s = self.ops[j]
                src = ('d', s[6]) if s[6] is not None else ('e', s[0], s[7])
                if deps.get(src, -1) < j:
                    deps[src] = j
            wr = set(writes)
            for w in writes:
                ov = self.overlaps.get(w)
                if ov:
                    wr |= ov
            for r in reads:
                add(lastw.get(r))
            for w in wr:
                add(lastw.get(w))
                rd = readers.get(w)
                if rd:
                    for j in rd.values():
                        add(j)
            o[8] = deps
            me = ('d', dsem) if dsem is not None else ('e', eng)
            for r in reads:
                readers.setdefault(r, {})[me] = i
            for w in wr:
                lastw[w] = i
                readers[w] = {}
            if dsem is not None:
                dma_lists.setdefault(dsem, []).append(i)
        self.dma_lists = dma_lists
        for i, o in enumerate(self.ops):
            for src, j in o[8].items():
                if src[0] == 'e':
                    s = self.ops[j]
                    if s[0] != o[0] or (SAME_ENGINE_SYNC and o[0] in ('act', 'dve', 'pool') and o[6] is None):
                        s[9] = True
                    elif s[0] == o[0] and o[6] is not None:
                        s[9] = True
        cnt = {}
        for o in self.ops:
            if o[6] is None and o[9]:
                k = (o[0], o[7])
                cnt[k] = cnt.get(k, 0) + 1
                o[10] = cnt[k]
        self.sem_keys = sorted(cnt.keys())

    def emit(self, nc, final_waits):
        engs = {'pe': 'tensor', 'act': 'scalar', 'dve': 'vector', 'pool': 'gpsimd', 'sp': 'sync'}
        csem = {k: nc.alloc_semaphore("c_%s_%d" % k) for k in self.sem_keys}
        dsem = {k: nc.alloc_semaphore("d_%s" % k) for k in self.dma_lists}
        ops = self.ops
        by_eng = {e: [] for e in engs}
        for i, o in enumerate(ops):
            by_eng[o[0]].append(i)
        dma_lists = self.dma_lists

        def run(ename, eng):
            waited = {}
            for i in by_eng[ename]:
                o = ops[i]
                need = {}
                for src, j in o[8].items():
                    s = ops[j]
                    if src[0] == 'd':
                        sem = dsem[src[1]]
                        val = 16 * bisect.bisect_left(dma_lists[src[1]], i)
                        key = ('d', src[1])
                    else:
                        if not s[9]:
                            continue
                        if s[0] == ename and o[6] is None and not (SAME_ENGINE_SYNC and ename in ('act', 'dve', 'pool')):
                            continue
                        sem = csem[(s[0], s[7])]
                        val = s[10]
                        key = ('e', s[0], s[7])
                    if need.get(key, (None, 0))[1] < val:
                        need[key] = (sem, val)
                for key, (sem, val) in need.items():
                    if waited.get(key, 0) < val:
                        eng.wait_ge(sem, val)
                        waited[key] = val
                ins = getattr(eng, o[1])(*o[2], **o[3])
                if o[6] is not None:
                    ins.then_inc(dsem[o[6]], 16)
                elif o[9]:
                    ins.then_inc(csem[(o[0], o[7])], 1)
            if ename == 'sp':
                for k in final_waits:
                    if k in dma_lists:
                        eng.wait_ge(dsem[k], 16 * len(dma_lists[k]))

        with nc.Block() as block:
            @block.tensor
            def _(e):
                run('pe', e)

            @block.scalar
            def _(e):
                run('act', e)

            @block.vector
            def _(e):
                run('dve', e)

            @block.gpsimd
            def _(e):
                run('pool', e)

            @block.sync
            def _(e):
                run('sp', e)


class Cfg:
    def __init__(self, depth=4, lp=2048, ns=32, T=512):
        self.depth, self.lp, self.ns, self.T = depth, lp, ns, T
        self.ws = ns * 4
        self.debug = False
        self.npt = lp // T


def build_program(cfg):
    DEPTH, LP, NS, T = cfg.depth, cfg.lp, cfg.ns, cfg.T
    WS = cfg.ws
    NB = 1 + NS
    nc = bass.Bass("TRN2", target_bir_lowering=False)
    P = Prog()

    def din(name, shape):
        return nc.dram_tensor(name, list(shape), F32, kind="ExternalInput").ap()

    def dout(name, shape):
        return nc.dram_tensor(name, list(shape), F32, kind="ExternalOutput").ap()

    xp = din("xp", [128, KC, LP])
    xs = din("xs", [128, KC, WS])
    cT = din("cT", [128, KC, NB])
    st_pool = din("st_pool", [DEPTH, 128, 4, NS * 19])
    st_conv = din("st_conv", [DEPTH, 128, 20, NS * 7])
    st_ffn = din("st_ffn", [DEPTH, 128, JC, NS * 6])
    st_ssm = din("st_ssm", [DEPTH, NS, 128, 1536])
    pvec_d = din("pvec", [128, DEPTH * PL + 16])
    hvec_d = din("hvec", [24, 2 * DEPTH])
    NCST = 128 * 6 + NS + 60
    cst_d = din("cst", [128, NCST])
    w_ada = din("w_ada", [DEPTH, 48, 128, KC * 256])
    w_in = din("w_in", [DEPTH, 19, 128, KC * 256])
    w_pool = din("w_pool", [DEPTH, 128, 4 * 128])
    w_out = din("w_out", [DEPTH, 8, 128, KC * 256])
    w_up = din("w_up", [DEPTH, JC, 128, KC * 256])
    w_down = din("w_down", [DEPTH, 32, 128, 22 * 128])

    o_yp = dout("o_yp", [128, KC, LP])
    o_ys = dout("o_ys", [128, KC, WS])
    o_pool_p = dout("o_pool_p", [DEPTH, 128, 4 * 15])
    o_conv_p = dout("o_conv_p", [DEPTH, 128, 20 * 3])
    o_ssm_p = dout("o_ssm_p", [DEPTH, 128, 1536])
    o_ffn_p = dout("o_ffn_p", [DEPTH, 128, JC * 2])
    o_pool_s = dout("o_pool_s", [DEPTH, 128, 4, NS * 19])
    o_conv_s = dout("o_conv_s", [DEPTH, 128, 20, NS * 7])
    o_ssm_s = dout("o_ssm_s", [DEPTH, NS, 128, 1536])
    o_ffn_s = dout("o_ffn_s", [DEPTH, 128, JC, NS * 6])
    dbg = dout("dbg", [128, 8192]) if cfg.debug else None
    h_scr = nc.dram_tensor("h_scr", [DEPTH, 128, 1536], F32, kind="Internal").ap()
    mod_scr = nc.dram_tensor("mod_scr", [DEPTH, 128, 96 * NB], F32, kind="Internal").ap()

    def sb(name, shape, dt=F32):
        return nc.alloc_sbuf_tensor("s_" + name, list(shape), dt)

    UPW = max(15 + T, NS * 19)
    GBW = max(3 + T, NS * 7)
    xT = sb("xT", [128, KC, T])
    hm = sb("hm", [128, KC, T], BF16)
    arena = sb("arena", [128, 11264])
    def arena_views(t, compact):
        if not compact:
            z = arena[:, 0:3072].bitcast(BF16).rearrange("p (k t) -> p k t", t=T)
            x_ = arena[:, 3072:9216].rearrange("p (k t) -> p k t", t=T)
            b_ = arena[:, 9216:11264].bitcast(BF16).rearrange("p (k t) -> p k t", t=T)
            a_ = arena[:, 0:JC * T // 2].bitcast(BF16).rearrange("p (k t) -> p k t", t=T)
        else:
            o1 = 6 * t
            o2 = o1 + 12 * t
            o3 = o2 + 4 * t
            o4 = o3 + (JC * t + 1) // 2
            assert o4 <= 8000
            z = arena[:, 0:o1].bitcast(BF16).rearrange("p (k t) -> p k t", t=t)
            x_ = arena[:, o1:o2].rearrange("p (k t) -> p k t", t=t)
            b_ = arena[:, o2:o3].bitcast(BF16).rearrange("p (k t) -> p k t", t=t)
            a_ = arena[:, o3:o3 + (JC * t) // 2].bitcast(BF16).rearrange("p (k t) -> p k t", t=t)
        return z, x_, b_, a_
    z_s, xc, bc, act_t = arena_views(T, False)
    assert 96 * NB <= 11264 - 8000
    modall = arena[:, 8000:8000 + 96 * NB].rearrange("p (a b) -> p a b", b=NB)
    upool = sb("upool", [128, 4, UPW])
    wbuf = [sb("wbuf%d" % i, [128, 4096], BF16) for i in range(3)]
    wpool_t = sb("wpool_t", [128, 512], BF16)
    gbuf = [sb("gbuf%d" % i, [128, GBW]) for i in range(2)]
    accb = [sb("accb%d" % i, [128, max(T, UPW)]) for i in range(2)]
    pa, pb = accb[0], accb[1]
    silb = [sb("silb0", [128, T])] * 2
    sqb = [sb("sqb%d" % i, [128, T], BF16) for i in range(4)]
    dpool = sqb[0]
    tmpn = accb
    rstd = sb("rstd", [128, T])
    dtr = sb("dtr", [24, 2 * T])
    aneg = sb("aneg", [24, 2])
    stt = sb("stt", [128, 48])
    cst_t = sb("cs_t", [128, 48])
    sm = sb("sm", [128, 96])
    daq_p = sb("daq", [128, 24])
    cdall_p = sb("cdall", [128, 24])
    daq, cdall = daq_p, cdall_p
    xdt = sb("xdt", [128, 1536], BF16)
    xdtw = sb("xdtw", [128, 1536], BF16)
    btok = sb("btok", [128, 512], BF16)
    cbm = sb("cbm", [128, 512])
    dab = [sb("dab%d" % i, [128, 384]) for i in range(2)]
    teb = [sb("teb%d" % i, [128, 384]) for i in range(2)]
    mtb = [sb("mtb%d" % i, [128, 384], BF16) for i in range(2)]
    yo = sb("yo", [128, 1536])
    Hb = [sb("H%d" % i, [128, 1536]) for i in range(2)]
    Hbf = [sb("Hbf0", [128, 1536], BF16)] * 2
    cqb = [sb("cq%d" % i, [128, 512], BF16) for i in range(2)]
    bqb = [sb("bq%d" % i, [128, 512], BF16) for i in range(2)]
    pvec = sb("pvec", [128, DEPTH * PL + 16])
    hvec = sb("hvec", [24, 2 * DEPTH])
    cst = sb("cst", [128, NCST])
    ident_bf = sb("ident_bf", [128, 128], BF16)
    ones_bf = sb("ones_bf", [128, 128], BF16)
    zeros_bf = sb("zeros_bf", [128, 128], BF16)
    mods_p = sb("mods_p", [128, DEPTH, 96])
    csil = sb("csil", [128, KC, NB], BF16)
    cin = arena[:, 0:KC * NB].rearrange("p (a b) -> p a b", b=NB)
    hist_pool = sb("hist_pool", [128, DEPTH, 4, 15])
    hist_conv = sb("hist_conv", [128, DEPTH, 20, 3])
    hist_ffn = sb("hist_ffn", [128, DEPTH, JC, 2])
    ps = nc.alloc_psum_tensor("ps", [128, 4096], F32)

    ident = cst[:, 0:128]
    tri_p = cst[:, 128:256]
    same_p = cst[:, 256:384]
    tri_s = cst[:, 384:512]
    same_s = cst[:, 512:640]
    ones_f = cst[:, 256:384]
    rowmask_all = cst[:, 768:768 + NS]
    invcnt = cst[:, 768 + NS:768 + NS + 60].rearrange("p (g t) -> p g t", t=15)

    ARENA_KEYS = [('act', j) for j in range(JC)] + [('z', k) for k in range(12)] + [('xc', k) for k in range(12)] + [('bc', k) for k in range(8)]
    P.alias(['modall'], ARENA_KEYS)
    P.alias(['cin'], ARENA_KEYS)
    P.alias([('act', j) for j in range(JC)], [('z', k) for k in range(12)] + [('xc', k) for k in range(12)] + [('bc', k) for k in range(8)])

    def bank(b):
        return ps[:, 512 * b:512 * (b + 1)]
    mmctr = [0]

    def next_bank():
        b = mmctr[0] % 8
        mmctr[0] += 1
        return b

    tiles_p = list(range(cfg.npt)) + (['s'] if NS > 0 else [])
    wlist = []
    for l in range(DEPTH):
        for j in range(48):
            wlist.append((w_ada[l, j], KC, 256))
    for tl in tiles_p:
        for l in range(DEPTH):
            for j in range(19):
                wlist.append((w_in[l, j], KC, 256))
            for j in range(8):
                wlist.append((w_out[l, j], KC, 256))
            for j in range(JC):
                wlist.append((w_up[l, j], KC, 256))
            for j in range(32):
                wlist.append((w_down[l, j], 22, 128))
    wstate = {'issued': 0, 'next': 0}
    NSLOT = 3

    def w_issue(n):
        while wstate['issued'] < min(n, len(wlist)):
            i = wstate['issued']
            ap, a, b = wlist[i]
            s = i % NSLOT
            P.op('pool', 'dma_start', out=wbuf[s][:, 0:a * b], in_=ap, writes=[('w', s)], dsem='w%d' % s)
            wstate['issued'] += 1

    def w_get(a, b):
        i = wstate['next']
        assert wlist[i][1] == a and wlist[i][2] == b, (i, wlist[i][1:], a, b)
        w_issue(i + NSLOT)
        wstate['next'] += 1
        s = i % NSLOT
        return wbuf[s][:, 0:a * b].rearrange("p (a b) -> p a b", b=b), ('w', s)

    P.op('sp', 'dma_start', out=pvec[:], in_=pvec_d, writes=['pvec'], dsem='ld')
    P.op('sp', 'dma_start', out=hvec[:], in_=hvec_d, writes=['hvec'], dsem='ld')
    P.op('sp', 'dma_start', out=cst[:], in_=cst_d, writes=['cst'], dsem='ld')
    P.op('sp', 'dma_start', out=cin[:], in_=cT, writes=['cin'], dsem='ld')
    for i_ in range(2):
        P.op('dve', 'memset', cqb[i_][:], 0.0, writes=[('cq', i_)])
    P.op('dve', 'tensor_copy', ident_bf[:], ident, reads=['cst'], writes=['identbf'])
    P.op('dve', 'tensor_copy', ones_bf[:], ones_f, reads=['cst'], writes=['onesbf'])
    P.op('dve', 'memset', zeros_bf[:], 0.0, writes=['zerosbf'])
    P.op('dve', 'memset', hist_pool[:], 0.0, writes=['hpool'])
    P.op('dve', 'memset', hist_conv[:], 0.0, writes=['hconv'])
    P.op('dve', 'memset', hist_ffn[:], 0.0, writes=['hffn'])
    P.op('act', 'activation', csil[:], cin[:], AF.Silu, reads=['cin'], writes=['csil'])

    def pv(l, off, n):
        return pvec[:, l * PL + off:l * PL + off + n]

    for l in range(DEPTH):
        for j in range(48):
            wt, wk = w_get(KC, 256)
            for half in range(2):
                c = 2 * j + half
                b = next_bank()
                for k in range(KC):
                    P.op('pe', 'matmul', bank(b)[:, 0:NB], wt[:, k, half * 128:(half + 1) * 128], csil[:, k, :],
                         start=(k == 0), stop=(k == KC - 1), reads=[wk, 'csil'], writes=[('ps', b)])
                one = 1.0 if (16 <= c < 32 or 64 <= c < 80) else 0.0
                P.op('dve', 'tensor_scalar', modall[:, c, :], bank(b)[:, 0:NB], pv(l, O_BADA + c, 1), one, ALU.add, ALU.add,
                     reads=[('ps', b), 'pvec'], writes=['modall'])
        P.op('dve', 'tensor_tensor', modall[:, 16:32, :], modall[:, 16:32, :], pv(l, O_N1, 16).unsqueeze(2).to_broadcast([128, 16, NB]),
             ALU.mult, reads=['modall', 'pvec'], writes=['modall'])
        P.op('dve', 'tensor_tensor', modall[:, 64:80, :], modall[:, 64:80, :], pv(l, O_N2, 16).unsqueeze(2).to_broadcast([128, 16, NB]),
             ALU.mult, reads=['modall', 'pvec'], writes=['modall'])
        P.op('dve', 'tensor_copy', mods_p[:, l, :], modall[:, :, 0], reads=['modall'], writes=['mods_p'])
        if NS > 0:
            P.op('sp', 'dma_start', out=mod_scr[l], in_=modall[:].rearrange("p a b -> p (a b)"), reads=['modall'], writes=[('modscr', l)], dsem='msc')

    class TileCtx:
        pass

    def norm_mod(tc, l, sc_c, sh_c):
        n = tc.ncol
        b = next_bank()
        for k in range(KC):
            s = sqb[k % 4]
            P.op('act', 'activation', s[:, 0:n], xT[:, k, 0:n], AF.Square, reads=[('x', k)], writes=[('sq', k % 4)])
            P.op('pe', 'matmul', bank(b)[:, 0:n], ones_bf[:], s[:, 0:n], start=(k == 0), stop=(k == KC - 1),
                 reads=[('sq', k % 4), 'onesbf'], writes=[('ps', b)])
        P.op('act', 'activation', rstd[:, 0:n], bank(b)[:, 0:n], AF.Sqrt, bias=EPS_AP, scale=1.0 / D, reads=[('ps', b), 'cst'], writes=['rstd'])
        P.op('dve', 'reciprocal', rstd[:, 0:n], rstd[:, 0:n], reads=['rstd'], writes=['rstd'])
        for k in range(KC):
            if tc.kind == 'p':
                tb = next_bank()
                P.op('dve', 'tensor_tensor', bank(tb)[:, 0:n], xT[:, k, 0:n], rstd[:, 0:n], ALU.mult, reads=[('x', k), 'rstd'], writes=[('ps', tb)])
                P.op('act', 'activation', hm[:, k, 0:n], bank(tb)[:, 0:n], AF.Identity, bias=mods_p[:, l, sh_c + k:sh_c + k + 1], scale=mods_p[:, l, sc_c + k:sc_c + k + 1],
                     reads=[('ps', tb), 'mods_p'], writes=[('hm', k)])
                continue
            t = tmpn[k % 2]
            P.op('dve', 'tensor_tensor', t[:, 0:n], xT[:, k, 0:n], rstd[:, 0:n], ALU.mult, reads=[('x', k), 'rstd'], writes=[('acc', k % 2)])
            tv = t[:, 0:n].rearrange("p (s w) -> p s w", w=tc.slen)
            a_b = tc.mod(l, sc_c + k)
            s_b = tc.mod(l, sh_c + k)
            P.op('dve', 'tensor_tensor', tv, tv, a_b, ALU.mult, reads=[('acc', k % 2), tc.modkey], writes=[('acc', k % 2)])
            hv = hm[:, k, 0:n].rearrange("p (s w) -> p s w", w=tc.slen)
            P.op('dve', 'tensor_tensor', hv, tv, s_b, ALU.add, reads=[('acc', k % 2), tc.modkey], writes=[('hm', k)])

    def resid_add(tc, l, b, k, g_c):
        n = tc.ncol
        if tc.kind == 'p':
            P.op('dve', 'scalar_tensor_tensor', xT[:, k, 0:n], bank(b)[:, 0:n], mods_p[:, l, g_c + k:g_c + k + 1], xT[:, k, 0:n], ALU.mult, ALU.add,
                 reads=[('ps', b), 'mods_p', ('x', k)], writes=[('x', k)])
            return
        t = tmpn[k % 2]
        tv = t[:, 0:n].rearrange("p (s w) -> p s w", w=tc.slen)
        pvw = bank(b)[:, 0:n].rearrange("p (s w) -> p s w", w=tc.slen)
        P.op('dve', 'tensor_tensor', tv, pvw, tc.mod(l, g_c + k), ALU.mult, reads=[('ps', b), tc.modkey], writes=[('acc', k % 2)])
        P.op('dve', 'tensor_tensor', xT[:, k, 0:n], xT[:, k, 0:n], t[:, 0:n], ALU.add, reads=[('x', k), ('acc', k % 2)], writes=[('x', k)])

    convctr = [0]

    def conv_silu(tc, psum_ap, psum_key, ntap, hist_ap, hist_key, st_in, st_out, st_sem, wtaps, bias_ap, out_ap, out_key, M=128):
        n = tc.ncol
        hl = ntap - 1
        i = convctr[0] % 2
        convctr[0] += 1
        g = gbuf[i]
        wd = hl + tc.slen
        gv = g[0:M, 0:tc.nseq * wd].rearrange("p (s w) -> p s w", w=wd)
        gk = ('gbuf', i)
        if tc.kind == 'p':
            P.op('act', 'copy', gv[:, 0, 0:hl], hist_ap, reads=[hist_key], writes=[gk])
        else:
            P.op('sp', 'dma_start', out=g[0:M, 0:tc.nseq * wd], in_=st_in, writes=[gk], dsem='gin%d' % i)
        P.op('act', 'copy', gv[:, :, hl:wd], psum_ap.rearrange("p (s w) -> p s w", w=tc.slen), reads=[psum_key], writes=[gk])
        if tc.kind == 'p':
            P.op('act', 'copy', hist_ap, gv[:, 0, tc.slen:tc.slen + hl], reads=[gk], writes=[hist_key])
        else:
            P.op('act', 'dma_start', out=st_out, in_=g[0:M, 0:tc.nseq * wd], reads=[gk], writes=[], dsem=st_sem)
        a = accb[i]
        av = a[0:M, 0:n].rearrange("p (s w) -> p s w", w=tc.slen)
        ak = ('acc', i)
        P.op('dve', 'tensor_scalar', av, gv[:, :, 0:tc.slen], wtaps[:, 0:1], None, ALU.mult, reads=[gk, 'pvec'], writes=[ak])
        for t_ in range(1, ntap):
            P.op('dve', 'scalar_tensor_tensor', av, gv[:, :, t_:t_ + tc.slen], wtaps[:, t_:t_ + 1], av, ALU.mult, ALU.add,
                 reads=[gk, ak, 'pvec'], writes=[ak])
        P.op('act', 'activation', out_ap, a[0:M, 0:n], AF.Silu, bias=bias_ap, reads=[ak, 'pvec'], writes=[out_key])

    EPS_AP = cst[:, 640:641]
    ONE24 = cst[0:24, 641:642]

    hctr = [0]
    for ti, tl in enumerate(tiles_p):
        P.epoch = ti + 1
        tc = TileCtx()
        if tl == 's':
            tc.kind, tc.ncol, tc.nseq, tc.slen, tc.nchunk, tc.W = 's', WS, NS, 4, 1, WS
            z_s, xc, bc, act_t = arena_views(WS, True)
            assert 6 * WS + 12 * WS + 4 * WS + (JC * WS + 1) // 2 <= 5600 and 5600 + 2 * (NS * 24 + 32) <= 8000
            daq = arena[:, 5600:5600 + NS * 24]
            cdall = arena[:, 5600 + NS * 24 + 32:5600 + 2 * NS * 24 + 32]
            P.op('dve', 'memset', rstd[:, 0:1], 0.0, writes=ARENA_KEYS + ['modall', 'rstd', 'daq', 'cdall'])
            tc.modkey = 'modall'
            tc.mod = lambda l, c: modall[:, c, 1:NB].unsqueeze(2).to_broadcast([128, NS, 4])
            P.op('sp', 'dma_start', out=xT[:, :, 0:WS], in_=xs, writes=[('x', k) for k in range(KC)], dsem='xin')
            TRI, SAME = tri_s, same_s
        else:
            tc.kind, tc.ncol, tc.nseq, tc.slen, tc.nchunk, tc.W = 'p', T, 1, T, T // 128, 128
            tc.modkey = 'mods_p'
            tc.mod = lambda l, c: mods_p[:, l, c:c + 1].unsqueeze(2).to_broadcast([128, 1, T])
            P.op('sp', 'dma_start', out=xT[:, :, :], in_=xp[:, :, tl * T:(tl + 1) * T], writes=[('x', k) for k in range(KC)], dsem='xin')
            TRI, SAME = tri_p, same_p
        n = tc.ncol
        W = tc.W
        first_p = (tl == 0)
        last_p = (tl == cfg.npt - 1)
        for l in range(DEPTH):
            if tl == 's':
                P.op('sp', 'dma_start', out=modall[:].rearrange("p a b -> p (a b)"), in_=mod_scr[l], reads=[('modscr', l)], writes=['modall'], dsem='msc')
                P.op('sp', 'dma_start', out=upool[:, :, 0:NS * 19], in_=st_pool[l], writes=['upool'], dsem='upin')
            P.op('pool', 'dma_start', out=wpool_t[:], in_=w_pool[l], writes=['wpool'], dsem='wp')
            P.op('act', 'activation', aneg[:, 0:1], hvec[:, 2 * l + 1:2 * l + 2], AF.Exp, reads=['hvec'], writes=['aneg'])
            P.op('dve', 'tensor_scalar', aneg[:, 1:2], aneg[:, 0:1], -1.0, None, ALU.mult, reads=['aneg'], writes=['aneg'])
            norm_mod(tc, l, 16, 0)
            if tc.kind == 'p':
                for g in range(4):
                    P.op('act', 'copy', upool[:, g, 0:15], hist_pool[:, l, g, :], reads=['hpool'], writes=['upool'])
            for j in range(19):
                wt, wk = w_get(KC, 256)
                for half in range(2):
                    c = 2 * j + half
                    if c > 36:
                        continue
                    M = 128 if c < 36 else 24
                    b = next_bank()
                    for k in range(KC):
                        P.op('pe', 'matmul', bank(b)[0:M, 0:n], wt[:, k, half * 128:half * 128 + M], hm[:, k, 0:n],
                             start=(k == 0), stop=(k == KC - 1), reads=[wk, ('hm', k)], writes=[('ps', b)])
                    pk = ('ps', b)
                    if c < 4:
                        uv = upool[:, c, 0:tc.nseq * (15 + tc.slen)].rearrange("p (s w) -> p s w", w=15 + tc.slen)
                        P.op('act', 'copy', uv[:, :, 15:15 + tc.slen], bank(b)[:, 0:n].rearrange("p (s w) -> p s w", w=tc.slen),
                             reads=[pk], writes=['upool'])
                    elif c < 16:
                        P.op('act', 'activation', z_s[:, c - 4, 0:n], bank(b)[:, 0:n], AF.Silu, reads=[pk], writes=[('z', c - 4)])
                    elif c < 36:
                        kk = c - 16
                        if kk < 12:
                            oap, okey = xc[:, kk, 0:n], ('xc', kk)
                        else:
                            oap, okey = bc[:, kk - 12, 0:n], ('bc', kk - 12)
                        conv_silu(tc, bank(b)[:, 0:n], pk, 4, hist_conv[:, l, kk, :], 'hconv',
                                  st_conv[l, :, kk, :] if tl == 's' else None, o_conv_s[l, :, kk, :] if tl == 's' else None, 'ocs',
                                  pv(l, O_CW + 4 * kk, 4), pv(l, O_CB + kk, 1), oap, okey)
                    else:
                        v = dtr[:, 0:n]
                        u = dtr[:, T:T + n]
                        P.op('act', 'activation', v, bank(b)[0:24, 0:n], AF.Identity, bias=hvec[:, 2 * l:2 * l + 1], reads=[pk, 'hvec'], writes=['dtr'])
                        P.op('act', 'activation', u, v, AF.Abs, reads=['dtr'], writes=['dtr'])
                        P.op('act', 'activation', u, u, AF.Exp, scale=-1.0, reads=['dtr'], writes=['dtr'])
                        P.op('act', 'activation', u, u, AF.Ln, bias=ONE24, reads=['dtr', 'cst'], writes=['dtr'])
                        P.op('dve', 'tensor_scalar', v, v, 0.0, None, ALU.max, reads=['dtr'], writes=['dtr'])
                        P.op('dve', 'tensor_tensor', v, v, u, ALU.add, reads=['dtr'], writes=['dtr'])
                        P.op('dve', 'tensor_scalar', u, v, aneg[:, 1:2], None, ALU.mult, reads=['dtr', 'aneg'], writes=['dtr'])
            if tc.kind == 's':
                P.op('act', 'dma_start', out=o_pool_s[l], in_=upool[:, :, 0:NS * 19], reads=['upool'], writes=[], dsem='ops')
            wdp = 15 + tc.slen
            for g in range(4):
                f = upool[:, g, 0:tc.nseq * wdp].rearrange("p (s w) -> p s w", w=wdp)
                A_ = pa[:, 0:tc.nseq * wdp].rearrange("p (s w) -> p s w", w=wdp)
                B_ = pb[:, 0:tc.nseq * wdp].rearrange("p (s w) -> p s w", w=wdp)
                P.op('dve', 'tensor_tensor', A_[:, :, 1:wdp], f[:, :, 1:wdp], f[:, :, 0:wdp - 1], ALU.add, reads=['upool'], writes=[('acc', 0)])
                cur, curk, oth, othk = A_, ('acc', 0), B_, ('acc', 1)
                sh = 2
                lo = 1
                for _ in range(g):
                    lo2 = lo + sh
                    P.op('dve', 'tensor_tensor', oth[:, :, lo2:wdp], cur[:, :, lo2:wdp], cur[:, :, lo2 - sh:wdp - sh], ALU.add, reads=[curk], writes=[othk])
                    cur, curk, oth, othk = oth, othk, cur, curk
                    sh *= 2
                    lo = lo2
                dv = dpool[:, 0:n].rearrange("p (s w) -> p s w", w=tc.slen)
                P.op('dve', 'scalar_tensor_tensor', dv, cur[:, :, 15:wdp], 1.0 / WINS[g], f[:, :, 15:wdp], ALU.mult, ALU.subtract,
                     reads=[curk, 'upool'], writes=[('sq', 0)])
                if first_p:
                    P.op('dve', 'tensor_tensor', oth[:, 0, 0:15], cur[:, 0, 15:30], invcnt[:, g, :], ALU.mult, reads=[curk, 'cst'], writes=[othk])
                    P.op('dve', 'tensor_tensor', dpool[:, 0:15], oth[:, 0, 0:15], f[:, 0, 15:30], ALU.subtract, reads=[othk, 'upool'], writes=[('sq', 0)])
                b = next_bank()
                P.op('pe', 'matmul', bank(b)[:, 0:n], wpool_t[:, g * 128:(g + 1) * 128], dpool[:, 0:n], start=True, stop=True,
                     reads=['wpool', ('sq', 0)], writes=[('ps', b)])
                P.op('act', 'activation', hm[:, g, 0:n], bank(b)[:, 0:n], AF.Identity, scale=pv(l, O_PSC + g, 1), reads=[('ps', b), 'pvec'], writes=[('hm', g)])
                if tc.kind == 'p':
                    P.op('act', 'copy', hist_pool[:, l, g, :], upool[:, g, tc.slen:tc.slen + 15], reads=['upool'], writes=['hpool'])
            for ci in range(tc.nchunk):
                c0 = ci * 128
                P.op('pe', 'transpose', bank(6)[0:W, 0:24], dtr[:, c0:c0 + W], ident[0:24, 0:24], reads=['dtr', 'cst'], writes=[('ps', 6)])
                P.op('pe', 'transpose', bank(6)[0:W, 24:48], dtr[:, T + c0:T + c0 + W], ident[0:24, 0:24], reads=['dtr', 'cst'], writes=[('ps', 6)])
                P.op('act', 'copy', stt[0:W, :], bank(6)[0:W, 0:48], reads=[('ps', 6)], writes=['stt'])
                P.op('pe', 'matmul', bank(6)[0:W, 64:88], TRI[0:W, 0:W], stt[0:W, 24:48], start=True, stop=True, reads=['stt', 'cst'], writes=[('ps', 6)])
                P.op('pe', 'matmul', bank(6)[0:W, 88:112], SAME[0:W, 0:W], stt[0:W, 24:48], start=True, stop=True, reads=['stt', 'cst'], writes=[('ps', 6)])
                P.op('act', 'copy', cst_t[0:W, :], bank(6)[0:W, 64:112], reads=[('ps', 6)], writes=['cs_t'])
                P.op('dve', 'tensor_tensor', sm[0:W, 72:96], cst_t[0:W, 24:48], cst_t[0:W, 0:24], ALU.subtract, reads=['cs_t'], writes=['sm'])
                P.op('act', 'activation', sm[0:W, 0:24], sm[0:W, 72:96], AF.Exp, reads=['sm'], writes=['sm'])
                P.op('act', 'activation', sm[0:W, 48:72], cst_t[0:W, 0:24], AF.Exp, reads=['cs_t'], writes=['sm'])
                P.op('dve', 'tensor_tensor', sm[0:W, 24:48], sm[0:W, 0:24], stt[0:W, 0:24], ALU.mult, reads=['sm', 'stt'], writes=['sm'])
                P.op('dve', 'tensor_scalar', sm[0:W, 72:96], cst_t[0:W, 0:24], -1.0, None, ALU.mult, reads=['cs_t', 'sm'], writes=['sm'])
                nsq = tc.nseq
                dq = daq[0:W, 0:nsq * 24].rearrange("p (q h) -> p q h", h=24)
                P.op('dve', 'tensor_tensor', dq, stt[0:W, 24:48].unsqueeze(1).to_broadcast([W, nsq, 24]),
                     rowmask_all[0:W, 0:nsq].unsqueeze(2).to_broadcast([W, nsq, 24]) if tc.kind == 's' else ones_f[0:W, 0:nsq].unsqueeze(2).to_broadcast([W, nsq, 24]),
                     ALU.mult, reads=['stt', 'cst'], writes=['daq'])
                tot = nsq * 24
                off = 0
                while off < tot:
                    wcols = min(384, tot - off)
                    P.op('pe', 'matmul', bank(7)[:, 0:wcols], ones_f[0:W, :], daq[0:W, off:off + wcols], start=True, stop=True,
                         reads=['daq', 'cst'], writes=[('ps', 7)])
                    P.op('act', 'activation', cdall[:, off:off + wcols], bank(7)[:, 0:wcols], AF.Exp, reads=[('ps', 7)], writes=['cdall'])
                    off += wcols
                pbt = bank(7).bitcast(BF16)
                for g in range(4):
                    P.op('pe', 'transpose', pbt[0:W, g * 128:(g + 1) * 128], bc[:, g, c0:c0 + W], ident_bf[:], reads=[('bc', g), 'identbf'], writes=[('ps', 7)])
                P.op('act', 'copy', btok[0:W, :], pbt[0:W, 0:512], reads=[('ps', 7)], writes=['btok'])
                A3 = ps[:, 0:1536]
                for k in range(12):
                    P.op('pe', 'transpose', A3[0:W, k * 128:(k + 1) * 128], xc[:, k, c0:c0 + W], ident, reads=[('xc', k), 'cst'],
                         writes=[('ps', 0), ('ps', 1), ('ps', 2)])
                A3v = A3[0:W, :].rearrange("p (h d) -> p h d", d=64)
                P.op('dve', 'tensor_tensor', xdt[0:W, :].rearrange("p (h d) -> p h d", d=64), A3v, stt[0:W, 0:24].unsqueeze(2).to_broadcast([W, 24, 64]),
                     ALU.mult, reads=[('ps', 0), ('ps', 1), ('ps', 2), 'stt'], writes=['xdt'])
                P.op('dve', 'tensor_tensor', xdtw[0:W, :].rearrange("p (h d) -> p h d", d=64), A3v, sm[0:W, 24:48].unsqueeze(2).to_broadcast([W, 24, 64]),
                     ALU.mult, reads=[('ps', 0), ('ps', 1), ('ps', 2), 'sm'], writes=['xdtw'])
                xcv = xc[:, :, c0:c0 + W]
                P.op('dve', 'tensor_tensor', xcv, xcv, pv(l, O_DS, 12).unsqueeze(2).to_broadcast([128, 12, W]), ALU.mult,
                     reads=[('xc', k) for k in range(12)] + ['pvec'], writes=[('xc', k) for k in range(12)])
                for g in range(4):
                    P.op('pe', 'matmul', bank(6)[0:W, g * 128:g * 128 + W], bc[:, g, c0:c0 + W], bc[:, 4 + g, c0:c0 + W], start=True, stop=True,
                         reads=[('bc', g), ('bc', 4 + g)], writes=[('ps', 6)])
                P.op('dve', 'tensor_tensor', cbm[0:W, :].rearrange("p (g w) -> p g w", w=128)[:, :, 0:W],
                     bank(6)[0:W, :].rearrange("p (g w) -> p g w", w=128)[:, :, 0:W],
                     TRI[0:W, 0:W].unsqueeze(1).to_broadcast([W, 4, W]), ALU.mult, reads=[('ps', 6), 'cst'], writes=['cbm'])
                B3 = ps[:, 1536:3072]
                if nsq > 1:
                    for b_ in range(3):
                        P.op('pe', 'matmul', A3[0:W, b_ * 512:(b_ + 1) * 512], zeros_bf[:, 0:W], hm[:, 0, 0:512], start=True, stop=True,
                             reads=['zerosbf', ('hm', 0)], writes=[('ps', 0), ('ps', 1), ('ps', 2)])
                for q in range(nsq):
                    if tc.kind == 'p' and ci > 0:
                        hi = tc.hi
                    else:
                        hi = hctr[0] % 2
                        hctr[0] += 1
                        tc.hi = hi
                    H, Hk = Hb[hi], ('H', hi)
                    Hf, Hfk = Hbf[0], ('Hbf', 0)
                    if tc.kind == 'p':
                        if ci == 0:
                            if first_p:
                                P.op('dve', 'memset', H[:], 0.0, writes=[Hk])
                            else:
                                P.op('sp', 'dma_start', out=H[:], in_=h_scr[l], reads=[('hscr', l)], writes=[Hk], dsem='hin%d' % hi)
                    else:
                        P.op('sp', 'dma_start', out=H[:], in_=st_ssm[l, q], writes=[Hk], dsem='hin%d' % hi)
                    P.op('act', 'copy', Hf[:], H[:], reads=[Hk], writes=[Hfk])
                    if tc.kind == 's':
                        cq, cqk = cqb[q % 2], ('cq', q % 2)
                        cqv = cq[:, :].rearrange("p (g w) -> p g w", w=128)
                        if q >= 2:
                            P.op('dve', 'memset', cqv[:, :, 4 * (q - 2):4 * (q - 2) + 4], 0.0, writes=[cqk])
                        P.op('dve', 'tensor_copy', cqv[:, :, 4 * q:4 * q + 4], bc[:, 4:8, c0 + 4 * q:c0 + 4 * q + 4],
                             reads=[('bc', 4), ('bc', 5), ('bc', 6), ('bc', 7)], writes=[cqk])
                        bq, bqk = bqb[q % 2], ('bq', q % 2)
                        P.op('dve', 'tensor_scalar', bq[0:W, :], btok[0:W, :], rowmask_all[0:W, q:q + 1], None, ALU.mult, reads=['btok', 'cst'], writes=[bqk])
                    for g in range(4):
                        if tc.kind == 's':
                            lhs = cq[:, g * 128:g * 128 + W]
                            rk = [cqk]
                        else:
                            lhs = bc[:, 4 + g, c0:c0 + W]
                            rk = [('bc', 4 + g)]
                        for pp in range(3):
                            c_ = g * 384 + pp * 128
                            P.op('pe', 'matmul', A3[0:W, c_:c_ + 128], lhs, Hf[:, c_:c_ + 128], start=(nsq == 1), stop=(q == nsq - 1), skip_group_check=(nsq > 1),
                                 reads=rk + [Hfk], writes=[('ps', 0), ('ps', 1), ('ps', 2)])
                    for g in range(4):
                        if tc.kind == 's':
                            lhs = bq[0:W, g * 128:(g + 1) * 128]
                            rk = [bqk]
                        else:
                            lhs = btok[0:W, g * 128:(g + 1) * 128]
                            rk = ['btok']
                        for pp in range(3):
                            c_ = g * 384 + pp * 128
                            P.op('pe', 'matmul', B3[:, c_:c_ + 128], lhs, xdtw[0:W, c_:c_ + 128], start=True, stop=True,
                                 reads=rk + ['xdtw'], writes=[('ps', 3), ('ps', 4), ('ps', 5)])
                    Hv = H[:, :].rearrange("p (h d) -> p h d", d=64)
                    P.op('dve', 'tensor_tensor', Hv, Hv, cdall[:, q * 24:(q + 1) * 24].unsqueeze(2).to_broadcast([128, 24, 64]), ALU.mult,
                         reads=[Hk, 'cdall'], writes=[Hk])
                    P.op('dve', 'tensor_tensor', H[:], H[:], B3, ALU.add, reads=[Hk, ('ps', 3), ('ps', 4), ('ps', 5)], writes=[Hk])
                    if tc.kind == 's':
                        P.op('act', 'dma_start', out=o_ssm_s[l, q], in_=H[:], reads=[Hk], writes=[], dsem='hout%d' % hi)
                    elif ci == tc.nchunk - 1:
                        if last_p:
                            P.op('act', 'dma_start', out=o_ssm_p[l], in_=H[:], reads=[Hk], writes=[], dsem='hout%d' % hi)
                        else:
                            P.op('act', 'dma_start', out=h_scr[l], in_=H[:], reads=[Hk], writes=[('hscr', l)], dsem='hout%d' % hi)
                if tc.kind == 's':
                    for q_ in range(max(0, nsq - 2), nsq):
                        P.op('dve', 'memset', cqb[q_ % 2][:, :].rearrange("p (g w) -> p g w", w=128)[:, :, 4 * q_:4 * q_ + 4], 0.0, writes=[('cq', q_ % 2)])
                P.op('dve', 'tensor_tensor', yo[0:W, :].rearrange("p (h d) -> p h d", d=64), A3v, sm[0:W, 48:72].unsqueeze(2).to_broadcast([W, 24, 64]),
                     ALU.mult, reads=[('ps', 0), ('ps', 1), ('ps', 2), 'sm'], writes=['yo'])
                def stageA(r):
                    h0 = 3 * r
                    sb_ = 6 + (r % 2)
                    da, dak = dab[r % 2], ('dab', r % 2)
                    dav = da[0:W, :].rearrange("p (j w) -> p j w", w=128)[:, :, 0:W]
                    for jj in range(3):
                        P.op('act', 'activation', da[0:W, jj * 128:jj * 128 + W], TRI[0:W, 0:W], AF.Identity, scale=stt[0:W, 24 + h0 + jj:24 + h0 + jj + 1],
                             reads=['stt', 'cst'], writes=[dak])
                    if W == 128:
                        P.op('pe', 'matmul', bank(sb_)[0:W, 0:384], ones_f[0:W, 0:W], da[0:W, 0:384], start=True, stop=True,
                             reads=[dak, 'cst'], writes=[('ps', sb_)])
                    else:
                        for jj in range(3):
                            P.op('pe', 'matmul', bank(sb_)[0:W, jj * 128:jj * 128 + W], ones_f[0:W, 0:W], da[0:W, jj * 128:jj * 128 + W], start=True, stop=True,
                                 reads=[dak, 'cst'], writes=[('ps', sb_)])

                def stageB(r):
                    h0 = 3 * r
                    sb_ = 6 + (r % 2)
                    te = teb[r % 2]
                    for jj in range(3):
                        P.op('act', 'activation', te[0:W, jj * 128:jj * 128 + W], bank(sb_)[0:W, jj * 128:jj * 128 + W], AF.Exp,
                             bias=sm[0:W, 72 + h0 + jj:72 + h0 + jj + 1], reads=[('ps', sb_), 'sm'], writes=[('teb', r % 2, jj)])

                def stageC(r):
                    g = r // 2
                    h0 = 3 * r
                    te = teb[r % 2]
                    mt, mtk = mtb[r % 2], ('mtb', r % 2)
                    tev = te[0:W, :].rearrange("p (j w) -> p j w", w=128)[:, :, 0:W]
                    P.op('dve', 'scalar_tensor_tensor', mt[0:W, :].rearrange("p (j w) -> p j w", w=128)[:, :, 0:W], tev, 1.0,
                         cbm[0:W, g * 128:g * 128 + W].unsqueeze(1).to_broadcast([W, 3, W]), ALU.min, ALU.mult,
                         reads=[('teb', r % 2, 0), ('teb', r % 2, 1), ('teb', r % 2, 2), 'cbm'], writes=[mtk])
                    for jj in range(3):
                        h = h0 + jj
                        P.op('pe', 'matmul', B3[0:W, h * 64:(h + 1) * 64], mt[0:W, jj * 128:jj * 128 + W], xdt[0:W, h * 64:(h + 1) * 64], start=True, stop=True,
                             reads=[mtk, 'xdt'], writes=[('ps', 3), ('ps', 4), ('ps', 5)])
                for it in range(10):
                    if it < 8:
                        stageA(it)
                    if 1 <= it < 9:
                        stageB(it - 1)
                    if it >= 2:
                        stageC(it - 2)
                P.op('dve', 'tensor_tensor', yo[0:W, :], yo[0:W, :], B3[0:W, :], ALU.add, reads=['yo', ('ps', 3), ('ps', 4), ('ps', 5)], writes=['yo'])
                for k in range(12):
                    P.op('pe', 'transpose', A3[:, k * 128:k * 128 + W], yo[0:W, k * 128:(k + 1) * 128], ident[0:W, 0:W], reads=['yo', 'cst'],
                         writes=[('ps', 0), ('ps', 1), ('ps', 2)])
                xcv = xc[:, :, c0:c0 + W]
                P.op('dve', 'tensor_tensor', xcv, xcv, A3.rearrange("p (k w) -> p k w", w=128)[:, :, 0:W], ALU.add,
                     reads=[('xc', k) for k in range(12)] + [('ps', 0), ('ps', 1), ('ps', 2)], writes=[('xc', k) for k in range(12)])
            if cfg.debug and tl == 's' and l == 0:
                P.op('sp', 'dma_start', out=dbg[:, 0:48], in_=stt[:, :], reads=['stt'], writes=[], dsem='dbg')
                P.op('sp', 'dma_start', out=dbg[:, 48:96], in_=cst_t[:, :], reads=['cs_t'], writes=[], dsem='dbg')
                P.op('sp', 'dma_start', out=dbg[:, 96:192], in_=sm[:, :], reads=['sm'], writes=[], dsem='dbg')
                P.op('sp', 'dma_start', out=dbg[:, 192:192 + NS * 24], in_=cdall[:, :], reads=['cdall'], writes=[], dsem='dbg')
                P.op('sp', 'dma_start', out=dbg[:, 1024:1536], in_=cbm[:, :], reads=['cbm'], writes=[], dsem='dbg')
                P.op('pool', 'dma_start', out=dbg[:, 1536:2048], in_=btok[:, :], reads=['btok'], writes=[], dsem='dbg2')
                P.op('pool', 'dma_start', out=dbg[:, 2048:3584], in_=xdt[:, :], reads=['xdt'], writes=[], dsem='dbg2')
                P.op('pool', 'dma_start', out=dbg[:, 3584:5120], in_=xdtw[:, :], reads=['xdtw'], writes=[], dsem='dbg2')
                P.op('sp', 'dma_start', out=dbg[:, 5120:6656], in_=yo[:, :], reads=['yo'], writes=[], dsem='dbg')
            b = next_bank()
            for k in range(12):
                P.op('dve', 'tensor_tensor', xc[:, k, 0:n], xc[:, k, 0:n], z_s[:, k, 0:n], ALU.mult, reads=[('xc', k), ('z', k)], writes=[('xc', k)])
                s = sqb[k % 4]
                P.op('act', 'activation', s[:, 0:n], xc[:, k, 0:n], AF.Square, reads=[('xc', k)], writes=[('sq', k % 4)])
                P.op('pe', 'matmul', bank(b)[:, 0:n], ones_bf[:], s[:, 0:n], start=(k == 0), stop=(k == 11), reads=[('sq', k % 4), 'onesbf'], writes=[('ps', b)])
            P.op('act', 'activation', rstd[:, 0:n], bank(b)[:, 0:n], AF.Sqrt, bias=EPS_AP, scale=1.0 / 1536, reads=[('ps', b), 'cst'], writes=['rstd'])
            P.op('dve', 'reciprocal', rstd[:, 0:n], rstd[:, 0:n], reads=['rstd'], writes=['rstd'])
            for k in range(12):
                P.op('dve', 'scalar_tensor_tensor', hm[:, 4 + k, 0:n], xc[:, k, 0:n], pv(l, O_SN + k, 1), rstd[:, 0:n], ALU.mult, ALU.mult,
                     reads=[('xc', k), 'rstd', 'pvec'], writes=[('hm', 4 + k)])
            for j in range(8):
                wt, wk = w_get(KC, 256)
                for half in range(2):
                    c = 2 * j + half
                    b = next_bank()
                    for k in range(KC):
                        P.op('pe', 'matmul', bank(b)[:, 0:n], wt[:, k, half * 128:(half + 1) * 128], hm[:, k, 0:n], start=(k == 0), stop=(k == KC - 1),
                             reads=[wk, ('hm', k)], writes=[('ps', b)])
                    resid_add(tc, l, b, c, 32)
            norm_mod(tc, l, 64, 48)
            for j in range(JC):
                wt, wk = w_get(KC, 256)
                bg = next_bank()
                for k in range(KC):
                    P.op('pe', 'matmul', bank(bg)[:, 0:n], wt[:, k, 0:128], hm[:, k, 0:n], start=(k == 0), stop=(k == KC - 1), reads=[wk, ('hm', k)], writes=[('ps', bg)])
                bv = next_bank()
                for k in range(KC):
                    P.op('pe', 'matmul', bank(bv)[:, 0:n], wt[:, k, 128:256], hm[:, k, 0:n], start=(k == 0), stop=(k == KC - 1), reads=[wk, ('hm', k)], writes=[('ps', bv)])
                si = j % 2
                conv_silu(tc, bank(bg)[:, 0:n], ('ps', bg), 3, hist_ffn[:, l, j, :], 'hffn',
                          st_ffn[l, :, j, :] if tl == 's' else None, o_ffn_s[l, :, j, :] if tl == 's' else None, 'ofs',
                          pv(l, O_FW + 3 * j, 3), pv(l, O_FB + j, 1), silb[si][:, 0:n], ('silb', 0))
                P.op('dve', 'tensor_tensor', act_t[:, j, 0:n], silb[si][:, 0:n], bank(bv)[:, 0:n], ALU.mult, reads=[('silb', 0), ('ps', bv)], writes=[('act', j)])
            for c in range(KC):
                b = next_bank()
                for hh in range(2):
                    wt, wk = w_get(22, 128)
                    nk = 22 if hh == 0 else 21
                    for kk in range(nk):
                        jj = hh * 22 + kk
                        P.op('pe', 'matmul', bank(b)[:, 0:n], wt[:, kk, :], act_t[:, jj, 0:n], start=(jj == 0), stop=(jj == JC - 1),
                             reads=[wk, ('act', jj)], writes=[('ps', b)])
                resid_add(tc, l, b, c, 80)
        b = next_bank()
        for k in range(KC):
            s = sqb[k % 4]
            P.op('act', 'activation', s[:, 0:n], xT[:, k, 0:n], AF.Square, reads=[('x', k)], writes=[('sq', k % 4)])
            P.op('pe', 'matmul', bank(b)[:, 0:n], ones_bf[:], s[:, 0:n], start=(k == 0), stop=(k == KC - 1), reads=[('sq', k % 4), 'onesbf'], writes=[('ps', b)])
        P.op('act', 'activation', rstd[:, 0:n], bank(b)[:, 0:n], AF.Sqrt, bias=EPS_AP, scale=1.0 / D, reads=[('ps', b), 'cst'], writes=['rstd'])
        P.op('dve', 'reciprocal', rstd[:, 0:n], rstd[:, 0:n], reads=['rstd'], writes=['rstd'])
        for k in range(KC):
            P.op('dve', 'scalar_tensor_tensor', xT[:, k, 0:n], xT[:, k, 0:n], pvec[:, DEPTH * PL + k:DEPTH * PL + k + 1], rstd[:, 0:n], ALU.mult, ALU.mult,
                 reads=[('x', k), 'rstd', 'pvec'], writes=[('x', k)])
        if tl == 's':
            P.op('sp', 'dma_start', out=o_ys, in_=xT[:, :, 0:WS], reads=[('x', k) for k in range(KC)], writes=[], dsem='xout')
        else:
            P.op('sp', 'dma_start', out=o_yp[:, :, tl * T:(tl + 1) * T], in_=xT[:, :, :], reads=[('x', k) for k in range(KC)], writes=[], dsem='xout')
        if tl != 's' and last_p:
            for l in range(DEPTH):
                P.op('sp', 'dma_start', out=o_pool_p[l], in_=hist_pool[:, l].rearrange("p a b -> p (a b)"), reads=['hpool'], writes=[], dsem='hst')
                P.op('sp', 'dma_start', out=o_conv_p[l], in_=hist_conv[:, l].rearrange("p a b -> p (a b)"), reads=['hconv'], writes=[], dsem='hst')
                P.op('sp', 'dma_start', out=o_ffn_p[l], in_=hist_ffn[:, l].rearrange("p a b -> p (a b)"), reads=['hffn'], writes=[], dsem='hst')
    assert wstate['next'] == len(wlist), (wstate, len(wlist))
    P.analyze()
    P.emit(nc, ['dbg', 'dbg2', 'xout', 'hst', 'hout0', 'hout1', 'ops', 'ocs', 'ofs', 'msc'])
    return nc, len(P.ops)


def _fm(v, nchunk):
    return np.ascontiguousarray(np.asarray(v, np.float32).reshape(nchunk, 128).T)


def _wtile(w, ncols_pad, tile_cols):
    K, N = w.shape
    if N < ncols_pad:
        w = np.concatenate([w, np.zeros((K, ncols_pad - N), np.float32)], axis=1)
    kc = K // 128
    nt = ncols_pad // tile_cols
    a = w.reshape(kc, 128, nt, tile_cols).transpose(2, 1, 0, 3)
    return np.ascontiguousarray(a).reshape(nt, 128, kc * tile_cols)


def make_consts(cfg):
    NS, WS = cfg.ns, cfg.ws
    NCST = 128 * 6 + NS + 60
    c = np.zeros((128, NCST), np.float32)
    c[:, 0:128] = np.eye(128, dtype=np.float32)
    idx = np.arange(128)
    c[:, 128:256] = (idx[:, None] <= idx[None, :]).astype(np.float32)
    c[:, 256:384] = 1.0
    same = (idx[:, None] // 4 == idx[None, :] // 4)
    c[:, 384:512] = (same & (idx[:, None] <= idx[None, :])).astype(np.float32)
    c[:, 512:640] = same.astype(np.float32)
    c[:, 640] = EPS
    c[:, 641] = 1.0
    for q in range(NS):
        c[:, 768 + q] = (idx // 4 == q).astype(np.float32)
    for g, w in enumerate(WINS):
        for t in range(15):
            c[:, 768 + NS + g * 15 + t] = 1.0 / min(t + 1, w)
    cm = np.zeros((128, NS, WS), np.float32)
    for q in range(NS):
        cm[:, q, 4 * q:4 * q + 4] = 1.0
    return c, cm.reshape(128, NS * WS)


def prep_weights(cfg, inp):
    DEPTH = cfg.depth
    out = {}
    out['w_ada'] = np.stack([_wtile(np.asarray(inp['w_ada'][l]), 12288, 256) for l in range(DEPTH)])
    out['w_in'] = np.stack([_wtile(np.asarray(inp['w_in'][l]), 19 * 256, 256) for l in range(DEPTH)])
    out['w_out'] = np.stack([_wtile(np.asarray(inp['w_out'][l]), 2048, 256) for l in range(DEPTH)])
    wu = []
    for l in range(DEPTH):
        w = np.asarray(inp['w_up'][l])
        g = w[:, :DFF].reshape(KC, 128, JC, 128)
        v = w[:, DFF:].reshape(KC, 128, JC, 128)
        t = np.concatenate([g, v], axis=3)
        wu.append(np.ascontiguousarray(t.transpose(2, 1, 0, 3)).reshape(JC, 128, KC * 256))
    out['w_up'] = np.stack(wu)
    wd = []
    for l in range(DEPTH):
        w = np.asarray(inp['w_down'][l])
        w = np.concatenate([w, np.zeros((128, D), np.float32)], axis=0).reshape(2, 22, 128, KC, 128)
        wd.append(np.ascontiguousarray(w.transpose(3, 0, 2, 1, 4)).reshape(32, 128, 22 * 128))
    out['w_down'] = np.stack(wd)
    out['w_pool'] = np.stack([np.ascontiguousarray(np.asarray(inp['pool_w'][l]).transpose(1, 0, 2)).reshape(128, 512) for l in range(DEPTH)])
    pvec = np.zeros((128, DEPTH * PL + 16), np.float32)
    for l in range(DEPTH):
        o = l * PL
        pvec[:, o + O_N1:o + O_N1 + 16] = _fm(inp['norm1'][l], 16)
        pvec[:, o + O_N2:o + O_N2 + 16] = _fm(inp['norm2'][l], 16)
        pvec[:, o + O_BADA:o + O_BADA + 96] = _fm(inp['b_ada'][l], 96)
        pvec[:, o + O_PSC:o + O_PSC + 4] = _fm(inp['pool_scale'][l], 4)
        cw = np.asarray(inp['conv_w'][l]).reshape(4, 20, 128).transpose(2, 1, 0)
        pvec[:, o + O_CW:o + O_CW + 80] = cw.reshape(128, 80)
        pvec[:, o + O_CB:o + O_CB + 20] = _fm(inp['conv_b'][l], 20)
        pvec[:, o + O_SN:o + O_SN + 12] = _fm(inp['ssd_norm'][l], 12)
        fw = np.asarray(inp['ffn_conv_w'][l]).reshape(3, JC, 128).transpose(2, 1, 0)
        pvec[:, o + O_FW:o + O_FW + 129] = fw.reshape(128, 129)
        pvec[:, o + O_FB:o + O_FB + JC] = _fm(inp['ffn_conv_b'][l], JC)
        pvec[:, o + O_DS:o + O_DS + 12] = _fm(np.repeat(np.asarray(inp['d_skip'][l]), 64), 12)
    pvec[:, DEPTH * PL:DEPTH * PL + 16] = _fm(inp['norm_f'], 16)
    out['pvec'] = pvec
    hv = np.zeros((24, 2 * DEPTH), np.float32)
    for l in range(DEPTH):
        hv[:, 2 * l] = np.asarray(inp['dt_bias'][l])
        hv[:, 2 * l + 1] = np.asarray(inp['a_log'][l])
    out['hvec'] = hv
    return out


def _fm_tokens(x):
    t = x.shape[0]
    return np.ascontiguousarray(np.asarray(x, np.float32).T.reshape(KC, 128, t).transpose(1, 0, 2))


def _unfm_tokens(a):
    t = a.shape[2]
    return np.ascontiguousarray(a.transpose(1, 0, 2).reshape(D, t).T)


def core_inputs(cfg, inp, shared, b, seqs):
    DEPTH, NS, LP = cfg.depth, cfg.ns, cfg.lp
    m = dict(shared)
    m['xp'] = _fm_tokens(np.asarray(inp['x_prompt'][b][:LP]))
    xs = np.asarray(inp['x_sample'])[seqs].reshape(NS * 4, D)
    m['xs'] = _fm_tokens(xs)
    c = np.concatenate([np.asarray(inp['c_prompt'])[b:b + 1], np.asarray(inp['c_sample'])[seqs]], axis=0)
    m['cT'] = _fm_tokens(c)

    def hist_fm(st, nch, hl):
        a = np.asarray(st)[:DEPTH][:, seqs]
        a = a.reshape(DEPTH, NS, hl, nch, 128).transpose(0, 4, 3, 1, 2)
        o = np.zeros((DEPTH, 128, nch, NS, hl + 4), np.float32)
        o[..., :hl] = a
        return o.reshape(DEPTH, 128, nch, NS * (hl + 4))
    m['st_pool'] = hist_fm(inp['state_pool'], 4, 15)
    m['st_conv'] = hist_fm(inp['state_conv'], 20, 3)
    m['st_ffn'] = hist_fm(inp['state_ffn'], JC, 2)
    s = np.asarray(inp['state_ssm'])[:DEPTH][:, seqs]
    m['st_ssm'] = np.ascontiguousarray(s.transpose(0, 1, 4, 2, 3)).reshape(DEPTH, NS, 128, 1536)
    return m


def core_outputs(cfg, r):
    DEPTH, NS = cfg.depth, cfg.ns
    o = {}
    o['yp'] = _unfm_tokens(r['o_yp'])
    o['ys'] = _unfm_tokens(r['o_ys']).reshape(NS, 4, D)
    o['pool_p'] = r['o_pool_p'].reshape(DEPTH, 128, 4, 15).transpose(0, 3, 2, 1).reshape(DEPTH, 15, 512)
    o['conv_p'] = r['o_conv_p'].reshape(DEPTH, 128, 20, 3).transpose(0, 3, 2, 1).reshape(DEPTH, 3, 2560)
    o['ffn_p'] = r['o_ffn_p'].reshape(DEPTH, 128, JC, 2).transpose(0, 3, 2, 1).reshape(DEPTH, 2, DFF)
    o['ssm_p'] = r['o_ssm_p'].reshape(DEPTH, 128, 24, 64).transpose(0, 2, 3, 1)
    o['pool_s'] = r['o_pool_s'].reshape(DEPTH, 128, 4, NS, 19)[..., 4:].transpose(0, 3, 4, 2, 1).reshape(DEPTH, NS, 15, 512)
    o['conv_s'] = r['o_conv_s'].reshape(DEPTH, 128, 20, NS, 7)[..., 4:].transpose(0, 3, 4, 2, 1).reshape(DEPTH, NS, 3, 2560)
    o['ffn_s'] = r['o_ffn_s'].reshape(DEPTH, 128, JC, NS, 6)[..., 4:].transpose(0, 3, 4, 2, 1).reshape(DEPTH, NS, 2, DFF)
    o['ssm_s'] = r['o_ssm_s'].reshape(DEPTH, NS, 128, 24, 64).transpose(0, 1, 3, 4, 2)
    return o


_CACHE = {}


def kernel(**inp):
    cfg = Cfg(depth=4, lp=2048, ns=32, T=512)
    NCORE = 4
    if 'nc' not in _CACHE:
        _CACHE['nc'] = build_program(cfg)[0]
    nc = _CACHE['nc']
    shared = prep_weights(cfg, inp)
    cst, cm = make_consts(cfg)
    shared['cst'] = cst
    in_maps = []
    for b in range(NCORE):
        seqs = np.arange(b * cfg.ns, (b + 1) * cfg.ns)
        in_maps.append(core_inputs(cfg, inp, shared, b, seqs))
    res = run_bass_kernel_spmd(nc, in_maps, core_ids=list(range(NCORE)))
    outs = [core_outputs(cfg, r) for r in res.results]
    f = np.float32
    y_prompt = np.stack([o['yp'] for o in outs]).astype(f)
    y_sample = np.concatenate([o['ys'] for o in outs], axis=0).astype(f)
    pool_p = np.stack([o['pool_p'] for o in outs], axis=1).astype(f)
    conv_p = np.stack([o['conv_p'] for o in outs], axis=1).astype(f)
    ssm_p = np.stack([o['ssm_p'] for o in outs], axis=1).astype(f)
    ffn_p = np.stack([o['ffn_p'] for o in outs], axis=1).astype(f)
    pool_s = np.concatenate([o['pool_s'] for o in outs], axis=1).astype(f)
    conv_s = np.concatenate([o['conv_s'] for o in outs], axis=1).astype(f)
    ssm_s = np.concatenate([o['ssm_s'] for o in outs], axis=1).astype(f)
    ffn_s = np.concatenate([o['ffn_s'] for o in outs], axis=1).astype(f)
    return (y_prompt, y_sample, pool_p, conv_p, ssm_p, ffn_p, pool_s, conv_s, ssm_s, ffn_s)
```

```python
import bisect
import numpy as np
import ml_dtypes
import concourse.bass as bass
import concourse.mybir as mybir
from concourse.bass_utils import run_bass_kernel_spmd

F32 = mybir.dt.float32
BF16 = mybir.dt.bfloat16
ALU = mybir.AluOpType
AF = mybir.ActivationFunctionType

D = 2048
KC = 16
DFF = 5504
JC = 43
NHEAD = 24
EPS = 1e-6
PL = 428
O_N1, O_N2, O_BADA, O_PSC, O_CW, O_CB, O_SN, O_FW, O_FB, O_DS = 0, 16, 32, 128, 132, 212, 232, 244, 373, 416
WINS = (2, 4, 8, 16)
SAME_ENGINE_SYNC = True


class Prog:
    def __init__(self):
        self.ops = []
        self.overlaps = {}
        self.epoch = 0

    def alias(self, a_keys, b_keys):
        for a in a_keys:
            self.overlaps.setdefault(a, set()).update(b_keys)
        for b in b_keys:
            self.overlaps.setdefault(b, set()).update(a_keys)

    def op(self, eng, method, *args, reads=(), writes=(), dsem=None, **kw):
        self.ops.append([eng, method, args, kw, tuple(reads), tuple(writes), dsem, self.epoch, None, False, 0])

    def analyze(self):
        lastw = {}
        readers = {}
        dma_lists = {}
        for i, o in enumerate(self.ops):
            eng, reads, writes, dsem = o[0], o[4], o[5], o[6]
            deps = {}

            def add(j):
                if j is None or j == i:
                    return
                s = self.ops[j]
                src = ('d', s[6]) if s[6] is not None else ('e', s[0], s[7])
                if deps.get(src, -1) < j:
                    deps[src] = j
            wr = set(writes)
            for w in writes:
                ov = self.overlaps.get(w)
                if ov:
                    wr |= ov
            for r in reads:
                add(lastw.get(r))
            for w in wr:
                add(lastw.get(w))
                rd = readers.get(w)
                if rd:
                    for j in rd.values():
                        add(j)
            o[8] = deps
            me = ('d', dsem) if dsem is not None else ('e', eng)
            for r in reads:
                readers.setdefault(r, {})[me] = i
            for w in wr:
                lastw[w] = i
                readers[w] = {}
            if dsem is not None:
                dma_lists.setdefault(dsem, []).append(i)
        self.dma_lists = dma_lists
        for i, o in enumerate(self.ops):
            for src, j in o[8].items():
                if src[0] == 'e':
                    s = self.ops[j]
                    if s[0] != o[0] or (SAME_ENGINE_SYNC and o[0] in ('act', 'dve', 'pool') and o[6] is None):
                        s[9] = True
                    elif s[0] == o[0] and o[6] is not None:
                        s[9] = True
        cnt = {}
        for o in self.ops:
            if o[6] is None and o[9]:
                k = (o[0], o[7])
                cnt[k] = cnt.get(k, 0) + 1
                o[10] = cnt[k]
        self.sem_keys = sorted(cnt.keys())

    def emit(self, nc, final_waits):
        engs = {'pe': 'tensor', 'act': 'scalar', 'dve': 'vector', 'pool': 'gpsimd', 'sp': 'sync'}
        csem = {k: nc.alloc_semaphore("c_%s_%d" % k) for k in self.sem_keys}
        dsem = {k: nc.alloc_semaphore("d_%s" % k) for k in self.dma_lists}
        ops = self.ops
        by_eng = {e: [] for e in engs}
        for i, o in enumerate(ops):
            by_eng[o[0]].append(i)
        dma_lists = self.dma_lists

        def run(ename, eng):
            waited = {}
            for i in by_eng[ename]:
                o = ops[i]
                need = {}
                for src, j in o[8].items():
                    s = ops[j]
                    if src[0] == 'd':
                        sem = dsem[src[1]]
                        val = 16 * bisect.bisect_left(dma_lists[src[1]], i)
                        key = ('d', src[1])
                    else:
                        if not s[9]:
                            continue
                        if s[0] == ename and o[6] is None and not (SAME_ENGINE_SYNC and ename in ('act', 'dve', 'pool')):
                            continue
                        sem = csem[(s[0], s[7])]
                        val = s[10]
                        key = ('e', s[0], s[7])
                    if need.get(key, (None, 0))[1] < val:
                        need[key] = (sem, val)
                for key, (sem, val) in need.items():
                    if waited.get(key, 0) < val:
                        eng.wait_ge(sem, val)
                        waited[key] = val
                ins = getattr(eng, o[1])(*o[2], **o[3])
                if o[6] is not None:
                    ins.then_inc(dsem[o[6]], 16)
                elif o[9]:
                    ins.then_inc(csem[(o[0], o[7])], 1)
            if ename == 'sp':
                for k in final_waits:
                    if k in dma_lists:
                        eng.wait_ge(dsem[k], 16 * len(dma_lists[k]))

        with nc.Block() as block:
            @block.tensor
            def _(e):
                run('pe', e)

            @block.scalar
            def _(e):
                run('act', e)

            @block.vector
            def _(e):
                run('dve', e)

            @block.gpsimd
            def _(e):
                run('pool', e)

            @block.sync
            def _(e):
                run('sp', e)


class Cfg:
    def __init__(self, depth=4, lp=2048, ns=32, T=512):
        self.depth, self.lp, self.ns, self.T = depth, lp, ns, T
        self.ws = ns * 4
        self.debug = False
        self.npt = lp // T


def build_program(cfg):
    DEPTH, LP, NS, T = cfg.depth, cfg.lp, cfg.ns, cfg.T
    WS = cfg.ws
    NB = 1 + NS
    nc = bass.Bass("TRN2", target_bir_lowering=False)
    P = Prog()

    def din(name, shape):
        return nc.dram_tensor(name, list(shape), F32, kind="ExternalInput").ap()

    def dout(name, shape):
        return nc.dram_tensor(name, list(shape), F32, kind="ExternalOutput").ap()

    xp = din("xp", [128, KC, LP])
    xs = din("xs", [128, KC, WS])
    cT = din("cT", [128, KC, NB])
    st_pool = din("st_pool", [DEPTH, 128, 4, NS * 19])
    st_conv = din("st_conv", [DEPTH, 128, 20, NS * 7])
    st_ffn = din("st_ffn", [DEPTH, 128, JC, NS * 6])
    st_ssm = din("st_ssm", [DEPTH, NS, 128, 1536])
    pvec_d = din("pvec", [128, DEPTH * PL + 16])
    hvec_d = din("hvec", [24, 2 * DEPTH])
    NCST = 128 * 6 + NS + 60
    cst_d = din("cst", [128, NCST])
    w_ada = din("w_ada", [DEPTH, 48, 128, KC * 256])
    w_in = din("w_in", [DEPTH, 19, 128, KC * 256])
    w_pool = din("w_pool", [DEPTH, 128, 4 * 128])
    w_out = din("w_out", [DEPTH, 8, 128, KC * 256])
    w_up = din("w_up", [DEPTH, JC, 128, KC * 256])
    w_down = din("w_down", [DEPTH, 32, 128, 22 * 128])

    o_yp = dout("o_yp", [128, KC, LP])
    o_ys = dout("o_ys", [128, KC, WS])
    o_pool_p = dout("o_pool_p", [DEPTH, 128, 4 * 15])
    o_conv_p = dout("o_conv_p", [DEPTH, 128, 20 * 3])
    o_ssm_p = dout("o_ssm_p", [DEPTH, 128, 1536])
    o_ffn_p = dout("o_ffn_p", [DEPTH, 128, JC * 2])
    o_pool_s = dout("o_pool_s", [DEPTH, 128, 4, NS * 19])
    o_conv_s = dout("o_conv_s", [DEPTH, 128, 20, NS * 7])
    o_ssm_s = dout("o_ssm_s", [DEPTH, NS, 128, 1536])
    o_ffn_s = dout("o_ffn_s", [DEPTH, 128, JC, NS * 6])
    dbg = dout("dbg", [128, 8192]) if cfg.debug else None
    h_scr = nc.dram_tensor("h_scr", [DEPTH, 128, 1536], F32, kind="Internal").ap()
    mod_scr = nc.dram_tensor("mod_scr", [DEPTH, 128, 96 * NB], F32, kind="Internal").ap()

    def sb(name, shape, dt=F32):
        return nc.alloc_sbuf_tensor("s_" + name, list(shape), dt)

    UPW = max(15 + T, NS * 19)
    GBW = max(3 + T, NS * 7)
    xT = sb("xT", [128, KC, T])
    hm = sb("hm", [128, KC, T], BF16)
    arena = sb("arena", [128, 11264])
    def arena_views(t, compact):
        if not compact:
            z = arena[:, 0:3072].bitcast(BF16).rearrange("p (k t) -> p k t", t=T)
            x_ = arena[:, 3072:9216].rearrange("p (k t) -> p k t", t=T)
            b_ = arena[:, 9216:11264].bitcast(BF16).rearrange("p (k t) -> p k t", t=T)
            a_ = arena[:, 0:JC * T // 2].bitcast(BF16).rearrange("p (k t) -> p k t", t=T)
        else:
            o1 = 6 * t
            o2 = o1 + 12 * t
            o3 = o2 + 4 * t
            o4 = o3 + (JC * t + 1) // 2
            assert o4 <= 8000
            z = arena[:, 0:o1].bitcast(BF16).rearrange("p (k t) -> p k t", t=t)
            x_ = arena[:, o1:o2].rearrange("p (k t) -> p k t", t=t)
            b_ = arena[:, o2:o3].bitcast(BF16).rearrange("p (k t) -> p k t", t=t)
            a_ = arena[:, o3:o3 + (JC * t) // 2].bitcast(BF16).rearrange("p (k t) -> p k t", t=t)
        return z, x_, b_, a_
    z_s, xc, bc, act_t = arena_views(T, False)
    assert 96 * NB <= 11264 - 8000
    modall = arena[:, 8000:8000 + 96 * NB].rearrange("p (a b) -> p a b", b=NB)
    upool = sb("upool", [128, 4, UPW])
    wbuf = [sb("wbuf%d" % i, [128, 4096], BF16) for i in range(3)]
    wpool_t = sb("wpool_t", [128, 512], BF16)
    gbuf = [sb("gbuf%d" % i, [128, GBW]) for i in range(2)]
    accb = [sb("accb%d" % i, [128, max(T, UPW)]) for i in range(2)]
    pa, pb = accb[0], accb[1]
    silb = [sb("silb0", [128, T])] * 2
    sqb = [sb("sqb%d" % i, [128, T], BF16) for i in range(2)]
    dpool = sqb[0]
    tmpn = accb
    rstd = sb("rstd", [128, T])
    dtr = sb("dtr", [24, 2 * T])
    aneg = sb("aneg", [24, 2])
    stt = sb("stt", [128, 48])
    cst_t = sb("cs_t", [128, 48])
    sm = sb("sm", [128, 96])
    daq_p = sb("daq", [128, 24])
    cdall_p = sb("cdall", [128, 24])
    daq, cdall = daq_p, cdall_p
    xdt = sb("xdt", [128, 1536], BF16)
    xdtw = sb("xdtw", [128, 1536], BF16)
    btok = sb("btok", [128, 512], BF16)
    cbm = sb("cbm", [128, 512])
    dab = [sb("dab%d" % i, [128, 384]) for i in range(2)]
    teb = [sb("teb%d" % i, [128, 384]) for i in range(2)]
    mtb = [sb("mtb%d" % i, [128, 384], BF16) for i in range(2)]
    yo = sb("yo", [128, 1536])
    Hb = [sb("H%d" % i, [128, 1536]) for i in range(2)]
    Hbf = [sb("Hbf0", [128, 1536], BF16)] * 2
    cqb = [sb("cq%d" % i, [128, 512], BF16) for i in range(2)]
    bqb = [sb("bq%d" % i, [128, 512], BF16) for i in range(2)]
    pvec = sb("pvec", [128, DEPTH * PL + 16])
    hvec = sb("hvec", [24, 2 * DEPTH])
    cst = sb("cst", [128, NCST])
    ident_bf = sb("ident_bf", [128, 128], BF16)
    ones_bf = sb("ones_bf", [128, 128], BF16)
    zeros_bf = sb("zeros_bf", [128, 128], BF16)
    mods_p = sb("mods_p", [128, DEPTH, 96])
    csil = sb("csil", [128, KC, NB], BF16)
    cin = arena[:, 0:KC * NB].rearrange("p (a b) -> p a b", b=NB)
    hist_pool = sb("hist_pool", [128, DEPTH, 4, 15])
    hist_conv = sb("hist_conv", [128, DEPTH, 20, 3])
    hist_ffn = sb("hist_ffn", [128, DEPTH, JC, 2])
    ps = nc.alloc_psum_tensor("ps", [128, 4096], F32)

    ident = cst[:, 0:128]
    tri_p = cst[:, 128:256]
    same_p = cst[:, 256:384]
    tri_s = cst[:, 384:512]
    same_s = cst[:, 512:640]
    ones_f = cst[:, 256:384]
    rowmask_all = cst[:, 768:768 + NS]
    invcnt = cst[:, 768 + NS:768 + NS + 60].rearrange("p (g t) -> p g t", t=15)

    ARENA_KEYS = [('act', j) for j in range(JC)] + [('z', k) for k in range(12)] + [('xc', k) for k in range(12)] + [('bc', k) for k in range(8)]
    P.alias(['modall'], ARENA_KEYS)
    P.alias(['cin'], ARENA_KEYS)
    P.alias([('act', j) for j in range(JC)], [('z', k) for k in range(12)] + [('xc', k) for k in range(12)] + [('bc', k) for k in range(8)])

    def bank(b):
        return ps[:, 512 * b:512 * (b + 1)]
    mmctr = [0]

    def next_bank():
        b = mmctr[0] % 8
        mmctr[0] += 1
        return b

    tiles_p = list(range(cfg.npt)) + (['s'] if NS > 0 else [])
    wlist = []
    for l in range(DEPTH):
        for j in range(48):
            wlist.append((w_ada[l, j], KC, 256))
    for tl in tiles_p:
        for l in range(DEPTH):
            for j in range(19):
                wlist.append((w_in[l, j], KC, 256))
            for j in range(8):
                wlist.append((w_out[l, j], KC, 256))
            for j in range(JC):
                wlist.append((w_up[l, j], KC, 256))
            for j in range(32):
                wlist.append((w_down[l, j], 22, 128))
    wstate = {'issued': 0, 'next': 0}
    NSLOT = 3

    def w_issue(n):
        while wstate['issued'] < min(n, len(wlist)):
            i = wstate['issued']
            ap, a, b = wlist[i]
            s = i % NSLOT
            P.op('pool', 'dma_start', out=wbuf[s][:, 0:a * b], in_=ap, writes=[('w', s)], dsem='w%d' % s)
            wstate['issued'] += 1

    def w_get(a, b):
        i = wstate['next']
        assert wlist[i][1] == a and wlist[i][2] == b, (i, wlist[i][1:], a, b)
        w_issue(i + NSLOT)
        wstate['next'] += 1
        s = i % NSLOT
        return wbuf[s][:, 0:a * b].rearrange("p (a b) -> p a b", b=b), ('w', s)

    P.op('sp', 'dma_start', out=pvec[:], in_=pvec_d, writes=['pvec'], dsem='ld')
    P.op('sp', 'dma_start', out=hvec[:], in_=hvec_d, writes=['hvec'], dsem='ld')
    P.op('sp', 'dma_start', out=cst[:], in_=cst_d, writes=['cst'], dsem='ld')
    P.op('sp', 'dma_start', out=cin[:], in_=cT, writes=['cin'], dsem='ld')
    for i_ in range(2):
        P.op('dve', 'memset', cqb[i_][:], 0.0, writes=[('cq', i_)])
    P.op('dve', 'tensor_copy', ident_bf[:], ident, reads=['cst'], writes=['identbf'])
    P.op('dve', 'tensor_copy', ones_bf[:], ones_f, reads=['cst'], writes=['onesbf'])
    P.op('dve', 'memset', zeros_bf[:], 0.0, writes=['zerosbf'])
    P.op('dve', 'memset', hist_pool[:], 0.0, writes=['hpool'])
    P.op('dve', 'memset', hist_conv[:], 0.0, writes=['hconv'])
    P.op('dve', 'memset', hist_ffn[:], 0.0, writes=['hffn'])
    P.op('act', 'activation', csil[:], cin[:], AF.Silu, reads=['cin'], writes=['csil'])

    def pv(l, off, n):
        return pvec[:, l * PL + off:l * PL + off + n]

    for l in range(DEPTH):
        for j in range(48):
            wt, wk = w_get(KC, 256)
            for half in range(2):
                c = 2 * j + half
                b = next_bank()
                for k in range(KC):
                    P.op('pe', 'matmul', bank(b)[:, 0:NB], wt[:, k, half * 128:(half + 1) * 128], csil[:, k, :],
                         start=(k == 0), stop=(k == KC - 1), reads=[wk, 'csil'], writes=[('ps', b)])
                one = 1.0 if (16 <= c < 32 or 64 <= c < 80) else 0.0
                P.op('dve', 'tensor_scalar', modall[:, c, :], bank(b)[:, 0:NB], pv(l, O_BADA + c, 1), one, ALU.add, ALU.add,
                     reads=[('ps', b), 'pvec'], writes=['modall'])
        P.op('dve', 'tensor_tensor', modall[:, 16:32, :], modall[:, 16:32, :], pv(l, O_N1, 16).unsqueeze(2).to_broadcast([128, 16, NB]),
             ALU.mult, reads=['modall', 'pvec'], writes=['modall'])
        P.op('dve', 'tensor_tensor', modall[:, 64:80, :], modall[:, 64:80, :], pv(l, O_N2, 16).unsqueeze(2).to_broadcast([128, 16, NB]),
             ALU.mult, reads=['modall', 'pvec'], writes=['modall'])
        P.op('dve', 'tensor_copy', mods_p[:, l, :], modall[:, :, 0], reads=['modall'], writes=['mods_p'])
        if NS > 0:
            P.op('sp', 'dma_start', out=mod_scr[l], in_=modall[:].rearrange("p a b -> p (a b)"), reads=['modall'], writes=[('modscr', l)], dsem='msc')

    class TileCtx:
        pass

    def norm_mod(tc, l, sc_c, sh_c):
        n = tc.ncol
        b = next_bank()
        for k in range(KC):
            s = sqb[k % 2]
            P.op('act', 'activation', s[:, 0:n], xT[:, k, 0:n], AF.Square, reads=[('x', k)], writes=[('sq', k % 2)])
            P.op('pe', 'matmul', bank(b)[:, 0:n], ones_bf[:], s[:, 0:n], start=(k == 0), stop=(k == KC - 1),
                 reads=[('sq', k % 2), 'onesbf'], writes=[('ps', b)])
        P.op('act', 'activation', rstd[:, 0:n], bank(b)[:, 0:n], AF.Sqrt, bias=EPS_AP, scale=1.0 / D, reads=[('ps', b), 'cst'], writes=['rstd'])
        P.op('dve', 'reciprocal', rstd[:, 0:n], rstd[:, 0:n], reads=['rstd'], writes=['rstd'])
        for k in range(KC):
            t = tmpn[k % 2]
            P.op('dve', 'tensor_tensor', t[:, 0:n], xT[:, k, 0:n], rstd[:, 0:n], ALU.mult, reads=[('x', k), 'rstd'], writes=[('acc', k % 2)])
            if tc.kind == 'p':
                P.op('act', 'activation', hm[:, k, 0:n], t[:, 0:n], AF.Identity, bias=mods_p[:, l, sh_c + k:sh_c + k + 1], scale=mods_p[:, l, sc_c + k:sc_c + k + 1],
                     reads=[('acc', k % 2), 'mods_p'], writes=[('hm', k)])
                continue
            tv = t[:, 0:n].rearrange("p (s w) -> p s w", w=tc.slen)
            a_b = tc.mod(l, sc_c + k)
            s_b = tc.mod(l, sh_c + k)
            P.op('dve', 'tensor_tensor', tv, tv, a_b, ALU.mult, reads=[('acc', k % 2), tc.modkey], writes=[('acc', k % 2)])
            hv = hm[:, k, 0:n].rearrange("p (s w) -> p s w", w=tc.slen)
            P.op('dve', 'tensor_tensor', hv, tv, s_b, ALU.add, reads=[('acc', k % 2), tc.modkey], writes=[('hm', k)])

    def resid_add(tc, l, b, k, g_c):
        n = tc.ncol
        if tc.kind == 'p':
            P.op('dve', 'scalar_tensor_tensor', xT[:, k, 0:n], bank(b)[:, 0:n], mods_p[:, l, g_c + k:g_c + k + 1], xT[:, k, 0:n], ALU.mult, ALU.add,
                 reads=[('ps', b), 'mods_p', ('x', k)], writes=[('x', k)])
            return
        t = tmpn[k % 2]
        tv = t[:, 0:n].rearrange("p (s w) -> p s w", w=tc.slen)
        pvw = bank(b)[:, 0:n].rearrange("p (s w) -> p s w", w=tc.slen)
        P.op('dve', 'tensor_tensor', tv, pvw, tc.mod(l, g_c + k), ALU.mult, reads=[('ps', b), tc.modkey], writes=[('acc', k % 2)])
        P.op('dve', 'tensor_tensor', xT[:, k, 0:n], xT[:, k, 0:n], t[:, 0:n], ALU.add, reads=[('x', k), ('acc', k % 2)], writes=[('x', k)])

    convctr = [0]

    def conv_silu(tc, psum_ap, psum_key, ntap, hist_ap, hist_key, st_in, st_out, st_sem, wtaps, bias_ap, out_ap, out_key, M=128):
        n = tc.ncol
        hl = ntap - 1
        i = convctr[0] % 2
        convctr[0] += 1
        g = gbuf[i]
        wd = hl + tc.slen
        gv = g[0:M, 0:tc.nseq * wd].rearrange("p (s w) -> p s w", w=wd)
        gk = ('gbuf', i)
        if tc.kind == 'p':
            P.op('act', 'copy', gv[:, 0, 0:hl], hist_ap, reads=[hist_key], writes=[gk])
        else:
            P.op('sp', 'dma_start', out=g[0:M, 0:tc.nseq * wd], in_=st_in, writes=[gk], dsem='gin%d' % i)
        P.op('act', 'copy', gv[:, :, hl:wd], psum_ap.rearrange("p (s w) -> p s w", w=tc.slen), reads=[psum_key], writes=[gk])
        if tc.kind == 'p':
            P.op('act', 'copy', hist_ap, gv[:, 0, tc.slen:tc.slen + hl], reads=[gk], writes=[hist_key])
        else:
            P.op('act', 'dma_start', out=st_out, in_=g[0:M, 0:tc.nseq * wd], reads=[gk], writes=[], dsem=st_sem)
        a = accb[i]
        av = a[0:M, 0:n].rearrange("p (s w) -> p s w", w=tc.slen)
        ak = ('acc', i)
        P.op('dve', 'tensor_scalar', av, gv[:, :, 0:tc.slen], wtaps[:, 0:1], None, ALU.mult, reads=[gk, 'pvec'], writes=[ak])
        for t_ in range(1, ntap):
            P.op('dve', 'scalar_tensor_tensor', av, gv[:, :, t_:t_ + tc.slen], wtaps[:, t_:t_ + 1], av, ALU.mult, ALU.add,
                 reads=[gk, ak, 'pvec'], writes=[ak])
        P.op('act', 'activation', out_ap, a[0:M, 0:n], AF.Silu, bias=bias_ap, reads=[ak, 'pvec'], writes=[out_key])

    EPS_AP = cst[:, 640:641]
    ONE24 = cst[0:24, 641:642]

    hctr = [0]
    for ti, tl in enumerate(tiles_p):
        P.epoch = ti + 1
        tc = TileCtx()
        if tl == 's':
            tc.kind, tc.ncol, tc.nseq, tc.slen, tc.nchunk, tc.W = 's', WS, NS, 4, 1, WS
            z_s, xc, bc, act_t = arena_views(WS, True)
            assert 6 * WS + 12 * WS + 4 * WS + (JC * WS + 1) // 2 <= 5600 and 5600 + 2 * (NS * 24 + 32) <= 8000
            daq = arena[:, 5600:5600 + NS * 24]
            cdall = arena[:, 5600 + NS * 24 + 32:5600 + 2 * NS * 24 + 32]
            P.op('dve', 'memset', rstd[:, 0:1], 0.0, writes=ARENA_KEYS + ['modall', 'rstd', 'daq', 'cdall'])
            tc.modkey = 'modall'
            tc.mod = lambda l, c: modall[:, c, 1:NB].unsqueeze(2).to_broadcast([128, NS, 4])
            P.op('sp', 'dma_start', out=xT[:, :, 0:WS], in_=xs, writes=[('x', k) for k in range(KC)], dsem='xin')
            TRI, SAME = tri_s, same_s
        else:
            tc.kind, tc.ncol, tc.nseq, tc.slen, tc.nchunk, tc.W = 'p', T, 1, T, T // 128, 128
            tc.modkey = 'mods_p'
            tc.mod = lambda l, c: mods_p[:, l, c:c + 1].unsqueeze(2).to_broadcast([128, 1, T])
            P.op('sp', 'dma_start', out=xT[:, :, :], in_=xp[:, :, tl * T:(tl + 1) * T], writes=[('x', k) for k in range(KC)], dsem='xin')
            TRI, SAME = tri_p, same_p
        n = tc.ncol
        W = tc.W
        first_p = (tl == 0)
        last_p = (tl == cfg.npt - 1)
        for l in range(DEPTH):
            if tl == 's':
                P.op('sp', 'dma_start', out=modall[:].rearrange("p a b -> p (a b)"), in_=mod_scr[l], reads=[('modscr', l)], writes=['modall'], dsem='msc')
                P.op('sp', 'dma_start', out=upool[:, :, 0:NS * 19], in_=st_pool[l], writes=['upool'], dsem='upin')
            P.op('pool', 'dma_start', out=wpool_t[:], in_=w_pool[l], writes=['wpool'], dsem='wp')
            P.op('act', 'activation', aneg[:, 0:1], hvec[:, 2 * l + 1:2 * l + 2], AF.Exp, reads=['hvec'], writes=['aneg'])
            P.op('dve', 'tensor_scalar', aneg[:, 1:2], aneg[:, 0:1], -1.0, None, ALU.mult, reads=['aneg'], writes=['aneg'])
            norm_mod(tc, l, 16, 0)
            if tc.kind == 'p':
                for g in range(4):
                    P.op('act', 'copy', upool[:, g, 0:15], hist_pool[:, l, g, :], reads=['hpool'], writes=['upool'])
            for j in range(19):
                wt, wk = w_get(KC, 256)
                for half in range(2):
                    c = 2 * j + half
                    if c > 36:
                        continue
                    M = 128 if c < 36 else 24
                    b = next_bank()
                    for k in range(KC):
                        P.op('pe', 'matmul', bank(b)[0:M, 0:n], wt[:, k, half * 128:half * 128 + M], hm[:, k, 0:n],
                             start=(k == 0), stop=(k == KC - 1), reads=[wk, ('hm', k)], writes=[('ps', b)])
                    pk = ('ps', b)
                    if c < 4:
                        uv = upool[:, c, 0:tc.nseq * (15 + tc.slen)].rearrange("p (s w) -> p s w", w=15 + tc.slen)
                        P.op('act', 'copy', uv[:, :, 15:15 + tc.slen], bank(b)[:, 0:n].rearrange("p (s w) -> p s w", w=tc.slen),
                             reads=[pk], writes=['upool'])
                    elif c < 16:
                        P.op('act', 'activation', z_s[:, c - 4, 0:n], bank(b)[:, 0:n], AF.Silu, reads=[pk], writes=[('z', c - 4)])
                    elif c < 36:
                        kk = c - 16
                        if kk < 12:
                            oap, okey = xc[:, kk, 0:n], ('xc', kk)
                        else:
                            oap, okey = bc[:, kk - 12, 0:n], ('bc', kk - 12)
                        conv_silu(tc, bank(b)[:, 0:n], pk, 4, hist_conv[:, l, kk, :], 'hconv',
                                  st_conv[l, :, kk, :] if tl == 's' else None, o_conv_s[l, :, kk, :] if tl == 's' else None, 'ocs',
                                  pv(l, O_CW + 4 * kk, 4), pv(l, O_CB + kk, 1), oap, okey)
                    else:
                        v = dtr[:, 0:n]
                        u = dtr[:, T:T + n]
                        P.op('act', 'activation', v, bank(b)[0:24, 0:n], AF.Identity, bias=hvec[:, 2 * l:2 * l + 1], reads=[pk, 'hvec'], writes=['dtr'])
                        P.op('act', 'activation', u, v, AF.Abs, reads=['dtr'], writes=['dtr'])
                        P.op('act', 'activation', u, u, AF.Exp, scale=-1.0, reads=['dtr'], writes=['dtr'])
                        P.op('act', 'activation', u, u, AF.Ln, bias=ONE24, reads=['dtr', 'cst'], writes=['dtr'])
                        P.op('dve', 'tensor_scalar', v, v, 0.0, None, ALU.max, reads=['dtr'], writes=['dtr'])
                        P.op('dve', 'tensor_tensor', v, v, u, ALU.add, reads=['dtr'], writes=['dtr'])
                        P.op('dve', 'tensor_scalar', u, v, aneg[:, 1:2], None, ALU.mult, reads=['dtr', 'aneg'], writes=['dtr'])
            if tc.kind == 's':
                P.op('act', 'dma_start', out=o_pool_s[l], in_=upool[:, :, 0:NS * 19], reads=['upool'], writes=[], dsem='ops')
            wdp = 15 + tc.slen
            for g in range(4):
                f = upool[:, g, 0:tc.nseq * wdp].rearrange("p (s w) -> p s w", w=wdp)
                A_ = pa[:, 0:tc.nseq * wdp].rearrange("p (s w) -> p s w", w=wdp)
                B_ = pb[:, 0:tc.nseq * wdp].rearrange("p (s w) -> p s w", w=wdp)
                P.op('dve', 'tensor_tensor', A_[:, :, 1:wdp], f[:, :, 1:wdp], f[:, :, 0:wdp - 1], ALU.add, reads=['upool'], writes=[('acc', 0)])
                cur, curk, oth, othk = A_, ('acc', 0), B_, ('acc', 1)
                sh = 2
                lo = 1
                for _ in range(g):
                    lo2 = lo + sh
                    P.op('dve', 'tensor_tensor', oth[:, :, lo2:wdp], cur[:, :, lo2:wdp], cur[:, :, lo2 - sh:wdp - sh], ALU.add, reads=[curk], writes=[othk])
                    cur, curk, oth, othk = oth, othk, cur, curk
                    sh *= 2
                    lo = lo2
                dv = dpool[:, 0:n].rearrange("p (s w) -> p s w", w=tc.slen)
                P.op('dve', 'scalar_tensor_tensor', dv, cur[:, :, 15:wdp], 1.0 / WINS[g], f[:, :, 15:wdp], ALU.mult, ALU.subtract,
                     reads=[curk, 'upool'], writes=[('sq', 0)])
                if first_p:
                    P.op('dve', 'tensor_tensor', oth[:, 0, 0:15], cur[:, 0, 15:30], invcnt[:, g, :], ALU.mult, reads=[curk, 'cst'], writes=[othk])
                    P.op('dve', 'tensor_tensor', dpool[:, 0:15], oth[:, 0, 0:15], f[:, 0, 15:30], ALU.subtract, reads=[othk, 'upool'], writes=[('sq', 0)])
                b = next_bank()
                P.op('pe', 'matmul', bank(b)[:, 0:n], wpool_t[:, g * 128:(g + 1) * 128], dpool[:, 0:n], start=True, stop=True,
                     reads=['wpool', ('sq', 0)], writes=[('ps', b)])
                P.op('act', 'activation', hm[:, g, 0:n], bank(b)[:, 0:n], AF.Identity, scale=pv(l, O_PSC + g, 1), reads=[('ps', b), 'pvec'], writes=[('hm', g)])
                if tc.kind == 'p':
                    P.op('act', 'copy', hist_pool[:, l, g, :], upool[:, g, tc.slen:tc.slen + 15], reads=['upool'], writes=['hpool'])
            for ci in range(tc.nchunk):
                c0 = ci * 128
                P.op('pe', 'transpose', bank(6)[0:W, 0:24], dtr[:, c0:c0 + W], ident[0:24, 0:24], reads=['dtr', 'cst'], writes=[('ps', 6)])
                P.op('pe', 'transpose', bank(6)[0:W, 24:48], dtr[:, T + c0:T + c0 + W], ident[0:24, 0:24], reads=['dtr', 'cst'], writes=[('ps', 6)])
                P.op('act', 'copy', stt[0:W, :], bank(6)[0:W, 0:48], reads=[('ps', 6)], writes=['stt'])
                P.op('pe', 'matmul', bank(6)[0:W, 64:88], TRI[0:W, 0:W], stt[0:W, 24:48], start=True, stop=True, reads=['stt', 'cst'], writes=[('ps', 6)])
                P.op('pe', 'matmul', bank(6)[0:W, 88:112], SAME[0:W, 0:W], stt[0:W, 24:48], start=True, stop=True, reads=['stt', 'cst'], writes=[('ps', 6)])
                P.op('act', 'copy', cst_t[0:W, :], bank(6)[0:W, 64:112], reads=[('ps', 6)], writes=['cs_t'])
                P.op('dve', 'tensor_tensor', sm[0:W, 72:96], cst_t[0:W, 24:48], cst_t[0:W, 0:24], ALU.subtract, reads=['cs_t'], writes=['sm'])
                P.op('act', 'activation', sm[0:W, 0:24], sm[0:W, 72:96], AF.Exp, reads=['sm'], writes=['sm'])
                P.op('act', 'activation', sm[0:W, 48:72], cst_t[0:W, 0:24], AF.Exp, reads=['cs_t'], writes=['sm'])
                P.op('dve', 'tensor_tensor', sm[0:W, 24:48], sm[0:W, 0:24], stt[0:W, 0:24], ALU.mult, reads=['sm', 'stt'], writes=['sm'])
                P.op('dve', 'tensor_scalar', sm[0:W, 72:96], cst_t[0:W, 0:24], -1.0, None, ALU.mult, reads=['cs_t', 'sm'], writes=['sm'])
                nsq = tc.nseq
                dq = daq[0:W, 0:nsq * 24].rearrange("p (q h) -> p q h", h=24)
                P.op('dve', 'tensor_tensor', dq, stt[0:W, 24:48].unsqueeze(1).to_broadcast([W, nsq, 24]),
                     rowmask_all[0:W, 0:nsq].unsqueeze(2).to_broadcast([W, nsq, 24]) if tc.kind == 's' else ones_f[0:W, 0:nsq].unsqueeze(2).to_broadcast([W, nsq, 24]),
                     ALU.mult, reads=['stt', 'cst'], writes=['daq'])
                tot = nsq * 24
                off = 0
                while off < tot:
                    wcols = min(384, tot - off)
                    P.op('pe', 'matmul', bank(7)[:, 0:wcols], ones_f[0:W, :], daq[0:W, off:off + wcols], start=True, stop=True,
                         reads=['daq', 'cst'], writes=[('ps', 7)])
                    P.op('act', 'activation', cdall[:, off:off + wcols], bank(7)[:, 0:wcols], AF.Exp, reads=[('ps', 7)], writes=['cdall'])
                    off += wcols
                pbt = bank(7).bitcast(BF16)
                for g in range(4):
                    P.op('pe', 'transpose', pbt[0:W, g * 128:(g + 1) * 128], bc[:, g, c0:c0 + W], ident_bf[:], reads=[('bc', g), 'identbf'], writes=[('ps', 7)])
                P.op('act', 'copy', btok[0:W, :], pbt[0:W, 0:512], reads=[('ps', 7)], writes=['btok'])
                A3 = ps[:, 0:1536]
                for k in range(12):
                    P.op('pe', 'transpose', A3[0:W, k * 128:(k + 1) * 128], xc[:, k, c0:c0 + W], ident, reads=[('xc', k), 'cst'],
                         writes=[('ps', 0), ('ps', 1), ('ps', 2)])
                A3v = A3[0:W, :].rearrange("p (h d) -> p h d", d=64)
                P.op('dve', 'tensor_tensor', xdt[0:W, :].rearrange("p (h d) -> p h d", d=64), A3v, stt[0:W, 0:24].unsqueeze(2).to_broadcast([W, 24, 64]),
                     ALU.mult, reads=[('ps', 0), ('ps', 1), ('ps', 2), 'stt'], writes=['xdt'])
                P.op('dve', 'tensor_tensor', xdtw[0:W, :].rearrange("p (h d) -> p h d", d=64), A3v, sm[0:W, 24:48].unsqueeze(2).to_broadcast([W, 24, 64]),
                     ALU.mult, reads=[('ps', 0), ('ps', 1), ('ps', 2), 'sm'], writes=['xdtw'])
                xcv = xc[:, :, c0:c0 + W]
                P.op('dve', 'tensor_tensor', xcv, xcv, pv(l, O_DS, 12).unsqueeze(2).to_broadcast([128, 12, W]), ALU.mult,
                     reads=[('xc', k) for k in range(12)] + ['pvec'], writes=[('xc', k) for k in range(12)])
                for g in range(4):
                    P.op('pe', 'matmul', bank(6)[0:W, g * 128:g * 128 + W], bc[:, g, c0:c0 + W], bc[:, 4 + g, c0:c0 + W], start=True, stop=True,
                         reads=[('bc', g), ('bc', 4 + g)], writes=[('ps', 6)])
                P.op('dve', 'tensor_tensor', cbm[0:W, :].rearrange("p (g w) -> p g w", w=128)[:, :, 0:W],
                     bank(6)[0:W, :].rearrange("p (g w) -> p g w", w=128)[:, :, 0:W],
                     TRI[0:W, 0:W].unsqueeze(1).to_broadcast([W, 4, W]), ALU.mult, reads=[('ps', 6), 'cst'], writes=['cbm'])
                B3 = ps[:, 1536:3072]
                if nsq > 1:
                    for b_ in range(3):
                        P.op('pe', 'matmul', A3[0:W, b_ * 512:(b_ + 1) * 512], zeros_bf[:, 0:W], hm[:, 0, 0:512], start=True, stop=True,
                             reads=['zerosbf', ('hm', 0)], writes=[('ps', 0), ('ps', 1), ('ps', 2)])
                def seq_prep(q_):
                    cq_, cqk_ = cqb[q_ % 2], ('cq', q_ % 2)
                    cqv_ = cq_[:, :].rearrange("p (g w) -> p g w", w=128)
                    if q_ >= 2:
                        P.op('dve', 'memset', cqv_[:, :, 4 * (q_ - 2):4 * (q_ - 2) + 4], 0.0, writes=[cqk_])
                    P.op('dve', 'tensor_copy', cqv_[:, :, 4 * q_:4 * q_ + 4], bc[:, 4:8, c0 + 4 * q_:c0 + 4 * q_ + 4],
                         reads=[('bc', 4), ('bc', 5), ('bc', 6), ('bc', 7)], writes=[cqk_])
                    bq_, bqk_ = bqb[q_ % 2], ('bq', q_ % 2)
                    P.op('dve', 'tensor_scalar', bq_[0:W, :], btok[0:W, :], rowmask_all[0:W, q_:q_ + 1], None, ALU.mult, reads=['btok', 'cst'], writes=[bqk_])
                for q in range(nsq):
                    if tc.kind == 'p' and ci > 0:
                        hi = tc.hi
                    else:
                        hi = hctr[0] % 2
                        hctr[0] += 1
                        tc.hi = hi
                    H, Hk = Hb[hi], ('H', hi)
                    Hf, Hfk = Hbf[0], ('Hbf', 0)
                    if tc.kind == 'p':
                        if ci == 0:
                            if first_p:
                                P.op('dve', 'memset', H[:], 0.0, writes=[Hk])
                            else:
                                P.op('sp', 'dma_start', out=H[:], in_=h_scr[l], reads=[('hscr', l)], writes=[Hk], dsem='hin%d' % hi)
                    else:
                        P.op('sp', 'dma_start', out=H[:], in_=st_ssm[l, q], writes=[Hk], dsem='hin%d' % hi)
                    P.op('act', 'copy', Hf[:], H[:], reads=[Hk], writes=[Hfk])
                    if tc.kind == 's':
                        cq, cqk = cqb[q % 2], ('cq', q % 2)
                        bq, bqk = bqb[q % 2], ('bq', q % 2)
                        if q == 0:
                            seq_prep(0)
                    for g in range(4):
                        if tc.kind == 's':
                            lhs = cq[:, g * 128:g * 128 + W]
                            rk = [cqk]
                        else:
                            lhs = bc[:, 4 + g, c0:c0 + W]
                            rk = [('bc', 4 + g)]
                        for pp in range(3):
                            c_ = g * 384 + pp * 128
                            P.op('pe', 'matmul', A3[0:W, c_:c_ + 128], lhs, Hf[:, c_:c_ + 128], start=(nsq == 1), stop=(q == nsq - 1), skip_group_check=(nsq > 1),
                                 reads=rk + [Hfk], writes=[('ps', 0), ('ps', 1), ('ps', 2)])
                    for g in range(4):
                        if tc.kind == 's':
                            lhs = bq[0:W, g * 128:(g + 1) * 128]
                            rk = [bqk]
                        else:
                            lhs = btok[0:W, g * 128:(g + 1) * 128]
                            rk = ['btok']
                        for pp in range(3):
                            c_ = g * 384 + pp * 128
                            P.op('pe', 'matmul', B3[:, c_:c_ + 128], lhs, xdtw[0:W, c_:c_ + 128], start=True, stop=True,
                                 reads=rk + ['xdtw'], writes=[('ps', 3), ('ps', 4), ('ps', 5)])
                    if tc.kind == 's' and q + 1 < nsq:
                        seq_prep(q + 1)
                    Hv = H[:, :].rearrange("p (h d) -> p h d", d=64)
                    P.op('dve', 'tensor_tensor', Hv, Hv, cdall[:, q * 24:(q + 1) * 24].unsqueeze(2).to_broadcast([128, 24, 64]), ALU.mult,
                         reads=[Hk, 'cdall'], writes=[Hk])
                    P.op('dve', 'tensor_tensor', H[:], H[:], B3, ALU.add, reads=[Hk, ('ps', 3), ('ps', 4), ('ps', 5)], writes=[Hk])
                    if tc.kind == 's':
                        P.op('act', 'dma_start', out=o_ssm_s[l, q], in_=H[:], reads=[Hk], writes=[], dsem='hout%d' % hi)
                    elif ci == tc.nchunk - 1:
                        if last_p:
                            P.op('act', 'dma_start', out=o_ssm_p[l], in_=H[:], reads=[Hk], writes=[], dsem='hout%d' % hi)
                        else:
                            P.op('act', 'dma_start', out=h_scr[l], in_=H[:], reads=[Hk], writes=[('hscr', l)], dsem='hout%d' % hi)
                if tc.kind == 's':
                    for q_ in range(max(0, nsq - 2), nsq):
                        P.op('dve', 'memset', cqb[q_ % 2][:, :].rearrange("p (g w) -> p g w", w=128)[:, :, 4 * q_:4 * q_ + 4], 0.0, writes=[('cq', q_ % 2)])
                P.op('dve', 'tensor_tensor', yo[0:W, :].rearrange("p (h d) -> p h d", d=64), A3v, sm[0:W, 48:72].unsqueeze(2).to_broadcast([W, 24, 64]),
                     ALU.mult, reads=[('ps', 0), ('ps', 1), ('ps', 2), 'sm'], writes=['yo'])
                def stageA(r):
                    h0 = 3 * r
                    sb_ = 6 + (r % 2)
                    da, dak = dab[r % 2], ('dab', r % 2)
                    dav = da[0:W, :].rearrange("p (j w) -> p j w", w=128)[:, :, 0:W]
                    for jj in range(3):
                        P.op('act', 'activation', da[0:W, jj * 128:jj * 128 + W], TRI[0:W, 0:W], AF.Identity, scale=stt[0:W, 24 + h0 + jj:24 + h0 + jj + 1],
                             reads=['stt', 'cst'], writes=[dak])
                    if W == 128:
                        P.op('pe', 'matmul', bank(sb_)[0:W, 0:384], ones_f[0:W, 0:W], da[0:W, 0:384], start=True, stop=True,
                             reads=[dak, 'cst'], writes=[('ps', sb_)])
                    else:
                        for jj in range(3):
                            P.op('pe', 'matmul', bank(sb_)[0:W, jj * 128:jj * 128 + W], ones_f[0:W, 0:W], da[0:W, jj * 128:jj * 128 + W], start=True, stop=True,
                                 reads=[dak, 'cst'], writes=[('ps', sb_)])

                def stageB(r):
                    h0 = 3 * r
                    sb_ = 6 + (r % 2)
                    te = teb[r % 2]
                    for jj in range(3):
                        P.op('act', 'activation', te[0:W, jj * 128:jj * 128 + W], bank(sb_)[0:W, jj * 128:jj * 128 + W], AF.Exp,
                             bias=sm[0:W, 72 + h0 + jj:72 + h0 + jj + 1], reads=[('ps', sb_), 'sm'], writes=[('teb', r % 2, jj)])

                def stageC(r):
                    g = r // 2
                    h0 = 3 * r
                    te = teb[r % 2]
                    mt, mtk = mtb[r % 2], ('mtb', r % 2)
                    tev = te[0:W, :].rearrange("p (j w) -> p j w", w=128)[:, :, 0:W]
                    P.op('dve', 'scalar_tensor_tensor', mt[0:W, :].rearrange("p (j w) -> p j w", w=128)[:, :, 0:W], tev, 1.0,
                         cbm[0:W, g * 128:g * 128 + W].unsqueeze(1).to_broadcast([W, 3, W]), ALU.min, ALU.mult,
                         reads=[('teb', r % 2, 0), ('teb', r % 2, 1), ('teb', r % 2, 2), 'cbm'], writes=[mtk])
                    for jj in range(3):
                        h = h0 + jj
                        P.op('pe', 'matmul', B3[0:W, h * 64:(h + 1) * 64], mt[0:W, jj * 128:jj * 128 + W], xdt[0:W, h * 64:(h + 1) * 64], start=True, stop=True,
                             reads=[mtk, 'xdt'], writes=[('ps', 3), ('ps', 4), ('ps', 5)])
                for it in range(10):
                    if it < 8:
                        stageA(it)
                    if 1 <= it < 9:
                        stageB(it - 1)
                    if it >= 2:
                        stageC(it - 2)
                P.op('dve', 'tensor_tensor', yo[0:W, :], yo[0:W, :], B3[0:W, :], ALU.add, reads=['yo', ('ps', 3), ('ps', 4), ('ps', 5)], writes=['yo'])
                for k in range(12):
                    P.op('pe', 'transpose', A3[:, k * 128:k * 128 + W], yo[0:W, k * 128:(k + 1) * 128], ident[0:W, 0:W], reads=['yo', 'cst'],
                         writes=[('ps', 0), ('ps', 1), ('ps', 2)])
                xcv = xc[:, :, c0:c0 + W]
                P.op('dve', 'tensor_tensor', xcv, xcv, A3.rearrange("p (k w) -> p k w", w=128)[:, :, 0:W], ALU.add,
                     reads=[('xc', k) for k in range(12)] + [('ps', 0), ('ps', 1), ('ps', 2)], writes=[('xc', k) for k in range(12)])
            if cfg.debug and tl == 's' and l == 0:
                P.op('sp', 'dma_start', out=dbg[:, 0:48], in_=stt[:, :], reads=['stt'], writes=[], dsem='dbg')
                P.op('sp', 'dma_start', out=dbg[:, 48:96], in_=cst_t[:, :], reads=['cs_t'], writes=[], dsem='dbg')
                P.op('sp', 'dma_start', out=dbg[:, 96:192], in_=sm[:, :], reads=['sm'], writes=[], dsem='dbg')
                P.op('sp', 'dma_start', out=dbg[:, 192:192 + NS * 24], in_=cdall[:, :], reads=['cdall'], writes=[], dsem='dbg')
                P.op('sp', 'dma_start', out=dbg[:, 1024:1536], in_=cbm[:, :], reads=['cbm'], writes=[], dsem='dbg')
                P.op('pool', 'dma_start', out=dbg[:, 1536:2048], in_=btok[:, :], reads=['btok'], writes=[], dsem='dbg2')
                P.op('pool', 'dma_start', out=dbg[:, 2048:3584], in_=xdt[:, :], reads=['xdt'], writes=[], dsem='dbg2')
                P.op('pool', 'dma_start', out=dbg[:, 3584:5120], in_=xdtw[:, :], reads=['xdtw'], writes=[], dsem='dbg2')
                P.op('sp', 'dma_start', out=dbg[:, 5120:6656], in_=yo[:, :], reads=['yo'], writes=[], dsem='dbg')
            b = next_bank()
            for k in range(12):
                P.op('dve', 'tensor_tensor', xc[:, k, 0:n], xc[:, k, 0:n], z_s[:, k, 0:n], ALU.mult, reads=[('xc', k), ('z', k)], writes=[('xc', k)])
                s = sqb[k % 2]
                P.op('act', 'activation', s[:, 0:n], xc[:, k, 0:n], AF.Square, reads=[('xc', k)], writes=[('sq', k % 2)])
                P.op('pe', 'matmul', bank(b)[:, 0:n], ones_bf[:], s[:, 0:n], start=(k == 0), stop=(k == 11), reads=[('sq', k % 2), 'onesbf'], writes=[('ps', b)])
            P.op('act', 'activation', rstd[:, 0:n], bank(b)[:, 0:n], AF.Sqrt, bias=EPS_AP, scale=1.0 / 1536, reads=[('ps', b), 'cst'], writes=['rstd'])
            P.op('dve', 'reciprocal', rstd[:, 0:n], rstd[:, 0:n], reads=['rstd'], writes=['rstd'])
            for k in range(12):
                P.op('dve', 'scalar_tensor_tensor', hm[:, 4 + k, 0:n], xc[:, k, 0:n], pv(l, O_SN + k, 1), rstd[:, 0:n], ALU.mult, ALU.mult,
                     reads=[('xc', k), 'rstd', 'pvec'], writes=[('hm', 4 + k)])
            for j in range(8):
                wt, wk = w_get(KC, 256)
                for half in range(2):
                    c = 2 * j + half
                    b = next_bank()
                    for k in range(KC):
                        P.op('pe', 'matmul', bank(b)[:, 0:n], wt[:, k, half * 128:(half + 1) * 128], hm[:, k, 0:n], start=(k == 0), stop=(k == KC - 1),
                             reads=[wk, ('hm', k)], writes=[('ps', b)])
                    resid_add(tc, l, b, c, 32)
            norm_mod(tc, l, 64, 48)
            for j in range(JC):
                wt, wk = w_get(KC, 256)
                bg = next_bank()
                for k in range(KC):
                    P.op('pe', 'matmul', bank(bg)[:, 0:n], wt[:, k, 0:128], hm[:, k, 0:n], start=(k == 0), stop=(k == KC - 1), reads=[wk, ('hm', k)], writes=[('ps', bg)])
                bv = next_bank()
                for k in range(KC):
                    P.op('pe', 'matmul', bank(bv)[:, 0:n], wt[:, k, 128:256], hm[:, k, 0:n], start=(k == 0), stop=(k == KC - 1), reads=[wk, ('hm', k)], writes=[('ps', bv)])
                si = j % 2
                conv_silu(tc, bank(bg)[:, 0:n], ('ps', bg), 3, hist_ffn[:, l, j, :], 'hffn',
                          st_ffn[l, :, j, :] if tl == 's' else None, o_ffn_s[l, :, j, :] if tl == 's' else None, 'ofs',
                          pv(l, O_FW + 3 * j, 3), pv(l, O_FB + j, 1), silb[si][:, 0:n], ('silb', 0))
                P.op('dve', 'tensor_tensor', act_t[:, j, 0:n], silb[si][:, 0:n], bank(bv)[:, 0:n], ALU.mult, reads=[('silb', 0), ('ps', bv)], writes=[('act', j)])
            for c in range(KC):
                b = next_bank()
                for hh in range(2):
                    wt, wk = w_get(22, 128)
                    nk = 22 if hh == 0 else 21
                    for kk in range(nk):
                        jj = hh * 22 + kk
                        P.op('pe', 'matmul', bank(b)[:, 0:n], wt[:, kk, :], act_t[:, jj, 0:n], start=(jj == 0), stop=(jj == JC - 1),
                             reads=[wk, ('act', jj)], writes=[('ps', b)])
                resid_add(tc, l, b, c, 80)
        b = next_bank()
        for k in range(KC):
            s = sqb[k % 2]
            P.op('act', 'activation', s[:, 0:n], xT[:, k, 0:n], AF.Square, reads=[('x', k)], writes=[('sq', k % 2)])
            P.op('pe', 'matmul', bank(b)[:, 0:n], ones_bf[:], s[:, 0:n], start=(k == 0), stop=(k == KC - 1), reads=[('sq', k % 2), 'onesbf'], writes=[('ps', b)])
        P.op('act', 'activation', rstd[:, 0:n], bank(b)[:, 0:n], AF.Sqrt, bias=EPS_AP, scale=1.0 / D, reads=[('ps', b), 'cst'], writes=['rstd'])
        P.op('dve', 'reciprocal', rstd[:, 0:n], rstd[:, 0:n], reads=['rstd'], writes=['rstd'])
        for k in range(KC):
            P.op('dve', 'scalar_tensor_tensor', xT[:, k, 0:n], xT[:, k, 0:n], pvec[:, DEPTH * PL + k:DEPTH * PL + k + 1], rstd[:, 0:n], ALU.mult, ALU.mult,
                 reads=[('x', k), 'rstd', 'pvec'], writes=[('x', k)])
        if tl == 's':
            P.op('sp', 'dma_start', out=o_ys, in_=xT[:, :, 0:WS], reads=[('x', k) for k in range(KC)], writes=[], dsem='xout')
        else:
            P.op('sp', 'dma_start', out=o_yp[:, :, tl * T:(tl + 1) * T], in_=xT[:, :, :], reads=[('x', k) for k in range(KC)], writes=[], dsem='xout')
        if tl != 's' and last_p:
            for l in range(DEPTH):
                P.op('sp', 'dma_start', out=o_pool_p[l], in_=hist_pool[:, l].rearrange("p a b -> p (a b)"), reads=['hpool'], writes=[], dsem='hst')
                P.op('sp', 'dma_start', out=o_conv_p[l], in_=hist_conv[:, l].rearrange("p a b -> p (a b)"), reads=['hconv'], writes=[], dsem='hst')
                P.op('sp', 'dma_start', out=o_ffn_p[l], in_=hist_ffn[:, l].rearrange("p a b -> p (a b)"), reads=['hffn'], writes=[], dsem='hst')
    assert wstate['next'] == len(wlist), (wstate, len(wlist))
    P.analyze()
    P.emit(nc, ['dbg', 'dbg2', 'xout', 'hst', 'hout0', 'hout1', 'ops', 'ocs', 'ofs', 'msc'])
    return nc, len(P.ops)


def _fm(v, nchunk):
    return np.ascontiguousarray(np.asarray(v, np.float32).reshape(nchunk, 128).T)


def _wtile(w, ncols_pad, tile_cols):
    K, N = w.shape
    if N < ncols_pad:
        w = np.concatenate([w, np.zeros((K, ncols_pad - N), np.float32)], axis=1)
    kc = K // 128
    nt = ncols_pad // tile_cols
    a = w.reshape(kc, 128, nt, tile_cols).transpose(2, 1, 0, 3)
    return np.ascontiguousarray(a).reshape(nt, 128, kc * tile_cols)


def make_consts(cfg):
    NS, WS = cfg.ns, cfg.ws
    NCST = 128 * 6 + NS + 60
    c = np.zeros((128, NCST), np.float32)
    c[:, 0:128] = np.eye(128, dtype=np.float32)
    idx = np.arange(128)
    c[:, 128:256] = (idx[:, None] <= idx[None, :]).astype(np.float32)
    c[:, 256:384] = 1.0
    same = (idx[:, None] // 4 == idx[None, :] // 4)
    c[:, 384:512] = (same & (idx[:, None] <= idx[None, :])).astype(np.float32)
    c[:, 512:640] = same.astype(np.float32)
    c[:, 640] = EPS
    c[:, 641] = 1.0
    for q in range(NS):
        c[:, 768 + q] = (idx // 4 == q).astype(np.float32)
    for g, w in enumerate(WINS):
        for t in range(15):
            c[:, 768 + NS + g * 15 + t] = 1.0 / min(t + 1, w)
    cm = np.zeros((128, NS, WS), np.float32)
    for q in range(NS):
        cm[:, q, 4 * q:4 * q + 4] = 1.0
    return c, cm.reshape(128, NS * WS)


def prep_weights(cfg, inp):
    DEPTH = cfg.depth
    out = {}
    out['w_ada'] = np.stack([_wtile(np.asarray(inp['w_ada'][l]), 12288, 256) for l in range(DEPTH)])
    out['w_in'] = np.stack([_wtile(np.asarray(inp['w_in'][l]), 19 * 256, 256) for l in range(DEPTH)])
    out['w_out'] = np.stack([_wtile(np.asarray(inp['w_out'][l]), 2048, 256) for l in range(DEPTH)])
    wu = []
    for l in range(DEPTH):
        w = np.asarray(inp['w_up'][l])
        g = w[:, :DFF].reshape(KC, 128, JC, 128)
        v = w[:, DFF:].reshape(KC, 128, JC, 128)
        t = np.concatenate([g, v], axis=3)
        wu.append(np.ascontiguousarray(t.transpose(2, 1, 0, 3)).reshape(JC, 128, KC * 256))
    out['w_up'] = np.stack(wu)
    wd = []
    for l in range(DEPTH):
        w = np.asarray(inp['w_down'][l])
        w = np.concatenate([w, np.zeros((128, D), np.float32)], axis=0).reshape(2, 22, 128, KC, 128)
        wd.append(np.ascontiguousarray(w.transpose(3, 0, 2, 1, 4)).reshape(32, 128, 22 * 128))
    out['w_down'] = np.stack(wd)
    out['w_pool'] = np.stack([np.ascontiguousarray(np.asarray(inp['pool_w'][l]).transpose(1, 0, 2)).reshape(128, 512) for l in range(DEPTH)])
    pvec = np.zeros((128, DEPTH * PL + 16), np.float32)
    for l in range(DEPTH):
        o = l * PL
        pvec[:, o + O_N1:o + O_N1 + 16] = _fm(inp['norm1'][l], 16)
        pvec[:, o + O_N2:o + O_N2 + 16] = _fm(inp['norm2'][l], 16)
        pvec[:, o + O_BADA:o + O_BADA + 96] = _fm(inp['b_ada'][l], 96)
        pvec[:, o + O_PSC:o + O_PSC + 4] = _fm(inp['pool_scale'][l], 4)
        cw = np.asarray(inp['conv_w'][l]).reshape(4, 20, 128).transpose(2, 1, 0)
        pvec[:, o + O_CW:o + O_CW + 80] = cw.reshape(128, 80)
        pvec[:, o + O_CB:o + O_CB + 20] = _fm(inp['conv_b'][l], 20)
        pvec[:, o + O_SN:o + O_SN + 12] = _fm(inp['ssd_norm'][l], 12)
        fw = np.asarray(inp['ffn_conv_w'][l]).reshape(3, JC, 128).transpose(2, 1, 0)
        pvec[:, o + O_FW:o + O_FW + 129] = fw.reshape(128, 129)
        pvec[:, o + O_FB:o + O_FB + JC] = _fm(inp['ffn_conv_b'][l], JC)
        pvec[:, o + O_DS:o + O_DS + 12] = _fm(np.repeat(np.asarray(inp['d_skip'][l]), 64), 12)
    pvec[:, DEPTH * PL:DEPTH * PL + 16] = _fm(inp['norm_f'], 16)
    out['pvec'] = pvec
    hv = np.zeros((24, 2 * DEPTH), np.float32)
    for l in range(DEPTH):
        hv[:, 2 * l] = np.asarray(inp['dt_bias'][l])
        hv[:, 2 * l + 1] = np.asarray(inp['a_log'][l])
    out['hvec'] = hv
    return out


def _fm_tokens(x):
    t = x.shape[0]
    return np.ascontiguousarray(np.asarray(x, np.float32).T.reshape(KC, 128, t).transpose(1, 0, 2))


def _unfm_tokens(a):
    t = a.shape[2]
    return np.ascontiguousarray(a.transpose(1, 0, 2).reshape(D, t).T)


def core_inputs(cfg, inp, shared, b, seqs):
    DEPTH, NS, LP = cfg.depth, cfg.ns, cfg.lp
    m = dict(shared)
    m['xp'] = _fm_tokens(np.asarray(inp['x_prompt'][b][:LP]))
    xs = np.asarray(inp['x_sample'])[seqs].reshape(NS * 4, D)
    m['xs'] = _fm_tokens(xs)
    c = np.concatenate([np.asarray(inp['c_prompt'])[b:b + 1], np.asarray(inp['c_sample'])[seqs]], axis=0)
    m['cT'] = _fm_tokens(c)

    def hist_fm(st, nch, hl):
        a = np.asarray(st)[:DEPTH][:, seqs]
        a = a.reshape(DEPTH, NS, hl, nch, 128).transpose(0, 4, 3, 1, 2)
        o = np.zeros((DEPTH, 128, nch, NS, hl + 4), np.float32)
        o[..., :hl] = a
        return o.reshape(DEPTH, 128, nch, NS * (hl + 4))
    m['st_pool'] = hist_fm(inp['state_pool'], 4, 15)
    m['st_conv'] = hist_fm(inp['state_conv'], 20, 3)
    m['st_ffn'] = hist_fm(inp['state_ffn'], JC, 2)
    s = np.asarray(inp['state_ssm'])[:DEPTH][:, seqs]
    m['st_ssm'] = np.ascontiguousarray(s.transpose(0, 1, 4, 2, 3)).reshape(DEPTH, NS, 128, 1536)
    return m


def core_outputs(cfg, r):
    DEPTH, NS = cfg.depth, cfg.ns
    o = {}
    o['yp'] = _unfm_tokens(r['o_yp'])
    o['ys'] = _unfm_tokens(r['o_ys']).reshape(NS, 4, D)
    o['pool_p'] = r['o_pool_p'].reshape(DEPTH, 128, 4, 15).transpose(0, 3, 2, 1).reshape(DEPTH, 15, 512)
    o['conv_p'] = r['o_conv_p'].reshape(DEPTH, 128, 20, 3).transpose(0, 3, 2, 1).reshape(DEPTH, 3, 2560)
    o['ffn_p'] = r['o_ffn_p'].reshape(DEPTH, 128, JC, 2).transpose(0, 3, 2, 1).reshape(DEPTH, 2, DFF)
    o['ssm_p'] = r['o_ssm_p'].reshape(DEPTH, 128, 24, 64).transpose(0, 2, 3, 1)
    o['pool_s'] = r['o_pool_s'].reshape(DEPTH, 128, 4, NS, 19)[..., 4:].transpose(0, 3, 4, 2, 1).reshape(DEPTH, NS, 15, 512)
    o['conv_s'] = r['o_conv_s'].reshape(DEPTH, 128, 20, NS, 7)[..., 4:].transpose(0, 3, 4, 2, 1).reshape(DEPTH, NS, 3, 2560)
    o['ffn_s'] = r['o_ffn_s'].reshape(DEPTH, 128, JC, NS, 6)[..., 4:].transpose(0, 3, 4, 2, 1).reshape(DEPTH, NS, 2, DFF)
    o['ssm_s'] = r['o_ssm_s'].reshape(DEPTH, NS, 128, 24, 64).transpose(0, 1, 3, 4, 2)
    return o


_CACHE = {}


def kernel(**inp):
    cfg = Cfg(depth=4, lp=2048, ns=32, T=512)
    NCORE = 4
    if 'nc' not in _CACHE:
        _CACHE['nc'] = build_program(cfg)[0]
    nc = _CACHE['nc']
    shared = prep_weights(cfg, inp)
    cst, cm = make_consts(cfg)
    shared['cst'] = cst
    in_maps = []
    for b in range(NCORE):
        seqs = np.arange(b * cfg.ns, (b + 1) * cfg.ns)
        in_maps.append(core_inputs(cfg, inp, shared, b, seqs))
    res = run_bass_kernel_spmd(nc, in_maps, core_ids=list(range(NCORE)))
    outs = [core_outputs(cfg, r) for r in res.results]
    f = np.float32
    y_prompt = np.stack([o['yp'] for o in outs]).astype(f)
    y_sample = np.concatenate([o['ys'] for o in outs], axis=0).astype(f)
    pool_p = np.stack([o['pool_p'] for o in outs], axis=1).astype(f)
    conv_p = np.stack([o['conv_p'] for o in outs], axis=1).astype(f)
    ssm_p = np.stack([o['ssm_p'] for o in outs], axis=1).astype(f)
    ffn_p = np.stack([o['ffn_p'] for o in outs], axis=1).astype(f)
    pool_s = np.concatenate([o['pool_s'] for o in outs], axis=1).astype(f)
    conv_s = np.concatenate([o['conv_s'] for o in outs], axis=1).astype(f)
    ssm_s = np.concatenate([o['ssm_s'] for o in outs], axis=1).astype(f)
    ffn_s = np.concatenate([o['ffn_s'] for o in outs], axis=1).astype(f)
    return (y_prompt, y_sample, pool_p, conv_p, ssm_p, ffn_p, pool_s, conv_s, ssm_s, ffn_s)
```

```python
import bisect
import numpy as np
import ml_dtypes
import concourse.bass as bass
import concourse.mybir as mybir
from concourse.bass_utils import run_bass_kernel_spmd

F32 = mybir.dt.float32
BF16 = mybir.dt.bfloat16
ALU = mybir.AluOpType
AF = mybir.ActivationFunctionType

D = 2048
KC = 16
DFF = 5504
JC = 43
NHEAD = 24
EPS = 1e-6
PL = 428
O_N1, O_N2, O_BADA, O_PSC, O_CW, O_CB, O_SN, O_FW, O_FB, O_DS = 0, 16, 32, 128, 132, 212, 232, 244, 373, 416
WINS = (2, 4, 8, 16)
SAME_ENGINE_SYNC = True


class Prog:
    def __init__(self):
        self.ops = []
        self.overlaps = {}
        self.epoch = 0

    def alias(self, a_keys, b_keys):
        for a in a_keys:
            self.overlaps.setdefault(a, set()).update(b_keys)
        for b in b_keys:
            self.overlaps.setdefault(b, set()).update(a_keys)

    def op(self, eng, method, *args, reads=(), writes=(), dsem=None, **kw):
        self.ops.append([eng, method, args, kw, tuple(reads), tuple(writes), dsem, self.epoch, None, False, 0])

    def analyze(self):
        lastw = {}
        readers = {}
        dma_lists = {}
        for i, o in enumerate(self.ops):
            eng, reads, writes, dsem = o[0], o[4], o[5], o[6]
            deps = {}

            def add(j):
                if j is None or j == i:
                    return
                s = self.ops[j]
                src = ('d', s[6]) if s[6] is not None else ('e', s[0], s[7])
                if deps.get(src, -1) < j:
                    deps[src] = j
            wr = set(writes)
            for w in writes:
                ov = self.overlaps.get(w)
                if ov:
                    wr |= ov
            for r in reads:
                add(lastw.get(r))
            for w in wr:
                add(lastw.get(w))
                rd = readers.get(w)
                if rd:
                    for j in rd.values():
                        add(j)
            o[8] = deps
            me = ('d', dsem) if dsem is not None else ('e', eng)
            for r in reads:
                readers.setdefault(r, {})[me] = i
            for w in wr:
                lastw[w] = i
                readers[w] = {}
            if dsem is not None:
                dma_lists.setdefault(dsem, []).append(i)
        self.dma_lists = dma_lists
        for i, o in enumerate(self.ops):
            for src, j in o[8].items():
                if src[0] == 'e':
                    s = self.ops[j]
                    if s[0] != o[0] or (SAME_ENGINE_SYNC and o[0] in ('act', 'dve', 'pool') and o[6] is None):
                        s[9] = True
                    elif s[0] == o[0] and o[6] is not None:
                        s[9] = True
        cnt = {}
        for o in self.ops:
            if o[6] is None and o[9]:
                k = (o[0], o[7])
                cnt[k] = cnt.get(k, 0) + 1
                o[10] = cnt[k]
        self.sem_keys = sorted(cnt.keys())

    def emit(self, nc, final_waits):
        engs = {'pe': 'tensor', 'act': 'scalar', 'dve': 'vector', 'pool': 'gpsimd', 'sp': 'sync'}
        csem = {k: nc.alloc_semaphore("c_%s_%d" % k) for k in self.sem_keys}
        dsem = {k: nc.alloc_semaphore("d_%s" % k) for k in self.dma_lists}
        ops = self.ops
        by_eng = {e: [] for e in engs}
        for i, o in enumerate(ops):
            by_eng[o[0]].append(i)
        dma_lists = self.dma_lists

        def run(ename, eng):
            waited = {}
            for i in by_eng[ename]:
                o = ops[i]
                need = {}
                for src, j in o[8].items():
                    s = ops[j]
                    if src[0] == 'd':
                        sem = dsem[src[1]]
                        val = 16 * bisect.bisect_left(dma_lists[src[1]], i)
                        key = ('d', src[1])
                    else:
                        if not s[9]:
                            continue
                        if s[0] == ename and o[6] is None and not (SAME_ENGINE_SYNC and ename in ('act', 'dve', 'pool')):
                            continue
                        sem = csem[(s[0], s[7])]
                        val = s[10]
                        key = ('e', s[0], s[7])
                    if need.get(key, (None, 0))[1] < val:
                        need[key] = (sem, val)
                for key, (sem, val) in need.items():
                    if waited.get(key, 0) < val:
                        eng.wait_ge(sem, val)
                        waited[key] = val
                ins = getattr(eng, o[1])(*o[2], **o[3])
                if o[6] is not None:
                    ins.then_inc(dsem[o[6]], 16)
                elif o[9]:
                    ins.then_inc(csem[(o[0], o[7])], 1)
            if ename == 'sp':
                for k in final_waits:
                    if k in dma_lists:
                        eng.wait_ge(dsem[k], 16 * len(dma_lists[k]))

        with nc.Block() as block:
            @block.tensor
            def _(e):
                run('pe', e)

            @block.scalar
            def _(e):
                run('act', e)

            @block.vector
            def _(e):
                run('dve', e)

            @block.gpsimd
            def _(e):
                run('pool', e)

            @block.sync
            def _(e):
                run('sp', e)


class Cfg:
    def __init__(self, depth=4, lp=2048, ns=32, T=512):
        self.depth, self.lp, self.ns, self.T = depth, lp, ns, T
        self.ws = ns * 4
        self.debug = False
        self.npt = lp // T


def build_program(cfg):
    DEPTH, LP, NS, T = cfg.depth, cfg.lp, cfg.ns, cfg.T
    WS = cfg.ws
    NB = 1 + NS
    nc = bass.Bass("TRN2", target_bir_lowering=False)
    P = Prog()

    def din(name, shape):
        return nc.dram_tensor(name, list(shape), F32, kind="ExternalInput").ap()

    def dout(name, shape):
        return nc.dram_tensor(name, list(shape), F32, kind="ExternalOutput").ap()

    xp = din("xp", [128, KC, LP])
    xs = din("xs", [128, KC, WS])
    cT = din("cT", [128, KC, NB])
    st_pool = din("st_pool", [DEPTH, 128, 4, NS * 19])
    st_conv = din("st_conv", [DEPTH, 128, 20, NS * 7])
    st_ffn = din("st_ffn", [DEPTH, 128, JC, NS * 6])
    st_ssm = din("st_ssm", [DEPTH, NS, 128, 1536])
    pvec_d = din("pvec", [128, DEPTH * PL + 16])
    hvec_d = din("hvec", [24, 2 * DEPTH])
    NCST = 128 * 6 + NS + 60
    cst_d = din("cst", [128, NCST])
    w_ada = din("w_ada", [DEPTH, 48, 128, KC * 256])
    w_in = din("w_in", [DEPTH, 19, 128, KC * 256])
    w_pool = din("w_pool", [DEPTH, 128, 4 * 128])
    w_out = din("w_out", [DEPTH, 8, 128, KC * 256])
    w_up = din("w_up", [DEPTH, JC, 128, KC * 256])
    w_down = din("w_down", [DEPTH, 32, 128, 22 * 128])

    o_yp = dout("o_yp", [128, KC, LP])
    o_ys = dout("o_ys", [128, KC, WS])
    o_pool_p = dout("o_pool_p", [DEPTH, 128, 4 * 15])
    o_conv_p = dout("o_conv_p", [DEPTH, 128, 20 * 3])
    o_ssm_p = dout("o_ssm_p", [DEPTH, 128, 1536])
    o_ffn_p = dout("o_ffn_p", [DEPTH, 128, JC * 2])
    o_pool_s = dout("o_pool_s", [DEPTH, 128, 4, NS * 19])
    o_conv_s = dout("o_conv_s", [DEPTH, 128, 20, NS * 7])
    o_ssm_s = dout("o_ssm_s", [DEPTH, NS, 128, 1536])
    o_ffn_s = dout("o_ffn_s", [DEPTH, 128, JC, NS * 6])
    dbg = dout("dbg", [128, 8192]) if cfg.debug else None
    h_scr = nc.dram_tensor("h_scr", [DEPTH, 128, 1536], F32, kind="Internal").ap()
    mod_scr = nc.dram_tensor("mod_scr", [DEPTH, 128, 96 * NB], F32, kind="Internal").ap()

    def sb(name, shape, dt=F32):
        return nc.alloc_sbuf_tensor("s_" + name, list(shape), dt)

    UPW = max(15 + T, NS * 19)
    GBW = max(3 + T, NS * 7)
    xT = sb("xT", [128, KC, T])
    hm = sb("hm", [128, KC, T], BF16)
    arena = sb("arena", [128, 11264])
    def arena_views(t, compact):
        if not compact:
            z = arena[:, 0:3072].bitcast(BF16).rearrange("p (k t) -> p k t", t=T)
            x_ = arena[:, 3072:9216].rearrange("p (k t) -> p k t", t=T)
            b_ = arena[:, 9216:11264].bitcast(BF16).rearrange("p (k t) -> p k t", t=T)
            a_ = arena[:, 0:JC * T // 2].bitcast(BF16).rearrange("p (k t) -> p k t", t=T)
        else:
            o1 = 6 * t
            o2 = o1 + 12 * t
            o3 = o2 + 4 * t
            o4 = o3 + (JC * t + 1) // 2
            assert o4 <= 8000
            z = arena[:, 0:o1].bitcast(BF16).rearrange("p (k t) -> p k t", t=t)
            x_ = arena[:, o1:o2].rearrange("p (k t) -> p k t", t=t)
            b_ = arena[:, o2:o3].bitcast(BF16).rearrange("p (k t) -> p k t", t=t)
            a_ = arena[:, o3:o3 + (JC * t) // 2].bitcast(BF16).rearrange("p (k t) -> p k t", t=t)
        return z, x_, b_, a_
    z_s, xc, bc, act_t = arena_views(T, False)
    assert 96 * NB <= 11264 - 8000
    modall = arena[:, 8000:8000 + 96 * NB].rearrange("p (a b) -> p a b", b=NB)
    upool = sb("upool", [128, 4, UPW])
    wbuf = [sb("wbuf%d" % i, [128, 4096], BF16) for i in range(3)]
    wpool_t = sb("wpool_t", [128, 512], BF16)
    gbuf = [sb("gbuf%d" % i, [128, GBW]) for i in range(2)]
    accb = [sb("accb%d" % i, [128, max(T, UPW)]) for i in range(2)]
    pa, pb = accb[0], accb[1]
    silb = [sb("silb0", [128, T])] * 2
    sqb = [sb("sqb%d" % i, [128, T], BF16) for i in range(2)]
    dpool = sqb[0]
    tmpn = accb
    rstd = sb("rstd", [128, T])
    dtr = sb("dtr", [24, 2 * T])
    aneg = sb("aneg", [24, 2])
    stt = sb("stt", [128, 48])
    cst_t = sb("cs_t", [128, 48])
    sm = sb("sm", [128, 96])
    daq_p = sb("daq", [128, 24])
    cdall_p = sb("cdall", [128, 24])
    daq, cdall = daq_p, cdall_p
    xdt = sb("xdt", [128, 1536], BF16)
    xdtw = sb("xdtw", [128, 1536], BF16)
    btok = sb("btok", [128, 512], BF16)
    cbm = sb("cbm", [128, 512])
    dab = [sb("dab%d" % i, [128, 384]) for i in range(2)]
    teb = [sb("teb%d" % i, [128, 384]) for i in range(2)]
    mtb = [sb("mtb%d" % i, [128, 384], BF16) for i in range(2)]
    yo = sb("yo", [128, 1536])
    Hb = [sb("H%d" % i, [128, 1536]) for i in range(2)]
    Hbf = [sb("Hbf0", [128, 1536], BF16)] * 2
    cqb = [sb("cq%d" % i, [128, 512], BF16) for i in range(2)]
    bqb = [sb("bq%d" % i, [128, 512], BF16) for i in range(2)]
    pvec = sb("pvec", [128, DEPTH * PL + 16])
    hvec = sb("hvec", [24, 2 * DEPTH])
    cst = sb("cst", [128, NCST])
    ident_bf = sb("ident_bf", [128, 128], BF16)
    ones_bf = sb("ones_bf", [128, 128], BF16)
    zeros_bf = sb("zeros_bf", [128, 128], BF16)
    mods_p = sb("mods_p", [128, DEPTH, 96])
    csil = sb("csil", [128, KC, NB], BF16)
    cin = arena[:, 0:KC * NB].rearrange("p (a b) -> p a b", b=NB)
    hist_pool = sb("hist_pool", [128, DEPTH, 4, 15])
    hist_conv = sb("hist_conv", [128, DEPTH, 20, 3])
    hist_ffn = sb("hist_ffn", [128, DEPTH, JC, 2])
    ps = nc.alloc_psum_tensor("ps", [128, 4096], F32)

    ident = cst[:, 0:128]
    tri_p = cst[:, 128:256]
    same_p = cst[:, 256:384]
    tri_s = cst[:, 384:512]
    same_s = cst[:, 512:640]
    ones_f = cst[:, 256:384]
    rowmask_all = cst[:, 768:768 + NS]
    invcnt = cst[:, 768 + NS:768 + NS + 60].rearrange("p (g t) -> p g t", t=15)

    ARENA_KEYS = [('act', j) for j in range(JC)] + [('z', k) for k in range(12)] + [('xc', k) for k in range(12)] + [('bc', k) for k in range(8)]
    P.alias(['modall'], ARENA_KEYS)
    P.alias(['cin'], ARENA_KEYS)
    P.alias([('act', j) for j in range(JC)], [('z', k) for k in range(12)] + [('xc', k) for k in range(12)] + [('bc', k) for k in range(8)])

    def bank(b):
        return ps[:, 512 * b:512 * (b + 1)]
    mmctr = [0]

    def next_bank():
        b = mmctr[0] % 8
        mmctr[0] += 1
        return b

    tiles_p = list(range(cfg.npt)) + (['s'] if NS > 0 else [])
    wlist = []
    for l in range(DEPTH):
        for j in range(48):
            wlist.append((w_ada[l, j], KC, 256))
    for tl in tiles_p:
        for l in range(DEPTH):
            for j in range(19):
                wlist.append((w_in[l, j], KC, 256))
            for j in range(8):
                wlist.append((w_out[l, j], KC, 256))
            for j in range(JC):
                wlist.append((w_up[l, j], KC, 256))
            for j in range(32):
                wlist.append((w_down[l, j], 22, 128))
    wstate = {'issued': 0, 'next': 0}
    NSLOT = 3

    def w_issue(n):
        while wstate['issued'] < min(n, len(wlist)):
            i = wstate['issued']
            ap, a, b = wlist[i]
            s = i % NSLOT
            P.op('pool', 'dma_start', out=wbuf[s][:, 0:a * b], in_=ap, writes=[('w', s)], dsem='w%d' % s)
            wstate['issued'] += 1

    def w_get(a, b):
        i = wstate['next']
        assert wlist[i][1] == a and wlist[i][2] == b, (i, wlist[i][1:], a, b)
        w_issue(i + NSLOT)
        wstate['next'] += 1
        s = i % NSLOT
        return wbuf[s][:, 0:a * b].rearrange("p (a b) -> p a b", b=b), ('w', s)

    P.op('sp', 'dma_start', out=pvec[:], in_=pvec_d, writes=['pvec'], dsem='ld')
    P.op('sp', 'dma_start', out=hvec[:], in_=hvec_d, writes=['hvec'], dsem='ld')
    P.op('sp', 'dma_start', out=cst[:], in_=cst_d, writes=['cst'], dsem='ld')
    P.op('sp', 'dma_start', out=cin[:], in_=cT, writes=['cin'], dsem='ld')
    for i_ in range(2):
        P.op('dve', 'memset', cqb[i_][:], 0.0, writes=[('cq', i_)])
    P.op('dve', 'tensor_copy', ident_bf[:], ident, reads=['cst'], writes=['identbf'])
    P.op('dve', 'tensor_copy', ones_bf[:], ones_f, reads=['cst'], writes=['onesbf'])
    P.op('dve', 'memset', zeros_bf[:], 0.0, writes=['zerosbf'])
    P.op('dve', 'memset', hist_pool[:], 0.0, writes=['hpool'])
    P.op('dve', 'memset', hist_conv[:], 0.0, writes=['hconv'])
    P.op('dve', 'memset', hist_ffn[:], 0.0, writes=['hffn'])
    P.op('act', 'activation', csil[:], cin[:], AF.Silu, reads=['cin'], writes=['csil'])

    def pv(l, off, n):
        return pvec[:, l * PL + off:l * PL + off + n]

    for l in range(DEPTH):
        for j in range(48):
            wt, wk = w_get(KC, 256)
            for half in range(2):
                c = 2 * j + half
                b = next_bank()
                for k in range(KC):
                    P.op('pe', 'matmul', bank(b)[:, 0:NB], wt[:, k, half * 128:(half + 1) * 128], csil[:, k, :],
                         start=(k == 0), stop=(k == KC - 1), reads=[wk, 'csil'], writes=[('ps', b)])
                one = 1.0 if (16 <= c < 32 or 64 <= c < 80) else 0.0
                P.op('dve', 'tensor_scalar', modall[:, c, :], bank(b)[:, 0:NB], pv(l, O_BADA + c, 1), one, ALU.add, ALU.add,
                     reads=[('ps', b), 'pvec'], writes=['modall'])
        P.op('dve', 'tensor_tensor', modall[:, 16:32, :], modall[:, 16:32, :], pv(l, O_N1, 16).unsqueeze(2).to_broadcast([128, 16, NB]),
             ALU.mult, reads=['modall', 'pvec'], writes=['modall'])
        P.op('dve', 'tensor_tensor', modall[:, 64:80, :], modall[:, 64:80, :], pv(l, O_N2, 16).unsqueeze(2).to_broadcast([128, 16, NB]),
             ALU.mult, reads=['modall', 'pvec'], writes=['modall'])
        P.op('dve', 'tensor_copy', mods_p[:, l, :], modall[:, :, 0], reads=['modall'], writes=['mods_p'])
        if NS > 0:
            P.op('sp', 'dma_start', out=mod_scr[l], in_=modall[:].rearrange("p a b -> p (a b)"), reads=['modall'], writes=[('modscr', l)], dsem='msc')

    class TileCtx:
        pass

    def norm_mod(tc, l, sc_c, sh_c):
        n = tc.ncol
        b = next_bank()
        for k in range(KC):
            s = sqb[k % 2]
            P.op('act', 'activation', s[:, 0:n], xT[:, k, 0:n], AF.Square, reads=[('x', k)], writes=[('sq', k % 2)])
            P.op('pe', 'matmul', bank(b)[:, 0:n], ones_bf[:], s[:, 0:n], start=(k == 0), stop=(k == KC - 1),
                 reads=[('sq', k % 2), 'onesbf'], writes=[('ps', b)])
        P.op('act', 'activation', rstd[:, 0:n], bank(b)[:, 0:n], AF.Sqrt, bias=EPS_AP, scale=1.0 / D, reads=[('ps', b), 'cst'], writes=['rstd'])
        P.op('dve', 'reciprocal', rstd[:, 0:n], rstd[:, 0:n], reads=['rstd'], writes=['rstd'])
        for k in range(KC):
            t = tmpn[k % 2]
            P.op('dve', 'tensor_tensor', t[:, 0:n], xT[:, k, 0:n], rstd[:, 0:n], ALU.mult, reads=[('x', k), 'rstd'], writes=[('acc', k % 2)])
            if tc.kind == 'p':
                P.op('act', 'activation', hm[:, k, 0:n], t[:, 0:n], AF.Identity, bias=mods_p[:, l, sh_c + k:sh_c + k + 1], scale=mods_p[:, l, sc_c + k:sc_c + k + 1],
                     reads=[('acc', k % 2), 'mods_p'], writes=[('hm', k)])
                continue
            tv = t[:, 0:n].rearrange("p (s w) -> p s w", w=tc.slen)
            a_b = tc.mod(l, sc_c + k)
            s_b = tc.mod(l, sh_c + k)
            P.op('dve', 'tensor_tensor', tv, tv, a_b, ALU.mult, reads=[('acc', k % 2), tc.modkey], writes=[('acc', k % 2)])
            hv = hm[:, k, 0:n].rearrange("p (s w) -> p s w", w=tc.slen)
            P.op('dve', 'tensor_tensor', hv, tv, s_b, ALU.add, reads=[('acc', k % 2), tc.modkey], writes=[('hm', k)])

    def resid_add(tc, l, b, k, g_c):
        n = tc.ncol
        if tc.kind == 'p':
            P.op('dve', 'scalar_tensor_tensor', xT[:, k, 0:n], bank(b)[:, 0:n], mods_p[:, l, g_c + k:g_c + k + 1], xT[:, k, 0:n], ALU.mult, ALU.add,
                 reads=[('ps', b), 'mods_p', ('x', k)], writes=[('x', k)])
            return
        t = tmpn[k % 2]
        tv = t[:, 0:n].rearrange("p (s w) -> p s w", w=tc.slen)
        pvw = bank(b)[:, 0:n].rearrange("p (s w) -> p s w", w=tc.slen)
        P.op('dve', 'tensor_tensor', tv, pvw, tc.mod(l, g_c + k), ALU.mult, reads=[('ps', b), tc.modkey], writes=[('acc', k % 2)])
        P.op('dve', 'tensor_tensor', xT[:, k, 0:n], xT[:, k, 0:n], t[:, 0:n], ALU.add, reads=[('x', k), ('acc', k % 2)], writes=[('x', k)])

    convctr = [0]

    def conv_silu(tc, psum_ap, psum_key, ntap, hist_ap, hist_key, st_in, st_out, st_sem, wtaps, bias_ap, out_ap, out_key, M=128):
        n = tc.ncol
        hl = ntap - 1
        i = convctr[0] % 2
        convctr[0] += 1
        g = gbuf[i]
        wd = hl + tc.slen
        gv = g[0:M, 0:tc.nseq * wd].rearrange("p (s w) -> p s w", w=wd)
        gk = ('gbuf', i)
        if tc.kind == 'p':
            P.op('act', 'copy', gv[:, 0, 0:hl], hist_ap, reads=[hist_key], writes=[gk])
        else:
            P.op('sp', 'dma_start', out=g[0:M, 0:tc.nseq * wd], in_=st_in, writes=[gk], dsem='gin%d' % i)
        P.op('act', 'copy', gv[:, :, hl:wd], psum_ap.rearrange("p (s w) -> p s w", w=tc.slen), reads=[psum_key], writes=[gk])
        if tc.kind == 'p':
            P.op('act', 'copy', hist_ap, gv[:, 0, tc.slen:tc.slen + hl], reads=[gk], writes=[hist_key])
        else:
            P.op('act', 'dma_start', out=st_out, in_=g[0:M, 0:tc.nseq * wd], reads=[gk], writes=[], dsem=st_sem)
        a = accb[i]
        av = a[0:M, 0:n].rearrange("p (s w) -> p s w", w=tc.slen)
        ak = ('acc', i)
        P.op('dve', 'tensor_scalar', av, gv[:, :, 0:tc.slen], wtaps[:, 0:1], None, ALU.mult, reads=[gk, 'pvec'], writes=[ak])
        for t_ in range(1, ntap):
            P.op('dve', 'scalar_tensor_tensor', av, gv[:, :, t_:t_ + tc.slen], wtaps[:, t_:t_ + 1], av, ALU.mult, ALU.add,
                 reads=[gk, ak, 'pvec'], writes=[ak])
        P.op('act', 'activation', out_ap, a[0:M, 0:n], AF.Silu, bias=bias_ap, reads=[ak, 'pvec'], writes=[out_key])

    EPS_AP = cst[:, 640:641]
    ONE24 = cst[0:24, 641:642]

    hctr = [0]
    for ti, tl in enumerate(tiles_p):
        P.epoch = ti + 1
        tc = TileCtx()
        if tl == 's':
            tc.kind, tc.ncol, tc.nseq, tc.slen, tc.nchunk, tc.W = 's', WS, NS, 4, 1, WS
            z_s, xc, bc, act_t = arena_views(WS, True)
            assert 6 * WS + 12 * WS + 4 * WS + (JC * WS + 1) // 2 <= 5600 and 5600 + 2 * (NS * 24 + 32) <= 8000
            daq = arena[:, 5600:5600 + NS * 24]
            cdall = arena[:, 5600 + NS * 24 + 32:5600 + 2 * NS * 24 + 32]
            P.op('dve', 'memset', rstd[:, 0:1], 0.0, writes=ARENA_KEYS + ['modall', 'rstd', 'daq', 'cdall'])
            tc.modkey = 'modall'
            tc.mod = lambda l, c: modall[:, c, 1:NB].unsqueeze(2).to_broadcast([128, NS, 4])
            P.op('sp', 'dma_start', out=xT[:, :, 0:WS], in_=xs, writes=[('x', k) for k in range(KC)], dsem='xin')
            TRI, SAME = tri_s, same_s
        else:
            tc.kind, tc.ncol, tc.nseq, tc.slen, tc.nchunk, tc.W = 'p', T, 1, T, T // 128, 128
            tc.modkey = 'mods_p'
            tc.mod = lambda l, c: mods_p[:, l, c:c + 1].unsqueeze(2).to_broadcast([128, 1, T])
            P.op('sp', 'dma_start', out=xT[:, :, :], in_=xp[:, :, tl * T:(tl + 1) * T], writes=[('x', k) for k in range(KC)], dsem='xin')
            TRI, SAME = tri_p, same_p
        n = tc.ncol
        W = tc.W
        first_p = (tl == 0)
        last_p = (tl == cfg.npt - 1)
        for l in range(DEPTH):
            if tl == 's':
                P.op('sp', 'dma_start', out=modall[:].rearrange("p a b -> p (a b)"), in_=mod_scr[l], reads=[('modscr', l)], writes=['modall'], dsem='msc')
                P.op('sp', 'dma_start', out=upool[:, :, 0:NS * 19], in_=st_pool[l], writes=['upool'], dsem='upin')
            P.op('pool', 'dma_start', out=wpool_t[:], in_=w_pool[l], writes=['wpool'], dsem='wp')
            P.op('act', 'activation', aneg[:, 0:1], hvec[:, 2 * l + 1:2 * l + 2], AF.Exp, reads=['hvec'], writes=['aneg'])
            P.op('dve', 'tensor_scalar', aneg[:, 1:2], aneg[:, 0:1], -1.0, None, ALU.mult, reads=['aneg'], writes=['aneg'])
            norm_mod(tc, l, 16, 0)
            if tc.kind == 'p':
                for g in range(4):
                    P.op('act', 'copy', upool[:, g, 0:15], hist_pool[:, l, g, :], reads=['hpool'], writes=['upool'])
            for j in range(19):
                wt, wk = w_get(KC, 256)
                for half in range(2):
                    c = 2 * j + half
                    if c > 36:
                        continue
                    M = 128 if c < 36 else 24
                    b = next_bank()
                    for k in range(KC):
                        P.op('pe', 'matmul', bank(b)[0:M, 0:n], wt[:, k, half * 128:half * 128 + M], hm[:, k, 0:n],
                             start=(k == 0), stop=(k == KC - 1), reads=[wk, ('hm', k)], writes=[('ps', b)])
                    pk = ('ps', b)
                    if c < 4:
                        uv = upool[:, c, 0:tc.nseq * (15 + tc.slen)].rearrange("p (s w) -> p s w", w=15 + tc.slen)
                        P.op('act', 'copy', uv[:, :, 15:15 + tc.slen], bank(b)[:, 0:n].rearrange("p (s w) -> p s w", w=tc.slen),
                             reads=[pk], writes=['upool'])
                    elif c < 16:
                        P.op('act', 'activation', z_s[:, c - 4, 0:n], bank(b)[:, 0:n], AF.Silu, reads=[pk], writes=[('z', c - 4)])
                    elif c < 36:
                        kk = c - 16
                        if kk < 12:
                            oap, okey = xc[:, kk, 0:n], ('xc', kk)
                        else:
                            oap, okey = bc[:, kk - 12, 0:n], ('bc', kk - 12)
                        conv_silu(tc, bank(b)[:, 0:n], pk, 4, hist_conv[:, l, kk, :], 'hconv',
                                  st_conv[l, :, kk, :] if tl == 's' else None, o_conv_s[l, :, kk, :] if tl == 's' else None, 'ocs',
                                  pv(l, O_CW + 4 * kk, 4), pv(l, O_CB + kk, 1), oap, okey)
                    else:
                        v = dtr[:, 0:n]
                        u = dtr[:, T:T + n]
                        P.op('act', 'activation', v, bank(b)[0:24, 0:n], AF.Identity, bias=hvec[:, 2 * l:2 * l + 1], reads=[pk, 'hvec'], writes=['dtr'])
                        P.op('act', 'activation', u, v, AF.Abs, reads=['dtr'], writes=['dtr'])
                        P.op('act', 'activation', u, u, AF.Exp, scale=-1.0, reads=['dtr'], writes=['dtr'])
                        P.op('act', 'activation', u, u, AF.Ln, bias=ONE24, reads=['dtr', 'cst'], writes=['dtr'])
                        P.op('dve', 'tensor_scalar', v, v, 0.0, None, ALU.max, reads=['dtr'], writes=['dtr'])
                        P.op('dve', 'tensor_tensor', v, v, u, ALU.add, reads=['dtr'], writes=['dtr'])
                        P.op('dve', 'tensor_scalar', u, v, aneg[:, 1:2], None, ALU.mult, reads=['dtr', 'aneg'], writes=['dtr'])
            if tc.kind == 's':
                P.op('act', 'dma_start', out=o_pool_s[l], in_=upool[:, :, 0:NS * 19], reads=['upool'], writes=[], dsem='ops')
            wdp = 15 + tc.slen
            for g in range(4):
                f = upool[:, g, 0:tc.nseq * wdp].rearrange("p (s w) -> p s w", w=wdp)
                A_ = pa[:, 0:tc.nseq * wdp].rearrange("p (s w) -> p s w", w=wdp)
                B_ = pb[:, 0:tc.nseq * wdp].rearrange("p (s w) -> p s w", w=wdp)
                P.op('dve', 'tensor_tensor', A_[:, :, 1:wdp], f[:, :, 1:wdp], f[:, :, 0:wdp - 1], ALU.add, reads=['upool'], writes=[('acc', 0)])
                cur, curk, oth, othk = A_, ('acc', 0), B_, ('acc', 1)
                sh = 2
                lo = 1
                for _ in range(g):
                    lo2 = lo + sh
                    P.op('dve', 'tensor_tensor', oth[:, :, lo2:wdp], cur[:, :, lo2:wdp], cur[:, :, lo2 - sh:wdp - sh], ALU.add, reads=[curk], writes=[othk])
                    cur, curk, oth, othk = oth, othk, cur, curk
                    sh *= 2
                    lo = lo2
                dv = dpool[:, 0:n].rearrange("p (s w) -> p s w", w=tc.slen)
                P.op('dve', 'scalar_tensor_tensor', dv, cur[:, :, 15:wdp], 1.0 / WINS[g], f[:, :, 15:wdp], ALU.mult, ALU.subtract,
                     reads=[curk, 'upool'], writes=[('sq', 0)])
                if first_p:
                    P.op('dve', 'tensor_tensor', oth[:, 0, 0:15], cur[:, 0, 15:30], invcnt[:, g, :], ALU.mult, reads=[curk, 'cst'], writes=[othk])
                    P.op('dve', 'tensor_tensor', dpool[:, 0:15], oth[:, 0, 0:15], f[:, 0, 15:30], ALU.subtract, reads=[othk, 'upool'], writes=[('sq', 0)])
                b = next_bank()
                P.op('pe', 'matmul', bank(b)[:, 0:n], wpool_t[:, g * 128:(g + 1) * 128], dpool[:, 0:n], start=True, stop=True,
                     reads=['wpool', ('sq', 0)], writes=[('ps', b)])
                P.op('act', 'activation', hm[:, g, 0:n], bank(b)[:, 0:n], AF.Identity, scale=pv(l, O_PSC + g, 1), reads=[('ps', b), 'pvec'], writes=[('hm', g)])
                if tc.kind == 'p':
                    P.op('act', 'copy', hist_pool[:, l, g, :], upool[:, g, tc.slen:tc.slen + 15], reads=['upool'], writes=['hpool'])
            for ci in range(tc.nchunk):
                c0 = ci * 128
                P.op('pe', 'transpose', bank(6)[0:W, 0:24], dtr[:, c0:c0 + W], ident[0:24, 0:24], reads=['dtr', 'cst'], writes=[('ps', 6)])
                P.op('pe', 'transpose', bank(6)[0:W, 24:48], dtr[:, T + c0:T + c0 + W], ident[0:24, 0:24], reads=['dtr', 'cst'], writes=[('ps', 6)])
                P.op('act', 'copy', stt[0:W, :], bank(6)[0:W, 0:48], reads=[('ps', 6)], writes=['stt'])
                P.op('pe', 'matmul', bank(6)[0:W, 64:88], TRI[0:W, 0:W], stt[0:W, 24:48], start=True, stop=True, reads=['stt', 'cst'], writes=[('ps', 6)])
                P.op('pe', 'matmul', bank(6)[0:W, 88:112], SAME[0:W, 0:W], stt[0:W, 24:48], start=True, stop=True, reads=['stt', 'cst'], writes=[('ps', 6)])
                P.op('act', 'copy', cst_t[0:W, :], bank(6)[0:W, 64:112], reads=[('ps', 6)], writes=['cs_t'])
                P.op('dve', 'tensor_tensor', sm[0:W, 72:96], cst_t[0:W, 24:48], cst_t[0:W, 0:24], ALU.subtract, reads=['cs_t'], writes=['sm'])
                P.op('act', 'activation', sm[0:W, 0:24], sm[0:W, 72:96], AF.Exp, reads=['sm'], writes=['sm'])
                P.op('act', 'activation', sm[0:W, 48:72], cst_t[0:W, 0:24], AF.Exp, reads=['cs_t'], writes=['sm'])
                P.op('dve', 'tensor_tensor', sm[0:W, 24:48], sm[0:W, 0:24], stt[0:W, 0:24], ALU.mult, reads=['sm', 'stt'], writes=['sm'])
                P.op('dve', 'tensor_scalar', sm[0:W, 72:96], cst_t[0:W, 0:24], -1.0, None, ALU.mult, reads=['cs_t', 'sm'], writes=['sm'])
                nsq = tc.nseq
                dq = daq[0:W, 0:nsq * 24].rearrange("p (q h) -> p q h", h=24)
                P.op('dve', 'tensor_tensor', dq, stt[0:W, 24:48].unsqueeze(1).to_broadcast([W, nsq, 24]),
                     rowmask_all[0:W, 0:nsq].unsqueeze(2).to_broadcast([W, nsq, 24]) if tc.kind == 's' else ones_f[0:W, 0:nsq].unsqueeze(2).to_broadcast([W, nsq, 24]),
                     ALU.mult, reads=['stt', 'cst'], writes=['daq'])
                tot = nsq * 24
                off = 0
                while off < tot:
                    wcols = min(384, tot - off)
                    P.op('pe', 'matmul', bank(7)[:, 0:wcols], ones_f[0:W, :], daq[0:W, off:off + wcols], start=True, stop=True,
                         reads=['daq', 'cst'], writes=[('ps', 7)])
                    P.op('act', 'activation', cdall[:, off:off + wcols], bank(7)[:, 0:wcols], AF.Exp, reads=[('ps', 7)], writes=['cdall'])
                    off += wcols
                pbt = bank(7).bitcast(BF16)
                for g in range(4):
                    P.op('pe', 'transpose', pbt[0:W, g * 128:(g + 1) * 128], bc[:, g, c0:c0 + W], ident_bf[:], reads=[('bc', g), 'identbf'], writes=[('ps', 7)])
                P.op('act', 'copy', btok[0:W, :], pbt[0:W, 0:512], reads=[('ps', 7)], writes=['btok'])
                A3 = ps[:, 0:1536]
                for k in range(12):
                    P.op('pe', 'transpose', A3[0:W, k * 128:(k + 1) * 128], xc[:, k, c0:c0 + W], ident, reads=[('xc', k), 'cst'],
                         writes=[('ps', 0), ('ps', 1), ('ps', 2)])
                A3v = A3[0:W, :].rearrange("p (h d) -> p h d", d=64)
                P.op('dve', 'tensor_tensor', xdt[0:W, :].rearrange("p (h d) -> p h d", d=64), A3v, stt[0:W, 0:24].unsqueeze(2).to_broadcast([W, 24, 64]),
                     ALU.mult, reads=[('ps', 0), ('ps', 1), ('ps', 2), 'stt'], writes=['xdt'])
                P.op('dve', 'tensor_tensor', xdtw[0:W, :].rearrange("p (h d) -> p h d", d=64), A3v, sm[0:W, 24:48].unsqueeze(2).to_broadcast([W, 24, 64]),
                     ALU.mult, reads=[('ps', 0), ('ps', 1), ('ps', 2), 'sm'], writes=['xdtw'])
                xcv = xc[:, :, c0:c0 + W]
                P.op('dve', 'tensor_tensor', xcv, xcv, pv(l, O_DS, 12).unsqueeze(2).to_broadcast([128, 12, W]), ALU.mult,
                     reads=[('xc', k) for k in range(12)] + ['pvec'], writes=[('xc', k) for k in range(12)])
                for g in range(4):
                    P.op('pe', 'matmul', bank(6)[0:W, g * 128:g * 128 + W], bc[:, g, c0:c0 + W], bc[:, 4 + g, c0:c0 + W], start=True, stop=True,
                         reads=[('bc', g), ('bc', 4 + g)], writes=[('ps', 6)])
                P.op('dve', 'tensor_tensor', cbm[0:W, :].rearrange("p (g w) -> p g w", w=128)[:, :, 0:W],
                     bank(6)[0:W, :].rearrange("p (g w) -> p g w", w=128)[:, :, 0:W],
                     TRI[0:W, 0:W].unsqueeze(1).to_broadcast([W, 4, W]), ALU.mult, reads=[('ps', 6), 'cst'], writes=['cbm'])
                B3 = ps[:, 1536:3072]
                if nsq > 1:
                    for b_ in range(3):
                        P.op('pe', 'matmul', A3[0:W, b_ * 512:(b_ + 1) * 512], zeros_bf[:, 0:W], hm[:, 0, 0:512], start=True, stop=True,
                             reads=['zerosbf', ('hm', 0)], writes=[('ps', 0), ('ps', 1), ('ps', 2)])
                for q in range(nsq):
                    if tc.kind == 'p' and ci > 0:
                        hi = tc.hi
                    else:
                        hi = hctr[0] % 2
                        hctr[0] += 1
                        tc.hi = hi
                    H, Hk = Hb[hi], ('H', hi)
                    Hf, Hfk = Hbf[0], ('Hbf', 0)
                    if tc.kind == 'p':
                        if ci == 0:
                            if first_p:
                                P.op('dve', 'memset', H[:], 0.0, writes=[Hk])
                            else:
                                P.op('sp', 'dma_start', out=H[:], in_=h_scr[l], reads=[('hscr', l)], writes=[Hk], dsem='hin%d' % hi)
                    else:
                        P.op('sp', 'dma_start', out=H[:], in_=st_ssm[l, q], writes=[Hk], dsem='hin%d' % hi)
                    P.op('act', 'copy', Hf[:], H[:], reads=[Hk], writes=[Hfk])
                    if tc.kind == 's':
                        cq, cqk = cqb[q % 2], ('cq', q % 2)
                        cqv = cq[:, :].rearrange("p (g w) -> p g w", w=128)
                        if q >= 2:
                            P.op('dve', 'memset', cqv[:, :, 4 * (q - 2):4 * (q - 2) + 4], 0.0, writes=[cqk])
                        P.op('dve', 'tensor_copy', cqv[:, :, 4 * q:4 * q + 4], bc[:, 4:8, c0 + 4 * q:c0 + 4 * q + 4],
                             reads=[('bc', 4), ('bc', 5), ('bc', 6), ('bc', 7)], writes=[cqk])
                        bq, bqk = bqb[q % 2], ('bq', q % 2)
                        P.op('dve', 'tensor_scalar', bq[0:W, :], btok[0:W, :], rowmask_all[0:W, q:q + 1], None, ALU.mult, reads=['btok', 'cst'], writes=[bqk])
                    for g in range(4):
                        if tc.kind == 's':
                            lhs = cq[:, g * 128:g * 128 + W]
                            rk = [cqk]
                        else:
                            lhs = bc[:, 4 + g, c0:c0 + W]
                            rk = [('bc', 4 + g)]
                        for pp in range(3):
                            c_ = g * 384 + pp * 128
                            P.op('pe', 'matmul', A3[0:W, c_:c_ + 128], lhs, Hf[:, c_:c_ + 128], start=(nsq == 1), stop=(q == nsq - 1), skip_group_check=(nsq > 1),
                                 reads=rk + [Hfk], writes=[('ps', 0), ('ps', 1), ('ps', 2)])
                    for g in range(4):
                        if tc.kind == 's':
                            lhs = bq[0:W, g * 128:(g + 1) * 128]
                            rk = [bqk]
                        else:
                            lhs = btok[0:W, g * 128:(g + 1) * 128]
                            rk = ['btok']
                        for pp in range(3):
                            c_ = g * 384 + pp * 128
                            P.op('pe', 'matmul', B3[:, c_:c_ + 128], lhs, xdtw[0:W, c_:c_ + 128], start=True, stop=True,
                                 reads=rk + ['xdtw'], writes=[('ps', 3), ('ps', 4), ('ps', 5)])
                    Hv = H[:, :].rearrange("p (h d) -> p h d", d=64)
                    P.op('dve', 'tensor_tensor', Hv, Hv, cdall[:, q * 24:(q + 1) * 24].unsqueeze(2).to_broadcast([128, 24, 64]), ALU.mult,
                         reads=[Hk, 'cdall'], writes=[Hk])
                    P.op('dve', 'tensor_tensor', H[:], H[:], B3, ALU.add, reads=[Hk, ('ps', 3), ('ps', 4), ('ps', 5)], writes=[Hk])
                    if tc.kind == 's':
                        P.op('act', 'dma_start', out=o_ssm_s[l, q], in_=H[:], reads=[Hk], writes=[], dsem='hout%d' % hi)
                    elif ci == tc.nchunk - 1:
                        if last_p:
                            P.op('act', 'dma_start', out=o_ssm_p[l], in_=H[:], reads=[Hk], writes=[], dsem='hout%d' % hi)
                        else:
                            P.op('act', 'dma_start', out=h_scr[l], in_=H[:], reads=[Hk], writes=[('hscr', l)], dsem='hout%d' % hi)
                if tc.kind == 's':
                    for q_ in range(max(0, nsq - 2), nsq):
                        P.op('dve', 'memset', cqb[q_ % 2][:, :].rearrange("p (g w) -> p g w", w=128)[:, :, 4 * q_:4 * q_ + 4], 0.0, writes=[('cq', q_ % 2)])
                P.op('dve', 'tensor_tensor', yo[0:W, :].rearrange("p (h d) -> p h d", d=64), A3v, sm[0:W, 48:72].unsqueeze(2).to_broadcast([W, 24, 64]),
                     ALU.mult, reads=[('ps', 0), ('ps', 1), ('ps', 2), 'sm'], writes=['yo'])
                def stageA(r):
                    h0 = 3 * r
                    sb_ = 6 + (r % 2)
                    da, dak = dab[r % 2], ('dab', r % 2)
                    dav = da[0:W, :].rearrange("p (j w) -> p j w", w=128)[:, :, 0:W]
                    for jj in range(3):
                        P.op('act', 'activation', da[0:W, jj * 128:jj * 128 + W], TRI[0:W, 0:W], AF.Identity, scale=stt[0:W, 24 + h0 + jj:24 + h0 + jj + 1],
                             reads=['stt', 'cst'], writes=[('dab', r % 2, jj)])
                    if W == 128:
                        P.op('pe', 'matmul', bank(sb_)[0:W, 0:384], ones_f[0:W, 0:W], da[0:W, 0:384], start=True, stop=True,
                             reads=[('dab', r % 2, 0), ('dab', r % 2, 1), ('dab', r % 2, 2), 'cst'], writes=[('ps', sb_)])
                    else:
                        for jj in range(3):
                            P.op('pe', 'matmul', bank(sb_)[0:W, jj * 128:jj * 128 + W], ones_f[0:W, 0:W], da[0:W, jj * 128:jj * 128 + W], start=True, stop=True,
                                 reads=[('dab', r % 2, jj), 'cst'], writes=[('ps', sb_)])

                def stageB(r):
                    h0 = 3 * r
                    sb_ = 6 + (r % 2)
                    te = teb[r % 2]
                    for jj in range(3):
                        P.op('act', 'activation', te[0:W, jj * 128:jj * 128 + W], bank(sb_)[0:W, jj * 128:jj * 128 + W], AF.Exp,
                             bias=sm[0:W, 72 + h0 + jj:72 + h0 + jj + 1], reads=[('ps', sb_), 'sm'], writes=[('teb', r % 2, jj)])

                def stageC(r):
                    g = r // 2
                    h0 = 3 * r
                    te = teb[r % 2]
                    mt, mtk = mtb[r % 2], ('mtb', r % 2)
                    tev = te[0:W, :].rearrange("p (j w) -> p j w", w=128)[:, :, 0:W]
                    P.op('dve', 'scalar_tensor_tensor', mt[0:W, :].rearrange("p (j w) -> p j w", w=128)[:, :, 0:W], tev, 1.0,
                         cbm[0:W, g * 128:g * 128 + W].unsqueeze(1).to_broadcast([W, 3, W]), ALU.min, ALU.mult,
                         reads=[('teb', r % 2, 0), ('teb', r % 2, 1), ('teb', r % 2, 2), 'cbm'], writes=[mtk])
                    for jj in range(3):
                        h = h0 + jj
                        P.op('pe', 'matmul', B3[0:W, h * 64:(h + 1) * 64], mt[0:W, jj * 128:jj * 128 + W], xdt[0:W, h * 64:(h + 1) * 64], start=True, stop=True,
                             reads=[mtk, 'xdt'], writes=[('ps', 3), ('ps', 4), ('ps', 5)])
                for it in range(10):
                    if it < 8:
                        stageA(it)
                    if 1 <= it < 9:
                        stageB(it - 1)
                    if it >= 2:
                        stageC(it - 2)
                P.op('dve', 'tensor_tensor', yo[0:W, :], yo[0:W, :], B3[0:W, :], ALU.add, reads=['yo', ('ps', 3), ('ps', 4), ('ps', 5)], writes=['yo'])
                for k in range(12):
                    P.op('pe', 'transpose', A3[:, k * 128:k * 128 + W], yo[0:W, k * 128:(k + 1) * 128], ident[0:W, 0:W], reads=['yo', 'cst'],
                         writes=[('ps', 0), ('ps', 1), ('ps', 2)])
                xcv = xc[:, :, c0:c0 + W]
                P.op('dve', 'tensor_tensor', xcv, xcv, A3.rearrange("p (k w) -> p k w", w=128)[:, :, 0:W], ALU.add,
                     reads=[('xc', k) for k in range(12)] + [('ps', 0), ('ps', 1), ('ps', 2)], writes=[('xc', k) for k in range(12)])
            if cfg.debug and tl == 's' and l == 0:
                P.op('sp', 'dma_start', out=dbg[:, 0:48], in_=stt[:, :], reads=['stt'], writes=[], dsem='dbg')
                P.op('sp', 'dma_start', out=dbg[:, 48:96], in_=cst_t[:, :], reads=['cs_t'], writes=[], dsem='dbg')
                P.op('sp', 'dma_start', out=dbg[:, 96:192], in_=sm[:, :], reads=['sm'], writes=[], dsem='dbg')
                P.op('sp', 'dma_start', out=dbg[:, 192:192 + NS * 24], in_=cdall[:, :], reads=['cdall'], writes=[], dsem='dbg')
                P.op('sp', 'dma_start', out=dbg[:, 1024:1536], in_=cbm[:, :], reads=['cbm'], writes=[], dsem='dbg')
                P.op('pool', 'dma_start', out=dbg[:, 1536:2048], in_=btok[:, :], reads=['btok'], writes=[], dsem='dbg2')
                P.op('pool', 'dma_start', out=dbg[:, 2048:3584], in_=xdt[:, :], reads=['xdt'], writes=[], dsem='dbg2')
                P.op('pool', 'dma_start', out=dbg[:, 3584:5120], in_=xdtw[:, :], reads=['xdtw'], writes=[], dsem='dbg2')
                P.op('sp', 'dma_start', out=dbg[:, 5120:6656], in_=yo[:, :], reads=['yo'], writes=[], dsem='dbg')
            b = next_bank()
            for k in range(12):
                P.op('dve', 'tensor_tensor', xc[:, k, 0:n], xc[:, k, 0:n], z_s[:, k, 0:n], ALU.mult, reads=[('xc', k), ('z', k)], writes=[('xc', k)])
                s = sqb[k % 2]
                P.op('act', 'activation', s[:, 0:n], xc[:, k, 0:n], AF.Square, reads=[('xc', k)], writes=[('sq', k % 2)])
                P.op('pe', 'matmul', bank(b)[:, 0:n], ones_bf[:], s[:, 0:n], start=(k == 0), stop=(k == 11), reads=[('sq', k % 2), 'onesbf'], writes=[('ps', b)])
            P.op('act', 'activation', rstd[:, 0:n], bank(b)[:, 0:n], AF.Sqrt, bias=EPS_AP, scale=1.0 / 1536, reads=[('ps', b), 'cst'], writes=['rstd'])
            P.op('dve', 'reciprocal', rstd[:, 0:n], rstd[:, 0:n], reads=['rstd'], writes=['rstd'])
            for k in range(12):
                P.op('dve', 'scalar_tensor_tensor', hm[:, 4 + k, 0:n], xc[:, k, 0:n], pv(l, O_SN + k, 1), rstd[:, 0:n], ALU.mult, ALU.mult,
                     reads=[('xc', k), 'rstd', 'pvec'], writes=[('hm', 4 + k)])
            for j in range(8):
                wt, wk = w_get(KC, 256)
                for half in range(2):
                    c = 2 * j + half
                    b = next_bank()
                    for k in range(KC):
                        P.op('pe', 'matmul', bank(b)[:, 0:n], wt[:, k, half * 128:(half + 1) * 128], hm[:, k, 0:n], start=(k == 0), stop=(k == KC - 1),
                             reads=[wk, ('hm', k)], writes=[('ps', b)])
                    resid_add(tc, l, b, c, 32)
            norm_mod(tc, l, 64, 48)
            for j in range(JC):
                wt, wk = w_get(KC, 256)
                bg = next_bank()
                for k in range(KC):
                    P.op('pe', 'matmul', bank(bg)[:, 0:n], wt[:, k, 0:128], hm[:, k, 0:n], start=(k == 0), stop=(k == KC - 1), reads=[wk, ('hm', k)], writes=[('ps', bg)])
                bv = next_bank()
                for k in range(KC):
                    P.op('pe', 'matmul', bank(bv)[:, 0:n], wt[:, k, 128:256], hm[:, k, 0:n], start=(k == 0), stop=(k == KC - 1), reads=[wk, ('hm', k)], writes=[('ps', bv)])
                si = j % 2
                conv_silu(tc, bank(bg)[:, 0:n], ('ps', bg), 3, hist_ffn[:, l, j, :], 'hffn',
                          st_ffn[l, :, j, :] if tl == 's' else None, o_ffn_s[l, :, j, :] if tl == 's' else None, 'ofs',
                          pv(l, O_FW + 3 * j, 3), pv(l, O_FB + j, 1), silb[si][:, 0:n], ('silb', 0))
                P.op('dve', 'tensor_tensor', act_t[:, j, 0:n], silb[si][:, 0:n], bank(bv)[:, 0:n], ALU.mult, reads=[('silb', 0), ('ps', bv)], writes=[('act', j)])
            for c in range(KC):
                b = next_bank()
                for hh in range(2):
                    wt, wk = w_get(22, 128)
                    nk = 22 if hh == 0 else 21
                    for kk in range(nk):
                        jj = hh * 22 + kk
                        P.op('pe', 'matmul', bank(b)[:, 0:n], wt[:, kk, :], act_t[:, jj, 0:n], start=(jj == 0), stop=(jj == JC - 1),
                             reads=[wk, ('act', jj)], writes=[('ps', b)])
                resid_add(tc, l, b, c, 80)
        b = next_bank()
        for k in range(KC):
            s = sqb[k % 2]
            P.op('act', 'activation', s[:, 0:n], xT[:, k, 0:n], AF.Square, reads=[('x', k)], writes=[('sq', k % 2)])
            P.op('pe', 'matmul', bank(b)[:, 0:n], ones_bf[:], s[:, 0:n], start=(k == 0), stop=(k == KC - 1), reads=[('sq', k % 2), 'onesbf'], writes=[('ps', b)])
        P.op('act', 'activation', rstd[:, 0:n], bank(b)[:, 0:n], AF.Sqrt, bias=EPS_AP, scale=1.0 / D, reads=[('ps', b), 'cst'], writes=['rstd'])
        P.op('dve', 'reciprocal', rstd[:, 0:n], rstd[:, 0:n], reads=['rstd'], writes=['rstd'])
        for k in range(KC):
            P.op('dve', 'scalar_tensor_tensor', xT[:, k, 0:n], xT[:, k, 0:n], pvec[:, DEPTH * PL + k:DEPTH * PL + k + 1], rstd[:, 0:n], ALU.mult, ALU.mult,
                 reads=[('x', k), 'rstd', 'pvec'], writes=[('x', k)])
        if tl == 's':
            P.op('sp', 'dma_start', out=o_ys, in_=xT[:, :, 0:WS], reads=[('x', k) for k in range(KC)], writes=[], dsem='xout')
        else:
            P.op('sp', 'dma_start', out=o_yp[:, :, tl * T:(tl + 1) * T], in_=xT[:, :, :], reads=[('x', k) for k in range(KC)], writes=[], dsem='xout')
        if tl != 's' and last_p:
            for l in range(DEPTH):
                P.op('sp', 'dma_start', out=o_pool_p[l], in_=hist_pool[:, l].rearrange("p a b -> p (a b)"), reads=['hpool'], writes=[], dsem='hst')
                P.op('sp', 'dma_start', out=o_conv_p[l], in_=hist_conv[:, l].rearrange("p a b -> p (a b)"), reads=['hconv'], writes=[], dsem='hst')
                P.op('sp', 'dma_start', out=o_ffn_p[l], in_=hist_ffn[:, l].rearrange("p a b -> p (a b)"), reads=['hffn'], writes=[], dsem='hst')
    assert wstate['next'] == len(wlist), (wstate, len(wlist))
    P.analyze()
    P.emit(nc, ['dbg', 'dbg2', 'xout', 'hst', 'hout0', 'hout1', 'ops', 'ocs', 'ofs', 'msc'])
    return nc, len(P.ops)


def _fm(v, nchunk):
    return np.ascontiguousarray(np.asarray(v, np.float32).reshape(nchunk, 128).T)


def _wtile(w, ncols_pad, tile_cols):
    K, N = w.shape
    if N < ncols_pad:
        w = np.concatenate([w, np.zeros((K, ncols_pad - N), np.float32)], axis=1)
    kc = K // 128
    nt = ncols_pad // tile_cols
    a = w.reshape(kc, 128, nt, tile_cols).transpose(2, 1, 0, 3)
    return np.ascontiguousarray(a).reshape(nt, 128, kc * tile_cols)


def make_consts(cfg):
    NS, WS = cfg.ns, cfg.ws
    NCST = 128 * 6 + NS + 60
    c = np.zeros((128, NCST), np.float32)
    c[:, 0:128] = np.eye(128, dtype=np.float32)
    idx = np.arange(128)
    c[:, 128:256] = (idx[:, None] <= idx[None, :]).astype(np.float32)
    c[:, 256:384] = 1.0
    same = (idx[:, None] // 4 == idx[None, :] // 4)
    c[:, 384:512] = (same & (idx[:, None] <= idx[None, :])).astype(np.float32)
    c[:, 512:640] = same.astype(np.float32)
    c[:, 640] = EPS
    c[:, 641] = 1.0
    for q in range(NS):
        c[:, 768 + q] = (idx // 4 == q).astype(np.float32)
    for g, w in enumerate(WINS):
        for t in range(15):
            c[:, 768 + NS + g * 15 + t] = 1.0 / min(t + 1, w)
    cm = np.zeros((128, NS, WS), np.float32)
    for q in range(NS):
        cm[:, q, 4 * q:4 * q + 4] = 1.0
    return c, cm.reshape(128, NS * WS)


def prep_weights(cfg, inp):
    DEPTH = cfg.depth
    out = {}
    out['w_ada'] = np.stack([_wtile(np.asarray(inp['w_ada'][l]), 12288, 256) for l in range(DEPTH)])
    out['w_in'] = np.stack([_wtile(np.asarray(inp['w_in'][l]), 19 * 256, 256) for l in range(DEPTH)])
    out['w_out'] = np.stack([_wtile(np.asarray(inp['w_out'][l]), 2048, 256) for l in range(DEPTH)])
    wu = []
    for l in range(DEPTH):
        w = np.asarray(inp['w_up'][l])
        g = w[:, :DFF].reshape(KC, 128, JC, 128)
        v = w[:, DFF:].reshape(KC, 128, JC, 128)
        t = np.concatenate([g, v], axis=3)
        wu.append(np.ascontiguousarray(t.transpose(2, 1, 0, 3)).reshape(JC, 128, KC * 256))
    out['w_up'] = np.stack(wu)
    wd = []
    for l in range(DEPTH):
        w = np.asarray(inp['w_down'][l])
        w = np.concatenate([w, np.zeros((128, D), np.float32)], axis=0).reshape(2, 22, 128, KC, 128)
        wd.append(np.ascontiguousarray(w.transpose(3, 0, 2, 1, 4)).reshape(32, 128, 22 * 128))
    out['w_down'] = np.stack(wd)
    out['w_pool'] = np.stack([np.ascontiguousarray(np.asarray(inp['pool_w'][l]).transpose(1, 0, 2)).reshape(128, 512) for l in range(DEPTH)])
    pvec = np.zeros((128, DEPTH * PL + 16), np.float32)
    for l in range(DEPTH):
        o = l * PL
        pvec[:, o + O_N1:o + O_N1 + 16] = _fm(inp['norm1'][l], 16)
        pvec[:, o + O_N2:o + O_N2 + 16] = _fm(inp['norm2'][l], 16)
        pvec[:, o + O_BADA:o + O_BADA + 96] = _fm(inp['b_ada'][l], 96)
        pvec[:, o + O_PSC:o + O_PSC + 4] = _fm(inp['pool_scale'][l], 4)
        cw = np.asarray(inp['conv_w'][l]).reshape(4, 20, 128).transpose(2, 1, 0)
        pvec[:, o + O_CW:o + O_CW + 80] = cw.reshape(128, 80)
        pvec[:, o + O_CB:o + O_CB + 20] = _fm(inp['conv_b'][l], 20)
        pvec[:, o + O_SN:o + O_SN + 12] = _fm(inp['ssd_norm'][l], 12)
        fw = np.asarray(inp['ffn_conv_w'][l]).reshape(3, JC, 128).transpose(2, 1, 0)
        pvec[:, o + O_FW:o + O_FW + 129] = fw.reshape(128, 129)
        pvec[:, o + O_FB:o + O_FB + JC] = _fm(inp['ffn_conv_b'][l], JC)
        pvec[:, o + O_DS:o + O_DS + 12] = _fm(np.repeat(np.asarray(inp['d_skip'][l]), 64), 12)
    pvec[:, DEPTH * PL:DEPTH * PL + 16] = _fm(inp['norm_f'], 16)
    out['pvec'] = pvec
    hv = np.zeros((24, 2 * DEPTH), np.float32)
    for l in range(DEPTH):
        hv[:, 2 * l] = np.asarray(inp['dt_bias'][l])
        hv[:, 2 * l + 1] = np.asarray(inp['a_log'][l])
    out['hvec'] = hv
    return out


def _fm_tokens(x):
    t = x.shape[0]
    return np.ascontiguousarray(np.asarray(x, np.float32).T.reshape(KC, 128, t).transpose(1, 0, 2))


def _unfm_tokens(a):
    t = a.shape[2]
    return np.ascontiguousarray(a.transpose(1, 0, 2).reshape(D, t).T)


def core_inputs(cfg, inp, shared, b, seqs):
    DEPTH, NS, LP = cfg.depth, cfg.ns, cfg.lp
    m = dict(shared)
    m['xp'] = _fm_tokens(np.asarray(inp['x_prompt'][b][:LP]))
    xs = np.asarray(inp['x_sample'])[seqs].reshape(NS * 4, D)
    m['xs'] = _fm_tokens(xs)
    c = np.concatenate([np.asarray(inp['c_prompt'])[b:b + 1], np.asarray(inp['c_sample'])[seqs]], axis=0)
    m['cT'] = _fm_tokens(c)

    def hist_fm(st, nch, hl):
        a = np.asarray(st)[:DEPTH][:, seqs]
        a = a.reshape(DEPTH, NS, hl, nch, 128).transpose(0, 4, 3, 1, 2)
        o = np.zeros((DEPTH, 128, nch, NS, hl + 4), np.float32)
        o[..., :hl] = a
        return o.reshape(DEPTH, 128, nch, NS * (hl + 4))
    m['st_pool'] = hist_fm(inp['state_pool'], 4, 15)
    m['st_conv'] = hist_fm(inp['state_conv'], 20, 3)
    m['st_ffn'] = hist_fm(inp['state_ffn'], JC, 2)
    s = np.asarray(inp['state_ssm'])[:DEPTH][:, seqs]
    m['st_ssm'] = np.ascontiguousarray(s.transpose(0, 1, 4, 2, 3)).reshape(DEPTH, NS, 128, 1536)
    return m


def core_outputs(cfg, r):
    DEPTH, NS = cfg.depth, cfg.ns
    o = {}
    o['yp'] = _unfm_tokens(r['o_yp'])
    o['ys'] = _unfm_tokens(r['o_ys']).reshape(NS, 4, D)
    o['pool_p'] = r['o_pool_p'].reshape(DEPTH, 128, 4, 15).transpose(0, 3, 2, 1).reshape(DEPTH, 15, 512)
    o['conv_p'] = r['o_conv_p'].reshape(DEPTH, 128, 20, 3).transpose(0, 3, 2, 1).reshape(DEPTH, 3, 2560)
    o['ffn_p'] = r['o_ffn_p'].reshape(DEPTH, 128, JC, 2).transpose(0, 3, 2, 1).reshape(DEPTH, 2, DFF)
    o['ssm_p'] = r['o_ssm_p'].reshape(DEPTH, 128, 24, 64).transpose(0, 2, 3, 1)
    o['pool_s'] = r['o_pool_s'].reshape(DEPTH, 128, 4, NS, 19)[..., 4:].transpose(0, 3, 4, 2, 1).reshape(DEPTH, NS, 15, 512)
    o['conv_s'] = r['o_conv_s'].reshape(DEPTH, 128, 20, NS, 7)[..., 4:].transpose(0, 3, 4, 2, 1).reshape(DEPTH, NS, 3, 2560)
    o['ffn_s'] = r['o_ffn_s'].reshape(DEPTH, 128, JC, NS, 6)[..., 4:].transpose(0, 3, 4, 2, 1).reshape(DEPTH, NS, 2, DFF)
    o['ssm_s'] = r['o_ssm_s'].reshape(DEPTH, NS, 128, 24, 64).transpose(0, 1, 3, 4, 2)
    return o


_CACHE = {}


def kernel(**inp):
    cfg = Cfg(depth=4, lp=2048, ns=32, T=512)
    NCORE = 4
    if 'nc' not in _CACHE:
        _CACHE['nc'] = build_program(cfg)[0]
    nc = _CACHE['nc']
    shared = prep_weights(cfg, inp)
    cst, cm = make_consts(cfg)
    shared['cst'] = cst
    in_maps = []
    for b in range(NCORE):
        seqs = np.arange(b * cfg.ns, (b + 1) * cfg.ns)
        in_maps.append(core_inputs(cfg, inp, shared, b, seqs))
    res = run_bass_kernel_spmd(nc, in_maps, core_ids=list(range(NCORE)))
    outs = [core_outputs(cfg, r) for r in res.results]
    f = np.float32
    y_prompt = np.stack([o['yp'] for o in outs]).astype(f)
    y_sample = np.concatenate([o['ys'] for o in outs], axis=0).astype(f)
    pool_p = np.stack([o['pool_p'] for o in outs], axis=1).astype(f)
    conv_p = np.stack([o['conv_p'] for o in outs], axis=1).astype(f)
    ssm_p = np.stack([o['ssm_p'] for o in outs], axis=1).astype(f)
    ffn_p = np.stack([o['ffn_p'] for o in outs], axis=1).astype(f)
    pool_s = np.concatenate([o['pool_s'] for o in outs], axis=1).astype(f)
    conv_s = np.concatenate([o['conv_s'] for o in outs], axis=1).astype(f)
    ssm_s = np.concatenate([o['ssm_s'] for o in outs], axis=1).astype(f)
    ffn_s = np.concatenate([o['ffn_s'] for o in outs], axis=1).astype(f)
    return (y_prompt, y_sample, pool_p, conv_p, ssm_p, ffn_p, pool_s, conv_s, ssm_s, ffn_s)
```
